# Optimizing a Trainium2 kernel written in Bass

```python
import math
import jax, jax.numpy as jnp
from jax import lax
import numpy as np

D_MODEL = 2048
BATCH = 8
SEQ = 4096
DEPTH = 4

D_MIX = D_MODEL
ML_HEADS = 4
ML_DH = D_MIX // 16
ML_W = ML_HEADS * ML_DH
ML_CHUNK = 64
ML_CONV = 4
GLA_HEADS = 4
GLA_DK = D_MIX // 32
GLA_DV = D_MIX // 16
GLA_KW = GLA_HEADS * GLA_DK
GLA_VW = GLA_HEADS * GLA_DV
GLA_RANK = 16
GLA_TAU = 16.0
GLA_CHUNK = 64
MLA_HEADS = 8
MLA_NOPE = 128
MLA_ROPE = 64
MLA_DV = 128
MLA_VW = MLA_HEADS * MLA_DV
MLA_Q_RANK = D_MODEL // 4
MLA_KV_RANK = D_MODEL // 4
ATTN_BLOCK = 128
ROPE_THETA = 10000.0
MAX_POS_OFFSET = 1024
FFN_DIM = 5632
FFN_RES_WEIGHT = 0.5
DEEPNORM_ALPHA = (2.0 * DEPTH) ** 0.25
DEEPNORM_BETA = (8.0 * DEPTH) ** -0.25
NORM_EPS = 1e-5
ADA_STD = 0.1

IN_SIZES = (ML_W, ML_W, ML_W, ML_HEADS, ML_HEADS, ML_W,
            GLA_KW, GLA_KW, GLA_VW, GLA_RANK, GLA_VW,
            MLA_Q_RANK, MLA_KV_RANK, MLA_ROPE)
D_IN = sum(IN_SIZES)
IN_SPLITS = tuple(int(s) for s in np.cumsum(IN_SIZES)[:-1])

kernel_name = "hybrid_mlstm_gla_mla_macaron_deepnorm"


def _layer_norm(x, g, b):
    xf = x.astype(jnp.float32)
    mu = jnp.mean(xf, -1, keepdims=True)
    var = jnp.mean(jnp.square(xf - mu), -1, keepdims=True)
    return ((xf - mu) * lax.rsqrt(var + NORM_EPS) * g + b).astype(x.dtype)


def _head_layer_norm(x):
    xf = x.astype(jnp.float32)
    mu = jnp.mean(xf, -1, keepdims=True)
    var = jnp.mean(jnp.square(xf - mu), -1, keepdims=True)
    return ((xf - mu) * lax.rsqrt(var + NORM_EPS)).astype(x.dtype)


def _rms_norm(x, g=None):
    xf = x.astype(jnp.float32)
    y = xf * lax.rsqrt(jnp.mean(jnp.square(xf), -1, keepdims=True) + NORM_EPS)
    if g is not None:
        y = y * g
    return y.astype(x.dtype)


def _modulate(x, m):
    return x * (1.0 + m[:, 1][:, None, :]) + m[:, 0][:, None, :]


def _gate(m, y):
    return (1.0 + m[:, 2][:, None, :]) * y


def _swiglu(u, w_i, w_o):
    g, up = jnp.split(u @ w_i, 2, axis=-1)
    return (jax.nn.silu(g) * up) @ w_o


def _split_heads(t, n):
    b, s, w = t.shape
    return t.reshape(b, s, n, w // n).transpose(0, 2, 1, 3)


def _merge_heads(t):
    b, h, s, d = t.shape
    return t.transpose(0, 2, 1, 3).reshape(b, s, h * d)


def _causal_dwconv(x, w):
    k = w.shape[0]
    return lax.conv_general_dilated(x, w[:, None, :].astype(x.dtype), window_strides=(1,),
                                    padding=[(k - 1, 0)],
                                    dimension_numbers=('NWC', 'WIO', 'NWC'),
                                    feature_group_count=x.shape[-1])


def _rope_tables(positions):
    half = MLA_ROPE // 2
    inv_freq = ROPE_THETA ** (-jnp.arange(half, dtype=jnp.float32) / half)
    ang = positions.astype(jnp.float32)[..., None] * inv_freq
    return jnp.cos(ang)[:, :, None, :], jnp.sin(ang)[:, :, None, :]


def _apply_rope(x, cos, sin):
    half = x.shape[-1] // 2
    x1 = x[..., :half].astype(jnp.float32)
    x2 = x[..., half:].astype(jnp.float32)
    return jnp.concatenate([x1 * cos - x2 * sin, x1 * sin + x2 * cos], -1).astype(x.dtype)


def _mlstm_chunkwise(q, k, v, i_pre, f_pre):
    bsz, nh, seq, dh = q.shape
    nc = seq // ML_CHUNK
    q, k, v = (t.reshape(bsz, nh, nc, ML_CHUNK, dh) for t in (q, k, v))
    log_i = i_pre.reshape(bsz, nh, nc, ML_CHUNK)
    b = jnp.cumsum(jax.nn.log_sigmoid(f_pre).reshape(bsz, nh, nc, ML_CHUNK), axis=-1)
    b_last = b[..., -1]
    a = b_last[..., None] - b + log_i
    a_max = jnp.max(a, -1)
    w = jnp.exp(a - a_max[..., None])
    c_loc = jnp.einsum('bhcs,bhcsd,bhcse->bhcde', w, k, v)
    n_loc = jnp.einsum('bhcs,bhcsd->bhcd', w, k)

    def step(carry, xs):
        c_st, n_st, m_st = carry
        bl, am, cl, nl = xs
        m_new = jnp.maximum(bl + m_st, am)
        dec = jnp.exp(bl + m_st - m_new)
        inj = jnp.exp(am - m_new)
        c_new = dec[..., None, None] * c_st + inj[..., None, None] * cl
        n_new = dec[..., None] * n_st + inj[..., None] * nl
        return (c_new, n_new, m_new), (c_st, n_st, m_st)

    init = (jnp.zeros((bsz, nh, dh, dh), jnp.float32),
            jnp.zeros((bsz, nh, dh), jnp.float32),
            jnp.zeros((bsz, nh), jnp.float32))
    xs = tuple(jnp.moveaxis(t, 2, 0) for t in (b_last, a_max, c_loc, n_loc))
    _, (c_prev, n_prev, m_prev) = lax.scan(step, init, xs)
    c_prev, n_prev, m_prev = (jnp.moveaxis(t, 0, 2) for t in (c_prev, n_prev, m_prev))

    causal = jnp.tril(jnp.ones((ML_CHUNK, ML_CHUNK), bool))
    d_mat = jnp.where(causal, b[..., :, None] - b[..., None, :] + log_i[..., None, :], -jnp.inf)
    m_inter = b + m_prev[..., None]
    m_t = jnp.maximum(m_inter, jnp.max(d_mat, -1))
    s = jnp.einsum('bhctd,bhcsd->bhcts', q, k) * jnp.exp(d_mat - m_t[..., None])
    inter = jnp.exp(m_inter - m_t)
    num = (jnp.einsum('bhcts,bhcse->bhcte', s, v)
           + inter[..., None] * jnp.einsum('bhctd,bhcde->bhcte', q, c_prev))
    den = jnp.sum(s, -1) + inter * jnp.einsum('bhctd,bhcd->bhct', q, n_prev)
    h = num / jnp.maximum(jnp.abs(den), jnp.exp(-m_t))[..., None]
    return h.reshape(bsz, nh, seq, dh).astype(v.dtype)


def _gla_chunkwise(q, k, v, log_a):
    bsz, nh, seq, dk = q.shape
    dv = v.shape[-1]
    nc = seq // GLA_CHUNK
    q = q.astype(jnp.float32).reshape(bsz, nh, nc, GLA_CHUNK, dk)
    k = k.astype(jnp.float32).reshape(bsz, nh, nc, GLA_CHUNK, dk)
    v = v.reshape(bsz, nh, nc, GLA_CHUNK, dv)
    bc = jnp.cumsum(log_a.reshape(bsz, nh, nc, GLA_CHUNK, dk), axis=-2)
    b_last = bc[..., -1, :]
    q_dec = q * jnp.exp(bc)
    causal = jnp.tril(jnp.ones((GLA_CHUNK, GLA_CHUNK), bool))
    attn = jnp.where(causal, jnp.einsum('bhctd,bhcsd->bhcts', q_dec, k * jnp.exp(-bc)), 0.0)
    o_intra = jnp.einsum('bhcts,bhcse->bhcte', attn, v)
    s_loc = jnp.einsum('bhcsd,bhcse->bhcde', k * jnp.exp(b_last[..., None, :] - bc), v)

    def step(s_st, xs):
        bl, sl = xs
        return jnp.exp(bl)[..., None] * s_st + sl, s_st

    _, s_prev = lax.scan(step, jnp.zeros((bsz, nh, dk, dv), jnp.float32),
                         (jnp.moveaxis(b_last, 2, 0), jnp.moveaxis(s_loc, 2, 0)))
    o_inter = jnp.einsum('bhctd,bhcde->bhcte', q_dec, jnp.moveaxis(s_prev, 0, 2))
    return (o_intra + o_inter).reshape(bsz, nh, seq, dv).astype(v.dtype)


def _mla_causal_attention(q_nope, q_rope, k_nope, k_rope, v):
    bsz, seq, nh, _ = q_nope.shape
    nb = seq // ATTN_BLOCK
    scale = (MLA_NOPE + MLA_ROPE) ** -0.5
    qn = q_nope.reshape(bsz, nb, ATTN_BLOCK, nh, MLA_NOPE).transpose(1, 0, 2, 3, 4)
    qr = q_rope.reshape(bsz, nb, ATTN_BLOCK, nh, MLA_ROPE).transpose(1, 0, 2, 3, 4)
    key_pos = jnp.arange(seq)

    def block(args):
        qn_b, qr_b, blk = args
        s = (jnp.einsum('bqhd,bkhd->bhqk', qn_b, k_nope)
             + jnp.einsum('bqhr,bkr->bhqk', qr_b, k_rope)).astype(jnp.float32) * scale
        q_pos = blk * ATTN_BLOCK + jnp.arange(ATTN_BLOCK)
        s = jnp.where(key_pos[None, :] <= q_pos[:, None], s, -jnp.inf)
        p = jax.nn.softmax(s, axis=-1).astype(v.dtype)
        return jnp.einsum('bhqk,bkhe->bqhe', p, v)

    out = lax.map(block, (qn, qr, jnp.arange(nb)))
    return out.transpose(1, 0, 2, 3, 4).reshape(bsz, seq, nh * MLA_DV)


def _token_mixer(u, cos, sin, w_in, ml_conv, ml_bi, ml_bf, gla_wg, gla_bg,
                 mla_gq, mla_wuq, mla_gkv, mla_wuk, mla_wuv, w_out):
    bsz, seq, _ = u.shape
    (ml_q, ml_k, ml_v, ml_i, ml_f, ml_o, gl_q, gl_k, gl_v, gl_lr, gl_r,
     c_q, c_kv, k_r) = jnp.split(u @ w_in, IN_SPLITS, axis=-1)

    qk = jax.nn.silu(_causal_dwconv(jnp.concatenate([ml_q, ml_k], -1), ml_conv))
    ml_q, ml_k = jnp.split(qk, 2, axis=-1)
    i_pre = jnp.swapaxes(ml_i.astype(jnp.float32) + ml_bi, 1, 2)
    f_pre = jnp.swapaxes(ml_f.astype(jnp.float32) + ml_bf, 1, 2)
    h = _mlstm_chunkwise(_split_heads(ml_q, ML_HEADS),
                         _split_heads(ml_k, ML_HEADS) * ML_DH ** -0.5,
                         _split_heads(ml_v, ML_HEADS), i_pre, f_pre)
    y_ml = jax.nn.sigmoid(ml_o) * _merge_heads(_head_layer_norm(h))

    log_a = jax.nn.log_sigmoid((gl_lr @ gla_wg + gla_bg).astype(jnp.float32)) / GLA_TAU
    o = _gla_chunkwise(_split_heads(gl_q, GLA_HEADS) * GLA_DK ** -0.5,
                       _split_heads(gl_k, GLA_HEADS), _split_heads(gl_v, GLA_HEADS),
                       _split_heads(log_a, GLA_HEADS))
    y_gla = jax.nn.silu(gl_r) * _merge_heads(_rms_norm(o))

    q = (_rms_norm(c_q, mla_gq) @ mla_wuq).reshape(bsz, seq, MLA_HEADS, MLA_NOPE + MLA_ROPE)
    q_nope = q[..., :MLA_NOPE]
    q_rope = _apply_rope(q[..., MLA_NOPE:], cos, sin)
    ckv = _rms_norm(c_kv, mla_gkv)
    k_nope = (ckv @ mla_wuk).reshape(bsz, seq, MLA_HEADS, MLA_NOPE)
    v = (ckv @ mla_wuv).reshape(bsz, seq, MLA_HEADS, MLA_DV)
    k_rope = _apply_rope(k_r[:, :, None, :], cos, sin)[:, :, 0]
    y_mla = _mla_causal_attention(q_nope, q_rope, k_nope, k_rope, v)

    return jnp.concatenate([y_ml, y_gla, y_mla], -1) @ w_out


def setup_inputs(seed: int = 0) -> dict:
    key = jax.random.key(seed)
    ks = jax.random.split(key, 24)

    def nrm(k, shape, std):
        return std * jax.random.normal(k, shape, jnp.float32)

    x = nrm(ks[0], (BATCH, SEQ, D_MODEL), 1.0)
    c = nrm(ks[1], (BATCH, D_MODEL), 1.0)
    positions = (jax.random.randint(ks[2], (BATCH, 1), 0, MAX_POS_OFFSET, jnp.int32)
                 + jnp.arange(SEQ, dtype=jnp.int32)[None, :])
    w_ada = nrm(ks[3], (DEPTH, D_MODEL, 9 * D_MODEL), ADA_STD * D_MODEL ** -0.5)
    b_ada = nrm(ks[4], (DEPTH, 9 * D_MODEL), 0.01)
    ln_g = 1.0 + nrm(ks[5], (DEPTH, 3, D_MODEL), 0.02)
    ln_b = nrm(ks[6], (DEPTH, 3, D_MODEL), 0.02)
    ffn1_wi = nrm(ks[7], (DEPTH, D_MODEL, 2 * FFN_DIM), D_MODEL ** -0.5)
    ffn1_wo = nrm(ks[8], (DEPTH, FFN_DIM, D_MODEL), DEEPNORM_BETA * FFN_DIM ** -0.5)
    ffn2_wi = nrm(ks[9], (DEPTH, D_MODEL, 2 * FFN_DIM), D_MODEL ** -0.5)
    ffn2_wo = nrm(ks[10], (DEPTH, FFN_DIM, D_MODEL), DEEPNORM_BETA * FFN_DIM ** -0.5)
    w_in = nrm(ks[11], (DEPTH, D_MODEL, D_IN), D_MODEL ** -0.5)
    ml_conv = nrm(ks[12], (DEPTH, ML_CONV, 2 * ML_W), ML_CONV ** -0.5)
    ml_bi = nrm(ks[13], (DEPTH, ML_HEADS), 0.1)
    ml_bf = (jnp.linspace(3.0, 6.0, ML_HEADS, dtype=jnp.float32)[None, :]
             + nrm(ks[14], (DEPTH, ML_HEADS), 0.1))
    gla_wg = nrm(ks[15], (DEPTH, GLA_RANK, GLA_KW), GLA_RANK ** -0.5)
    gla_bg = nrm(ks[16], (DEPTH, GLA_KW), 0.1)
    mla_gq = 1.0 + nrm(ks[17], (DEPTH, MLA_Q_RANK), 0.02)
    mla_wuq = nrm(ks[18], (DEPTH, MLA_Q_RANK, MLA_HEADS * (MLA_NOPE + MLA_ROPE)), MLA_Q_RANK ** -0.5)
    mla_gkv = 1.0 + nrm(ks[19], (DEPTH, MLA_KV_RANK), 0.02)
    mla_wuk = nrm(ks[20], (DEPTH, MLA_KV_RANK, MLA_HEADS * MLA_NOPE), MLA_KV_RANK ** -0.5)
    mla_wuv = nrm(ks[21], (DEPTH, MLA_KV_RANK, MLA_HEADS * MLA_DV), MLA_KV_RANK ** -0.5)
    w_out = nrm(ks[22], (DEPTH, D_MIX, D_MODEL), DEEPNORM_BETA * D_MIX ** -0.5)
    return {"x": x, "c": c, "positions": positions, "w_ada": w_ada, "b_ada": b_ada,
            "ln_g": ln_g, "ln_b": ln_b, "ffn1_wi": ffn1_wi, "ffn1_wo": ffn1_wo,
            "ffn2_wi": ffn2_wi, "ffn2_wo": ffn2_wo, "w_in": w_in, "ml_conv": ml_conv,
            "ml_bi": ml_bi, "ml_bf": ml_bf, "gla_wg": gla_wg, "gla_bg": gla_bg,
            "mla_gq": mla_gq, "mla_wuq": mla_wuq, "mla_gkv": mla_gkv, "mla_wuk": mla_wuk,
            "mla_wuv": mla_wuv, "w_out": w_out}


def reference(x, c, positions, w_ada, b_ada, ln_g, ln_b, ffn1_wi, ffn1_wo, ffn2_wi, ffn2_wo,
              w_in, ml_conv, ml_bi, ml_bf, gla_wg, gla_bg, mla_gq, mla_wuq, mla_gkv,
              mla_wuk, mla_wuv, w_out):
    cos, sin = _rope_tables(positions)
    c_act = jax.nn.silu(c)
    for l in range(DEPTH):
        mod = (c_act @ w_ada[l] + b_ada[l]).reshape(c.shape[0], 3, 3, D_MODEL)

        u = _modulate(x, mod[:, 0])
        r = FFN_RES_WEIGHT * _gate(mod[:, 0], _swiglu(u, ffn1_wi[l], ffn1_wo[l]))
        x = _layer_norm(DEEPNORM_ALPHA * x + r, ln_g[l, 0], ln_b[l, 0])

        u = _modulate(x, mod[:, 1])
        y = _token_mixer(u, cos, sin, w_in[l], ml_conv[l], ml_bi[l], ml_bf[l], gla_wg[l],
                         gla_bg[l], mla_gq[l], mla_wuq[l], mla_gkv[l], mla_wuk[l],
                         mla_wuv[l], w_out[l])
        x = _layer_norm(DEEPNORM_ALPHA * x + _gate(mod[:, 1], y), ln_g[l, 1], ln_b[l, 1])

        u = _modulate(x, mod[:, 2])
        r = FFN_RES_WEIGHT * _gate(mod[:, 2], _swiglu(u, ffn2_wi[l], ffn2_wo[l]))
        x = _layer_norm(DEEPNORM_ALPHA * x + r, ln_g[l, 2], ln_b[l, 2])
    return x
```

```python
import contextlib
import math
import numpy as np
import concourse.bass as bass
import concourse.mybir as mybir
from concourse.bass_utils import run_bass_kernel_spmd

F32 = mybir.dt.float32
BF16 = mybir.dt.bfloat16
I32 = mybir.dt.int32
U8 = mybir.dt.uint8
AF = mybir.ActivationFunctionType
ALU = mybir.AluOpType
AX = mybir.AxisListType

D = 2048
S = 4096
DEPTH = 4
FF = 5632
NK = 16
NJ = 44
T = 512
NT = S // T
ALPHA = (2.0 * DEPTH) ** 0.25
EPS = 1e-5
D_IN = 4696
C_MLQ, C_MLK, C_MLV, C_MLI, C_MLF, C_MLO = 0, 512, 1024, 1536, 1540, 1544
C_GLQ, C_GLK, C_GLV, C_GLLR, C_GLR = 2056, 2312, 2568, 3080, 3096
C_CQ, C_CKV, C_KR = 3608, 4120, 4632


class Reg:
    __slots__ = ("w", "r")

    def __init__(self):
        self.w = None
        self.r = []


class Emitter:
    ENGS = ("pe", "act", "dve", "pool", "sp")

    def __init__(self, nc, n_dma_sems=48):
        self.nc = nc
        self.thunks = {e: [] for e in self.ENGS}
        self.count = {e: 0 for e in self.ENGS}
        self.waited = {e: {} for e in self.ENGS}
        self.sem = {}
        self.dma_sems = []
        self.dma_sem_cnt = []
        self.n_dma_sems = n_dma_sems
        self.dma_rr = 0
        self.ninstr = 0

    def setup_sems(self, stack):
        for e in ("pe", "act", "dve", "pool"):
            self.sem[e] = stack.enter_context(self.nc.semaphore("c_" + e))
        for i in range(self.n_dma_sems):
            self.dma_sems.append(stack.enter_context(self.nc.semaphore("d%d" % i)))
            self.dma_sem_cnt.append(0)

    def _wait(self, eng, tok, force=False):
        kind, key, val = tok
        if kind == "eng" and key == eng and not force:
            return
        k = (kind, key)
        if self.waited[eng].get(k, 0) >= val:
            return
        self.waited[eng][k] = val
        sem = self.sem[key] if kind == "eng" else self.dma_sems[key]
        self.thunks[eng].append(lambda e, sem=sem, val=val: e.wait_ge(sem, val))

    def _deps(self, eng, reads, writes, force=False):
        for r in reads:
            if r.w is not None:
                self._wait(eng, r.w, force)
        for w in writes:
            if w.w is not None:
                self._wait(eng, w.w, force)
            for t in w.r:
                self._wait(eng, t, force)

    def _mark(self, tok, reads, writes):
        for r in reads:
            r.r.append(tok)
            if len(r.r) > 64:
                last = {}
                for t in r.r:
                    k = (t[0], t[1])
                    if k not in last or last[k][2] < t[2]:
                        last[k] = t
                r.r = list(last.values())
        for w in writes:
            w.w = tok
            w.r = []
        self.ninstr += 1

    def op(self, eng, fn, reads=(), writes=()):
        self._deps(eng, reads, writes)
        self.count[eng] += 1
        sem = self.sem[eng]
        self.thunks[eng].append(lambda e, fn=fn, sem=sem: fn(e).then_inc(sem, 1))
        tok = ("eng", eng, self.count[eng])
        self._mark(tok, reads, writes)
        return tok

    def dma(self, out, in_, reads=(), writes=(), eng="sp", **kw):
        i = self.dma_rr
        self.dma_rr = (self.dma_rr + 1) % self.n_dma_sems
        if self.dma_sem_cnt[i] > 0:
            self._wait(eng, ("dma", i, self.dma_sem_cnt[i]))
        self._deps(eng, reads, writes, force=True)
        self.dma_sem_cnt[i] += 16
        sem = self.dma_sems[i]
        self.thunks[eng].append(
            lambda e, out=out, in_=in_, sem=sem, kw=kw: e.dma_start(out=out, in_=in_, **kw).then_inc(sem, 16))
        tok = ("dma", i, self.dma_sem_cnt[i])
        self._mark(tok, reads, writes)
        return tok

    def barrier(self):
        toks = [("eng", e, self.count[e]) for e in ("pe", "act", "dve", "pool") if self.count[e] > 0]
        toks += [("dma", i, c) for i, c in enumerate(self.dma_sem_cnt) if c > 0]
        for e in self.ENGS:
            for t in toks:
                self._wait(e, t)

    def finish(self):
        self.barrier()
        nc = self.nc
        th = self.thunks
        with nc.Block() as block:
            @block.sync
            def _(e):
                for t in th["sp"]:
                    t(e)

            @block.tensor
            def _(e):
                for t in th["pe"]:
                    t(e)

            @block.scalar
            def _(e):
                for t in th["act"]:
                    t(e)

            @block.vector
            def _(e):
                for t in th["dve"]:
                    t(e)

            @block.gpsimd
            def _(e):
                for t in th["pool"]:
                    t(e)


class Buf:
    __slots__ = ("ap", "reg")

    def __init__(self, ap, reg=None):
        self.ap = ap
        self.reg = reg if reg is not None else Reg()


def build(n_layers=DEPTH, stop_after=None, dbg=False):
    nc = bass.Bass("TRN2", target_bir_lowering=False)
    st = contextlib.ExitStack()
    E = Emitter(nc)
    E.setup_sems(st)

    def din(name, shape, dt=F32):
        return nc.dram_tensor(name, list(shape), dt, kind="ExternalInput").ap()

    def dscr(name, shape, dt):
        return nc.dram_tensor(name, list(shape), dt, kind="Internal").ap()

    x_in = din("x", [D, S])
    c_in = din("c", [128, NK])
    pos_in = din("pos", [1, S], I32)
    invf_in = din("invf", [64, 1])
    w_ada = din("w_ada", [n_layers, D, 9 * D])
    b_ada = din("b_ada", [n_layers, 128, 144])
    ln_g = din("ln_g", [n_layers, 128, 3, NK])
    ln_b = din("ln_b", [n_layers, 128, 3, NK])
    ffn_wi = [din("ffn1_wi", [n_layers, D, 2 * FF]), din("ffn2_wi", [n_layers, D, 2 * FF])]
    ffn_wo = [din("ffn1_wo", [n_layers, FF, D]), din("ffn2_wo", [n_layers, FF, D])]
    w_in = din("w_in", [n_layers, D, D_IN])
    ml_conv = din("ml_conv", [n_layers, 128, 8, 4])
    ml_bi = din("ml_bi", [n_layers, 4, 1])
    ml_bf = din("ml_bf", [n_layers, 4, 1])
    gla_wg = din("gla_wg", [n_layers, 16, 256])
    gla_bg = din("gla_bg", [n_layers, 64, 4])
    mla_gq = din("mla_gq", [n_layers, 128, 4])
    mla_gkv = din("mla_gkv", [n_layers, 128, 4])
    mla_wuq = din("mla_wuq", [n_layers, 512, 1536])
    mla_wuk = din("mla_wuk", [n_layers, 512, 1024])
    mla_wuv = din("mla_wuv", [n_layers, 512, 1024])
    w_out = din("w_out", [n_layers, D, D])
    y_out = nc.dram_tensor("y", [D, S], F32, kind="ExternalOutput").ap()

    xs = Buf(dscr("xs", [D, S], F32))
    xs_tiles = [Reg() for _ in range(NT)]
    out_tiles = [Reg() for _ in range(NT)]

    ARENA = 206000
    arena = nc.alloc_sbuf_tensor("arena", [128, ARENA], U8)
    top = [0]

    def carve(nparts, free_shape, dt, pos=None):
        n = int(np.prod(free_shape))
        nbytes = n * (4 if dt in (F32, I32) else 2)
        if pos is None:
            off = top[0]
            top[0] = off + (nbytes + 63) // 64 * 64
            assert top[0] <= ARENA, ("sbuf overflow", top[0])
        else:
            off = pos[0]
            pos[0] = off + (nbytes + 63) // 64 * 64
            assert pos[0] <= ARENA, ("sbuf overflow (phase)", pos[0])
        ap = arena[0:nparts, off:off + nbytes].bitcast(dt)
        if len(free_shape) == 2:
            ap = ap.rearrange("p (a b) -> p a b", a=free_shape[0])
        elif len(free_shape) == 3:
            ap = ap.rearrange("p (a b c) -> p a b c", a=free_shape[0], b=free_shape[1])
        return Buf(ap)

    psum = [Buf(st.enter_context(nc.psum_tensor("ps%d" % i, [128, 512], F32))[:]) for i in range(8)]

    ones_bf = carve(128, [128], BF16)
    onesD = carve(128, [128], BF16)
    ones512 = carve(128, [128], BF16)
    ones128 = carve(128, [128], BF16)
    tri = carve(128, [128], BF16)
    ident = carve(128, [128], BF16)
    cact = carve(128, [NK], F32)
    modsb = carve(128, [144], F32)
    sc1 = carve(128, [3, NK], F32)
    gz = carve(128, [3, NK], F32)
    lng = carve(128, [3, NK], F32)
    lnb = carve(128, [3, NK], F32)
    tmp144 = carve(128, [144], F32)
    iot_i = carve(128, [128], I32)
    iot_f = carve(128, [128], F32)
    negpi = carve(128, [1], F32)
    CV_F = 2048
    cv_ld = [carve(128, [CV_F], F32) for _ in range(2)]
    cv_st = [carve(128, [CV_F], BF16) for _ in range(2)]
    persist_top = top[0]

    def mm(out, lhsT, rhs, start, stop, reads, writes):
        E.op("pe", lambda e: e.matmul(out, lhsT=lhsT, rhs=rhs, start=start, stop=stop),
             reads=reads, writes=writes)

    def rsqrt_inplace(b, eps, ap=None):
        a = b.ap if ap is None else ap
        E.op("act", lambda e: e.activation(out=a, in_=a, func=AF.Sqrt, bias=float(eps)),
             reads=[b.reg], writes=[b.reg])
        E.op("dve", lambda e: e.reciprocal(out=a, in_=a), reads=[b.reg], writes=[b.reg])

    E.op("pool", lambda e: e.memset(ones_bf.ap, 1.0), writes=[ones_bf.reg])
    E.op("pool", lambda e: e.memset(onesD.ap, 1.0 / D), writes=[onesD.reg])
    E.op("pool", lambda e: e.memset(ones512.ap, 1.0 / 512), writes=[ones512.reg])
    E.op("pool", lambda e: e.memset(ones128.ap, 1.0 / 128), writes=[ones128.reg])
    E.op("pool", lambda e: e.memset(negpi.ap, -math.pi), writes=[negpi.reg])
    E.op("pool", lambda e: e.iota(iot_i.ap, pattern=[[1, 128]], base=0, channel_multiplier=-1),
         writes=[iot_i.reg])
    E.op("dve", lambda e: e.tensor_copy(out=iot_f.ap, in_=iot_i.ap), reads=[iot_i.reg], writes=[iot_f.reg])
    E.op("dve", lambda e: e.tensor_single_scalar(out=tri.ap, in_=iot_f.ap, scalar=0.0, op=ALU.is_ge),
         reads=[iot_f.reg], writes=[tri.reg])
    E.op("dve", lambda e: e.tensor_single_scalar(out=ident.ap, in_=iot_f.ap, scalar=0.0, op=ALU.is_equal),
         reads=[iot_f.reg], writes=[ident.reg])
    E.dma(cact.ap, c_in, writes=[cact.reg])
    E.op("act", lambda e: e.activation(out=cact.ap, in_=cact.ap, func=AF.Silu), reads=[cact.reg], writes=[cact.reg])

    class Unit:
        __slots__ = ("dst", "regs", "jobs", "nemit", "shape")

    cv_jobs = []
    cv_state = {"i": 0, "n": 0}
    cv_pending = []

    def new_unit(name, free_shape, srcs):
        u = Unit()
        n = int(np.prod(free_shape))
        u.dst = dscr(name, [128, n], BF16)
        u.shape = list(free_shape)
        u.nemit = 0
        u.jobs = []
        u.regs = []
        dst3 = u.dst.rearrange("p (a b) -> p a b", a=free_shape[0])
        for src, sel in srcs:
            u.jobs.append((src, sel(dst3)))
            u.regs.append(Reg())
            cv_jobs.append((u, len(u.jobs) - 1))
        return u

    def flush_cv():
        while cv_pending:
            dst, sb, sreg, jreg = cv_pending.pop(0)
            E.dma(dst, sb, reads=[sreg], writes=[jreg], eng="act")

    def emit_job(u, ji):
        assert ji == u.nemit
        u.nemit += 1
        flush_cv()
        src, dst = u.jobs[ji]
        i = cv_state["n"] % len(cv_ld)
        cv_state["n"] += 1
        shp = src.shape
        n = int(np.prod(shp[1:]))
        assert n <= CV_F, n
        ld = cv_ld[i].ap[:, 0:n].rearrange("p (a b) -> p a b", a=shp[1])
        sb = cv_st[i].ap[:, 0:n].rearrange("p (a b) -> p a b", a=shp[1])
        E.dma(ld, src, writes=[cv_ld[i].reg], eng="act")
        E.op("pool", lambda e, ld=ld, sb=sb: e.tensor_copy(out=sb, in_=ld),
             reads=[cv_ld[i].reg], writes=[cv_st[i].reg])
        cv_pending.append((dst, sb, cv_st[i].reg, u.regs[ji]))

    def pump(n=1):
        while n > 0 and cv_state["i"] < len(cv_jobs):
            u, ji = cv_jobs[cv_state["i"]]
            cv_state["i"] += 1
            if ji >= u.nemit:
                emit_job(u, ji)
                n -= 1

    def load_unit(u, buf_ap, buf_reg):
        if u.nemit < len(u.jobs):
            while u.nemit < len(u.jobs):
                emit_job(u, u.nemit)
        if any(p[3] in u.regs for p in cv_pending):
            flush_cv()
        src = u.dst.rearrange("p (a b) -> p a b", a=u.shape[0])
        E.dma(buf_ap, src, reads=u.regs, writes=[buf_reg])

    class Stream:
        def __init__(self, seq, bufs, depth=None):
            self.seq = seq
            self.bufs = bufs
            self.depth = len(bufs) - 1 if depth is None else depth
            self.il = 0
            self.iu = 0

        def _view(self, k):
            u = self.seq[k]
            b = self.bufs[k % len(self.bufs)]
            n = int(np.prod(u.shape))
            flat = b.ap
            if len(flat.shape) == 3:
                flat = flat.rearrange("p a b -> p (a b)")
            return b, flat[:, 0:n].rearrange("p (a b) -> p a b", a=u.shape[0])

        def _fill(self):
            while self.il < len(self.seq) and self.il <= self.iu + self.depth:
                b, v = self._view(self.il)
                load_unit(self.seq[self.il], v, b.reg)
                self.il += 1

        def prime(self):
            self._fill()

        def get(self):
            self._fill()
            b, v = self._view(self.iu)
            self.iu += 1
            return b, v

    def wi_units(w, l, tag):
        wv = w[l].rearrange("(k p) n -> p k n", p=128)
        us = []
        for j in range(NJ):
            srcs = [(wv[:, :, j * 128:(j + 1) * 128], lambda d: d[:, :, 0:128]),
                    (wv[:, :, FF + j * 128:FF + (j + 1) * 128], lambda d: d[:, :, 128:256])]
            us.append(new_unit("%s_wi%d_%d" % (tag, l, j), [NK, 256], srcs))
        return us

    def wo_units(w, l, tag):
        wv = w[l].rearrange("(k p) n -> p k n", p=128)
        us = []
        for m in range(NK):
            srcs = []
            for k0, k1 in ((0, 16), (16, 32), (32, 44)):
                srcs.append((wv[:, k0:k1, m * 128:(m + 1) * 128], lambda d, k0=k0, k1=k1: d[:, k0:k1, :]))
            us.append(new_unit("%s_wo%d_%d" % (tag, l, m), [NJ, 128], srcs))
        return us

    def colunit(w2d, name, c0, width, nk=NK):
        wv = w2d.rearrange("(k p) n -> p k n", p=128)
        kp = max(1, CV_F // width)
        srcs = []
        for k0 in range(0, nk, kp):
            k1 = min(nk, k0 + kp)
            srcs.append((wv[:, k0:k1, c0:c0 + width], lambda d, k0=k0, k1=k1: d[:, k0:k1, :]))
        return new_unit(name, [nk, width], srcs)

    units = []
    _mix_decl = []

    def adaln(l, pos0):
        pos = [pos0]
        NB_ = 4
        wbuf = [carve(128, [NK, 128], F32, pos) for _ in range(NB_)]
        bsb = carve(128, [144], F32, pos)
        E.dma(bsb.ap, b_ada[l], writes=[bsb.reg])
        E.dma(lng.ap, ln_g[l], writes=[lng.reg])
        E.dma(lnb.ap, ln_b[l], writes=[lnb.reg])
        wv = w_ada[l].rearrange("(k p) n -> p k n", p=128)
        ps = psum[0]

        def ld(j):
            wb = wbuf[j % NB_]
            E.dma(wb.ap, wv[:, :, j * 128:(j + 1) * 128], writes=[wb.reg])
        for j in range(NB_ - 1):
            ld(j)
        for j in range(144):
            if j + NB_ - 1 < 144:
                ld(j + NB_ - 1)
            wb = wbuf[j % NB_]
            for k in range(NK):
                mm(ps.ap[:, j:j + 1], wb.ap[:, k, :], cact.ap[:, k:k + 1], k == 0, k == NK - 1,
                   [wb.reg, cact.reg], [ps.reg])
        E.op("dve", lambda e: e.tensor_tensor(out=modsb.ap, in0=ps.ap[:, 0:144], in1=bsb.ap, op=ALU.add),
             reads=[ps.reg, bsb.reg], writes=[modsb.reg])
        m4 = modsb.ap.rearrange("p (s w k) -> p s w k", s=3, w=3)
        E.op("dve", lambda e: e.tensor_scalar_add(out=sc1.ap, in0=m4[:, :, 1, :], scalar1=1.0),
             reads=[modsb.reg], writes=[sc1.reg])
        for s_ in range(3):
            rw = (0.5 if s_ != 1 else 1.0) / ALPHA
            E.op("dve", lambda e, s_=s_, rw=rw: e.tensor_scalar(
                out=gz.ap[:, s_, :], in0=m4[:, s_, 2, :], scalar1=1.0, scalar2=rw, op0=ALU.add, op1=ALU.mult),
                reads=[modsb.reg], writes=[gz.reg])

    def shift_ap(s_):
        return modsb.ap.rearrange("p (s w k) -> p s w k", s=3, w=3)[:, s_, 0, :]

    def load_x_tile(src_ap, src_reg, ti, x32):
        v = src_ap.rearrange("(k p) t -> p k t", p=128)[:, :, ti * T:(ti + 1) * T]
        E.dma(x32.ap, v, reads=[src_reg], writes=[x32.reg])

    def modulate(s_, x32, u):
        sh = shift_ap(s_)
        for k in range(NK):
            E.op("dve", lambda e, k=k: e.tensor_scalar(
                out=u.ap[:, k, :], in0=x32.ap[:, k, :], scalar1=sc1.ap[:, s_, k:k + 1], scalar2=sh[:, k:k + 1],
                op0=ALU.mult, op1=ALU.add), reads=[x32.reg, sc1.reg, modsb.reg], writes=[u.reg])

    def proj_ln(s_, hbuf, nkk, wstream, x32, tmp, dst_ap, dst_reg, ti):
        ps_y = [psum[4], psum[5]]
        ps_mu, ps_sq = psum[6], psum[7]
        zb, sqb = tmp["zb"], tmp["sqb"]

        def stats(m):
            z = zb[m % 2]
            q = sqb[m % 2]
            mm(ps_mu.ap, onesD.ap, z.ap, m == 0, m == NK - 1, [onesD.reg, z.reg], [ps_mu.reg])
            mm(ps_sq.ap, onesD.ap, q.ap, m == 0, m == NK - 1, [onesD.reg, q.reg], [ps_sq.reg])

        for m in range(NK):
            wb, wap = wstream.get()
            py = ps_y[m % 2]
            for kk in range(nkk):
                mm(py.ap, wap[:, kk, :], hbuf.ap[:, kk, :], kk == 0, kk == nkk - 1,
                   [wb.reg, hbuf.reg], [py.reg])
            if m >= 1:
                stats(m - 1)
            E.op("dve", lambda e, m=m, py=py: e.scalar_tensor_tensor(
                out=x32.ap[:, m, :], in0=py.ap, scalar=gz.ap[:, s_, m:m + 1], in1=x32.ap[:, m, :],
                op0=ALU.mult, op1=ALU.add), reads=[py.reg, gz.reg, x32.reg], writes=[x32.reg])
            z = zb[m % 2]
            q = sqb[m % 2]
            E.op("act", lambda e, m=m, z=z: e.activation(out=z.ap, in_=x32.ap[:, m, :], func=AF.Copy),
                 reads=[x32.reg], writes=[z.reg])
            E.op("act", lambda e, m=m, q=q: e.activation(out=q.ap, in_=x32.ap[:, m, :], func=AF.Square),
                 reads=[x32.reg], writes=[q.reg])
        stats(NK - 1)
        mean, rstd = tmp["mean"], tmp["rstd"]
        E.op("act", lambda e: e.activation(out=mean.ap, in_=ps_mu.ap, func=AF.Copy),
             reads=[ps_mu.reg], writes=[mean.reg])
        E.op("dve", lambda e: e.tensor_tensor(out=rstd.ap, in0=mean.ap, in1=mean.ap, op=ALU.mult),
             reads=[mean.reg], writes=[rstd.reg])
        E.op("dve", lambda e: e.tensor_tensor(out=rstd.ap, in0=ps_sq.ap, in1=rstd.ap, op=ALU.subtract),
             reads=[ps_sq.reg, rstd.reg], writes=[rstd.reg])
        rsqrt_inplace(rstd, EPS / (ALPHA * ALPHA))
        t1 = tmp["t1"]
        for k in range(NK):
            a = t1[k % 2]
            E.op("pool", lambda e, k=k, a=a: e.tensor_tensor(out=a.ap, in0=x32.ap[:, k, :], in1=mean.ap,
                                                           op=ALU.subtract),
                 reads=[x32.reg, mean.reg], writes=[a.reg])
            E.op("dve", lambda e, k=k, a=a: e.tensor_tensor(out=a.ap, in0=a.ap, in1=rstd.ap, op=ALU.mult),
                 reads=[a.reg, rstd.reg], writes=[a.reg])
            E.op("act", lambda e, k=k, a=a: e.activation(
                out=x32.ap[:, k, :], in_=a.ap, func=AF.Identity, scale=lng.ap[:, s_, k:k + 1],
                bias=lnb.ap[:, s_, k:k + 1]), reads=[a.reg, lng.reg, lnb.reg], writes=[x32.reg])
        v = dst_ap.rearrange("(k p) t -> p k t", p=128)[:, :, ti * T:(ti + 1) * T]
        E.dma(v, x32.ap, reads=[x32.reg], writes=[dst_reg], eng="act")

    def ln_tmps(pos):
        return {
            "zb": [carve(128, [T], BF16, pos) for _ in range(2)],
            "sqb": [carve(128, [T], BF16, pos) for _ in range(2)],
            "mean": carve(128, [T], F32, pos),
            "rstd": carve(128, [T], F32, pos),
            "t1": [carve(128, [T], F32, pos) for _ in range(2)],
        }

    def ffn(l, s_, wiu, wou, src_ap, src_regs, dst_ap, dst_regs):
        E.barrier()
        pos = [persist_top]
        x32 = carve(128, [NK, T], F32, pos)
        u = carve(128, [NK, T], BF16, pos)
        h = carve(128, [NJ, T], BF16, pos)
        wib = [carve(128, [NK, 256], BF16, pos) for _ in range(3)]
        wob = [carve(128, [NJ, 128], BF16, pos) for _ in range(3)]
        sg = [carve(128, [T], F32, pos) for _ in range(2)]
        tmp = ln_tmps(pos)
        wis = Stream(list(wiu) * NT, wib)
        wos = Stream(list(wou) * NT, wob)
        wis.prime()
        for ti in range(NT):
            load_x_tile(src_ap, src_regs[ti], ti, x32)
            modulate(s_, x32, u)
            for j in range(NJ):
                wb, wap = wis.get()
                if j == 2:
                    wos.prime()
                pg, pu = psum[j % 2], psum[2 + j % 2]
                for k in range(NK):
                    mm(pg.ap, wap[:, k, 0:128], u.ap[:, k, :], k == 0, k == NK - 1, [wb.reg, u.reg], [pg.reg])
                for k in range(NK):
                    mm(pu.ap, wap[:, k, 128:256], u.ap[:, k, :], k == 0, k == NK - 1, [wb.reg, u.reg], [pu.reg])
                g_ = sg[j % 2]
                E.op("act", lambda e, pg=pg, g_=g_: e.activation(out=g_.ap, in_=pg.ap, func=AF.Silu),
                     reads=[pg.reg], writes=[g_.reg])
                E.op("dve", lambda e, pu=pu, g_=g_, j=j: e.tensor_tensor(
                    out=h.ap[:, j, :], in0=pu.ap, in1=g_.ap, op=ALU.mult),
                    reads=[pu.reg, g_.reg], writes=[h.reg])
                if j % 2 == 0:
                    pump(1)
            proj_ln(s_, h, NJ, wos, x32, tmp, dst_ap, dst_regs[ti], ti)

    mlq32 = Buf(dscr("mlq32", [512, S], F32)); mlk32 = Buf(dscr("mlk32", [512, S], F32))
    gate_i = Buf(dscr("gate_i", [4, S], F32)); gate_f = Buf(dscr("gate_f", [4, S], F32))
    mlv_tm = Buf(dscr("mlv_tm", [S, 512], BF16)); mlo_s = Buf(dscr("mlo_s", [512, S], BF16))
    glq32 = Buf(dscr("glq32", [4, 64, S], F32)); glk32 = Buf(dscr("glk32", [4, 64, S], F32))
    gla_neg = Buf(dscr("gla_neg", [4, 64, S], F32))
    glv_tm = Buf(dscr("glv_tm", [S, 512], BF16)); glr_s = Buf(dscr("glr_s", [512, S], BF16))
    QN = Buf(dscr("QN", [8, 128, S], BF16)); QR = Buf(dscr("QR", [8, 64, S], BF16))
    KN = Buf(dscr("KN", [8, 128, S], BF16)); KR = Buf(dscr("KR", [64, S], BF16))
    V_tm = Buf(dscr("V_tm", [S, 1024], BF16))
    ymix = Buf(dscr("ymix", [D, S], BF16))
    cos_s = Buf(dscr("cos_s", [64, S], F32)); sin_s = Buf(dscr("sin_s", [64, S], F32))

    def mixer_units(l):
        U = {}
        wi2 = w_in[l]
        wv = wi2.rearrange("(k p) n -> p k n", p=128)
        for nm, c0 in (("mlq0", C_MLQ), ("mlq1", C_MLQ + 256), ("mlk0", C_MLK), ("mlk1", C_MLK + 256),
                       ("mlo0", C_MLO), ("mlo1", C_MLO + 256), ("glr0", C_GLR), ("glr1", C_GLR + 256),
                       ("cq0", C_CQ), ("cq1", C_CQ + 256), ("ckv0", C_CKV), ("ckv1", C_CKV + 256),
                       ("glq", C_GLQ), ("glk", C_GLK),
                       ("mlv0", C_MLV), ("mlv1", C_MLV + 256), ("glv0", C_GLV), ("glv1", C_GLV + 256)):
            U[nm] = colunit(wi2, "u%d_%s" % (l, nm), c0, 256)
        U["small"] = new_unit("u%d_small" % l, [NK, 152], [
            (wv[:, :, C_KR:C_KR + 64], lambda d: d[:, :, 0:64]),
            (wv[:, :, C_KR + 32:C_KR + 64], lambda d: d[:, :, 64:96]),
            (wv[:, :, C_KR:C_KR + 32], lambda d: d[:, :, 96:128]),
            (wv[:, :, C_GLLR:C_GLLR + 16], lambda d: d[:, :, 128:144]),
            (wv[:, :, C_MLI:C_MLI + 8], lambda d: d[:, :, 144:152])])
        wq = mla_wuq[l].rearrange("(k p) n -> p k n", p=128)
        for g in range(4):
            U["qn%d" % g] = new_unit("u%d_qn%d" % (l, g), [4, 256], [
                (wq[:, :, (2 * g) * 192:(2 * g) * 192 + 128], lambda d: d[:, :, 0:128]),
                (wq[:, :, (2 * g + 1) * 192:(2 * g + 1) * 192 + 128], lambda d: d[:, :, 128:256])])
            srcs = []
            for hh in range(2):
                b0 = (2 * g + hh) * 192 + 128
                o = hh * 128
                srcs.append((wq[:, :, b0:b0 + 64], lambda d, o=o: d[:, :, o:o + 64]))
                srcs.append((wq[:, :, b0 + 32:b0 + 64], lambda d, o=o: d[:, :, o + 64:o + 96]))
                srcs.append((wq[:, :, b0:b0 + 32], lambda d, o=o: d[:, :, o + 96:o + 128]))
            U["qr%d" % g] = new_unit("u%d_qr%d" % (l, g), [4, 256], srcs)
            U["kn%d" % g] = colunit(mla_wuk[l], "u%d_kn%d" % (l, g), g * 256, 256, nk=4)
            U["vv%d" % g] = colunit(mla_wuv[l], "u%d_vv%d" % (l, g), g * 256, 256, nk=4)
        U["wo"] = [colunit(w_out[l], "u%d_wout%d" % (l, m), m * 128, 128) for m in range(NK)]
        return U

    def rsqrt_to(outb, in_ap, in_regs, eps, out_ap=None):
        o = outb.ap if out_ap is None else out_ap
        E.op("act", lambda e: e.activation(out=o, in_=in_ap, func=AF.Sqrt, bias=float(eps)),
             reads=in_regs, writes=[outb.reg])
        E.op("dve", lambda e: e.reciprocal(out=o, in_=o), reads=[outb.reg], writes=[outb.reg])

    def rope_tables():
        E.barrier()
        pos = [persist_top]
        pi_ = carve(64, [S], I32, pos)
        pf = carve(64, [S], F32, pos)
        ang = carve(64, [S], F32, pos)
        res = carve(64, [S], F32, pos)
        invf = carve(64, [1], F32, pos)
        E.dma(invf.ap, invf_in, writes=[invf.reg])
        E.dma(pi_.ap, pos_in.partition_broadcast(64), writes=[pi_.reg])
        E.op("dve", lambda e: e.tensor_copy(out=pf.ap, in_=pi_.ap), reads=[pi_.reg], writes=[pf.reg])
        E.op("dve", lambda e: e.tensor_scalar_mul(out=pf.ap, in0=pf.ap, scalar1=invf.ap[:, 0:1]),
             reads=[pf.reg, invf.reg], writes=[pf.reg])
        twopi = 2.0 * math.pi
        ki = carve(64, [S], I32, pos)
        kf = carve(64, [S], F32, pos)

        def reduce_sin(shift, negate_lo, dstb):
            E.op("dve", lambda e: e.tensor_scalar(out=kf.ap, in0=pf.ap, scalar1=float(shift), scalar2=1.0 / twopi,
                                                  op0=ALU.add, op1=ALU.mult), reads=[pf.reg], writes=[kf.reg])
            E.op("dve", lambda e: e.tensor_copy(out=ki.ap, in_=kf.ap), reads=[kf.reg], writes=[ki.reg])
            E.op("dve", lambda e: e.tensor_copy(out=kf.ap, in_=ki.ap), reads=[ki.reg], writes=[kf.reg])
            E.op("dve", lambda e: e.tensor_scalar_add(out=ang.ap, in0=pf.ap, scalar1=float(shift)),
                 reads=[pf.reg, res.reg], writes=[ang.reg])
            E.op("dve", lambda e: e.scalar_tensor_tensor(out=ang.ap, in0=kf.ap, scalar=-twopi, in1=ang.ap,
                                                         op0=ALU.mult, op1=ALU.add), reads=[kf.reg, ang.reg], writes=[ang.reg])
            E.op("dve", lambda e: e.tensor_single_scalar(out=kf.ap, in_=ang.ap, scalar=math.pi, op=ALU.is_gt),
                 reads=[ang.reg], writes=[kf.reg])
            E.op("dve", lambda e: e.scalar_tensor_tensor(out=ang.ap, in0=kf.ap, scalar=-twopi, in1=ang.ap,
                                                         op0=ALU.mult, op1=ALU.add), reads=[kf.reg, ang.reg], writes=[ang.reg])
            E.op("dve", lambda e: e.tensor_single_scalar(out=kf.ap, in_=ang.ap, scalar=-math.pi, op=ALU.is_lt),
                 reads=[ang.reg], writes=[kf.reg])
            E.op("dve", lambda e: e.scalar_tensor_tensor(out=ang.ap, in0=kf.ap, scalar=twopi, in1=ang.ap,
                                                         op0=ALU.mult, op1=ALU.add), reads=[kf.reg, ang.reg], writes=[ang.reg])
            E.op("dve", lambda e: e.tensor_scalar(out=ang.ap, in0=ang.ap, scalar1=-3.1415925, scalar2=3.1415925,
                                                  op0=ALU.max, op1=ALU.min), reads=[ang.reg], writes=[ang.reg])
            E.op("act", lambda e: e.activation(out=res.ap, in_=ang.ap, func=AF.Sin), reads=[ang.reg, dstb.reg], writes=[res.reg])
            if negate_lo:
                E.op("dve", lambda e: e.tensor_scalar_mul(out=res.ap[0:32, :], in0=res.ap[0:32, :], scalar1=-1.0),
                     reads=[res.reg], writes=[res.reg])
            E.dma(dstb.ap, res.ap, reads=[res.reg], writes=[dstb.reg])

        reduce_sin(0.0, True, sin_s)
        reduce_sin(0.5 * math.pi, False, cos_s)

    def inproj(l, U):
        E.barrier()
        pos = [persist_top]
        x32 = carve(128, [NK, T], F32, pos)
        u = carve(128, [NK, T], BF16, pos)
        wb = [carve(128, [NK, 256], BF16, pos) for _ in range(4)]
        conv = carve(128, [8, 4], F32, pos)
        halo = carve(128, [8, 3], F32, pos)
        bi = carve(4, [1], F32, pos); nbf = carve(4, [1], F32, pos)
        wg = carve(16, [256], F32, pos); nbg = carve(64, [4], F32, pos)
        gq = carve(128, [4], F32, pos); gkv = carve(128, [4], F32, pos)
        pre = [carve(128, [T + 3], F32, pos) for _ in range(2)]
        acc = [carve(128, [T], F32, pos) for _ in range(2)]
        stg32 = [carve(128, [T], F32, pos) for _ in range(3)]
        stg16 = [carve(128, [T], BF16, pos) for _ in range(3)]
        lr32 = carve(16, [T], F32, pos)
        cq32 = carve(128, [4, T], F32, pos)
        cqn = carve(128, [4, T], BF16, pos)
        ckvn = carve(128, [4, T], BF16, pos)
        sqb = [carve(128, [T], BF16, pos) for _ in range(2)]
        rstd = carve(128, [T], F32, pos)
        cs = carve(64, [T], F32, pos); sn = carve(64, [T], F32, pos)
        rt1 = [carve(64, [T], F32, pos) for _ in range(2)]
        rt2 = [carve(64, [T], F32, pos) for _ in range(2)]
        cnt = {"ps": 0, "s32": 0, "s16": 0, "pre": 0, "rt": 0}

        E.dma(conv.ap, ml_conv[l], writes=[conv.reg])
        E.dma(bi.ap, ml_bi[l], writes=[bi.reg])
        E.dma(nbf.ap, ml_bf[l], writes=[nbf.reg])
        E.op("dve", lambda e: e.tensor_scalar_mul(out=nbf.ap, in0=nbf.ap, scalar1=-1.0), reads=[nbf.reg], writes=[nbf.reg])
        E.dma(wg.ap, gla_wg[l], writes=[wg.reg])
        E.dma(nbg.ap, gla_bg[l], writes=[nbg.reg])
        E.op("dve", lambda e: e.tensor_scalar_mul(out=nbg.ap, in0=nbg.ap, scalar1=-1.0), reads=[nbg.reg], writes=[nbg.reg])
        E.dma(gq.ap, mla_gq[l], writes=[gq.reg])
        E.dma(gkv.ap, mla_gkv[l], writes=[gkv.reg])
        E.op("pool", lambda e: e.memset(halo.ap, 0.0), writes=[halo.reg])

        def nxt(key, n):
            v = cnt[key] % n
            cnt[key] += 1
            return v

        order = (["mlq0", "mlq1", "mlk0", "mlk1", "mlo0", "mlo1", "glr0", "glr1", "glq", "glk", "small",
                  "mlv0", "mlv1", "glv0", "glv1", "cq0", "cq1", "ckv0", "ckv1"]
                 + [n_ % g for g in range(4) for n_ in ("qn%d", "qr%d", "kn%d", "vv%d")])
        ws = Stream([U[n_] for n_ in order] * NT, wb)
        ws.prime()

        def getw(unit, nk=NK):
            b, v = ws.get()
            assert ws.seq[ws.iu - 1] is unit
            return b, v

        def proj_fm(wap, wreg, c0, w, rhs, rreg, nk=NK, tcols=None):
            ps = psum[nxt("ps", 4)]
            for k in range(nk):
                mm(ps.ap[0:w, :], wap[:, k, c0:c0 + w], rhs.ap[:, k, :], k == 0, k == nk - 1,
                   [wreg, rreg], [ps.reg])
            return ps

        def store(dst_ap, dst_reg, sb, sb_ap):
            E.dma(dst_ap, sb_ap, reads=[sb.reg], writes=[dst_reg], eng="act")

        for ti in range(NT):
            tsl = slice(ti * T, (ti + 1) * T)
            load_x_tile(xs.ap, xs_tiles[ti], ti, x32)
            modulate(1, x32, u)
            E.dma(cs.ap, cos_s.ap[:, tsl], reads=[cos_s.reg], writes=[cs.reg])
            E.dma(sn.ap, sin_s.ap[:, tsl], reads=[sin_s.reg], writes=[sn.reg])

            for gi, (nm, dstb) in enumerate((("mlq0", mlq32), ("mlq1", mlq32), ("mlk0", mlk32), ("mlk1", mlk32))):
                b, wap = getw(U[nm])
                for sub in range(2):
                    ch = gi * 2 + sub
                    ps = proj_fm(wap, b.reg, sub * 128, 128, u, u.reg)
                    p_ = pre[nxt("pre", 2)]
                    a_ = acc[cnt["pre"] % 2]
                    E.op("act", lambda e, p_=p_, ps=ps: e.activation(out=p_.ap[:, 3:T + 3], in_=ps.ap, func=AF.Copy),
                         reads=[ps.reg], writes=[p_.reg])
                    E.op("pool", lambda e, p_=p_, ch=ch: e.tensor_copy(out=p_.ap[:, 0:3], in_=halo.ap[:, ch, :]),
                         reads=[halo.reg], writes=[p_.reg])
                    E.op("dve", lambda e, p_=p_, a_=a_, ch=ch: e.tensor_scalar_mul(
                        out=a_.ap, in0=p_.ap[:, 0:T], scalar1=conv.ap[:, ch, 0:1]),
                        reads=[p_.reg, conv.reg], writes=[a_.reg])
                    for j in range(1, 4):
                        E.op("dve", lambda e, p_=p_, a_=a_, ch=ch, j=j: e.scalar_tensor_tensor(
                            out=a_.ap, in0=p_.ap[:, j:j + T], scalar=conv.ap[:, ch, j:j + 1], in1=a_.ap,
                            op0=ALU.mult, op1=ALU.add), reads=[p_.reg, conv.reg, a_.reg], writes=[a_.reg])
                    E.op("pool", lambda e, p_=p_, ch=ch: e.tensor_copy(out=halo.ap[:, ch, :], in_=p_.ap[:, T:T + 3]),
                         reads=[p_.reg], writes=[halo.reg])
                    s_ = stg32[nxt("s32", 3)]
                    E.op("act", lambda e, a_=a_, s_=s_: e.activation(out=s_.ap, in_=a_.ap, func=AF.Silu),
                         reads=[a_.reg], writes=[s_.reg])
                    row = (ch % 4) * 128
                    store(dstb.ap[row:row + 128, tsl], dstb.reg, s_, s_.ap)
            for gi, (nm, dstb, fn) in enumerate((("mlo0", mlo_s, AF.Sigmoid), ("mlo1", mlo_s, AF.Sigmoid),
                                                 ("glr0", glr_s, AF.Silu), ("glr1", glr_s, AF.Silu))):
                b, wap = getw(U[nm])
                for sub in range(2):
                    ps = proj_fm(wap, b.reg, sub * 128, 128, u, u.reg)
                    s_ = stg16[nxt("s16", 3)]
                    E.op("act", lambda e, ps=ps, s_=s_, fn=fn: e.activation(out=s_.ap, in_=ps.ap, func=fn),
                         reads=[ps.reg], writes=[s_.reg])
                    row = ((gi % 2) * 2 + sub) * 128
                    store(dstb.ap[row:row + 128, tsl], dstb.reg, s_, s_.ap)
            for nm, dstb in (("glq", glq32), ("glk", glk32)):
                b, wap = getw(U[nm])
                for hh in range(4):
                    ps = proj_fm(wap, b.reg, hh * 64, 64, u, u.reg)
                    s_ = stg32[nxt("s32", 3)]
                    E.op("act", lambda e, ps=ps, s_=s_: e.activation(out=s_.ap[0:64, :], in_=ps.ap[0:64, :], func=AF.Copy),
                         reads=[ps.reg], writes=[s_.reg])
                    store(dstb.ap[hh, :, tsl], dstb.reg, s_, s_.ap[0:64, :])
            b, wap = getw(U["small"])
            ps_r = proj_fm(wap, b.reg, 0, 64, u, u.reg)
            ps_w = proj_fm(wap, b.reg, 64, 64, u, u.reg)
            ri = nxt("rt", 2)
            E.op("dve", lambda e, ps_r=ps_r, ri=ri: e.tensor_tensor(out=rt1[ri].ap, in0=ps_r.ap[0:64, :], in1=cs.ap, op=ALU.mult),
                 reads=[ps_r.reg, cs.reg], writes=[rt1[ri].reg])
            E.op("dve", lambda e, ps_w=ps_w, ri=ri: e.tensor_tensor(out=rt2[ri].ap, in0=ps_w.ap[0:64, :], in1=sn.ap, op=ALU.mult),
                 reads=[ps_w.reg, sn.reg], writes=[rt2[ri].reg])
            s_ = stg16[nxt("s16", 3)]
            E.op("pool", lambda e, s_=s_, ri=ri: e.tensor_tensor(out=s_.ap[0:64, :], in0=rt1[ri].ap, in1=rt2[ri].ap, op=ALU.add),
                 reads=[rt1[ri].reg, rt2[ri].reg], writes=[s_.reg])
            store(KR.ap[:, tsl], KR.reg, s_, s_.ap[0:64, :])
            ps = proj_fm(wap, b.reg, 128, 16, u, u.reg)
            E.op("act", lambda e, ps=ps: e.activation(out=lr32.ap, in_=ps.ap[0:16, :], func=AF.Copy),
                 reads=[ps.reg], writes=[lr32.reg])
            for hh in range(4):
                ps2 = psum[nxt("ps", 4)]
                mm(ps2.ap[0:64, :], wg.ap[:, hh * 64:(hh + 1) * 64], lr32.ap, True, True, [wg.reg, lr32.reg], [ps2.reg])
                s_ = stg32[nxt("s32", 3)]
                E.op("act", lambda e, ps2=ps2, s_=s_, hh=hh: e.activation(
                    out=s_.ap[0:64, :], in_=ps2.ap[0:64, :], func=AF.Exp, scale=-1.0, bias=nbg.ap[:, hh:hh + 1]),
                    reads=[ps2.reg, nbg.reg], writes=[s_.reg])
                E.op("act", lambda e, s_=s_: e.activation(out=s_.ap[0:64, :], in_=s_.ap[0:64, :], func=AF.Ln, bias=1.0),
                     reads=[s_.reg], writes=[s_.reg])
                store(gla_neg.ap[hh, :, tsl], gla_neg.reg, s_, s_.ap[0:64, :])
            ps = proj_fm(wap, b.reg, 144, 4, u, u.reg)
            s_ = stg32[nxt("s32", 3)]
            E.op("act", lambda e, ps=ps, s_=s_: e.activation(out=s_.ap[0:4, :], in_=ps.ap[0:4, :], func=AF.Identity, bias=bi.ap[:, 0:1]),
                 reads=[ps.reg, bi.reg], writes=[s_.reg])
            store(gate_i.ap[:, tsl], gate_i.reg, s_, s_.ap[0:4, :])
            ps = proj_fm(wap, b.reg, 148, 4, u, u.reg)
            s_ = stg32[nxt("s32", 3)]
            E.op("act", lambda e, ps=ps, s_=s_: e.activation(out=s_.ap[0:4, :], in_=ps.ap[0:4, :], func=AF.Exp, scale=-1.0, bias=nbf.ap[:, 0:1]),
                 reads=[ps.reg, nbf.reg], writes=[s_.reg])
            E.op("act", lambda e, s_=s_: e.activation(out=s_.ap[0:4, :], in_=s_.ap[0:4, :], func=AF.Ln, bias=1.0),
                 reads=[s_.reg], writes=[s_.reg])
            store(gate_f.ap[:, tsl], gate_f.reg, s_, s_.ap[0:4, :])
            for nm, dstb, c0 in (("mlv0", mlv_tm, 0), ("mlv1", mlv_tm, 256), ("glv0", glv_tm, 0), ("glv1", glv_tm, 256)):
                b, wap = getw(U[nm])
                for blk in range(4):
                    ps = psum[4 + nxt("ps", 4) % 2]
                    for k in range(NK):
                        mm(ps.ap[:, 0:256], u.ap[:, k, blk * 128:(blk + 1) * 128], wap[:, k, :], k == 0, k == NK - 1,
                           [b.reg, u.reg], [ps.reg])
                    s_ = stg16[nxt("s16", 3)]
                    E.op("act", lambda e, ps=ps, s_=s_: e.activation(out=s_.ap[:, 0:256], in_=ps.ap[:, 0:256], func=AF.Copy),
                         reads=[ps.reg], writes=[s_.reg])
                    r0 = ti * T + blk * 128
                    store(dstb.ap[r0:r0 + 128, c0:c0 + 256], dstb.reg, s_, s_.ap[:, 0:256])
            for nm0, nm1, gsb, outn in (("cq0", "cq1", gq, cqn), ("ckv0", "ckv1", gkv, ckvn)):
                ps_ms = psum[6]
                for gi, nm in enumerate((nm0, nm1)):
                    b, wap = getw(U[nm])
                    for sub in range(2):
                        cidx = gi * 2 + sub
                        ps = proj_fm(wap, b.reg, sub * 128, 128, u, u.reg)
                        E.op("act", lambda e, ps=ps, cidx=cidx: e.activation(out=cq32.ap[:, cidx, :], in_=ps.ap, func=AF.Copy),
                             reads=[ps.reg], writes=[cq32.reg])
                        q_ = sqb[cidx % 2]
                        E.op("act", lambda e, ps=ps, q_=q_: e.activation(out=q_.ap, in_=ps.ap, func=AF.Square),
                             reads=[ps.reg], writes=[q_.reg])
                        mm(ps_ms.ap, ones512.ap, q_.ap, cidx == 0, cidx == 3, [ones512.reg, q_.reg], [ps_ms.reg])
                rsqrt_to(rstd, ps_ms.ap, [ps_ms.reg], EPS)
                for cidx in range(4):
                    E.op("dve", lambda e, cidx=cidx, gsb=gsb, outn=outn: e.scalar_tensor_tensor(
                        out=outn.ap[:, cidx, :], in0=cq32.ap[:, cidx, :], scalar=gsb.ap[:, cidx:cidx + 1], in1=rstd.ap,
                        op0=ALU.mult, op1=ALU.mult), reads=[cq32.reg, gsb.reg, rstd.reg], writes=[outn.reg])
            for g in range(4):
                b, wap = getw(U["qn%d" % g], nk=4)
                for hh in range(2):
                    ps = proj_fm(wap, b.reg, hh * 128, 128, cqn, cqn.reg, nk=4)
                    s_ = stg16[nxt("s16", 3)]
                    E.op("act", lambda e, ps=ps, s_=s_: e.activation(out=s_.ap, in_=ps.ap, func=AF.Copy),
                         reads=[ps.reg], writes=[s_.reg])
                    store(QN.ap[2 * g + hh, :, tsl], QN.reg, s_, s_.ap)
                b, wap = getw(U["qr%d" % g], nk=4)
                for hh in range(2):
                    ps_r = proj_fm(wap, b.reg, hh * 128, 64, cqn, cqn.reg, nk=4)
                    ps_w = proj_fm(wap, b.reg, hh * 128 + 64, 64, cqn, cqn.reg, nk=4)
                    ri = nxt("rt", 2)
                    E.op("dve", lambda e, ps_r=ps_r, ri=ri: e.tensor_tensor(out=rt1[ri].ap, in0=ps_r.ap[0:64, :], in1=cs.ap, op=ALU.mult),
                         reads=[ps_r.reg, cs.reg], writes=[rt1[ri].reg])
                    E.op("dve", lambda e, ps_w=ps_w, ri=ri: e.tensor_tensor(out=rt2[ri].ap, in0=ps_w.ap[0:64, :], in1=sn.ap, op=ALU.mult),
                         reads=[ps_w.reg, sn.reg], writes=[rt2[ri].reg])
                    s_ = stg16[nxt("s16", 3)]
                    E.op("pool", lambda e, s_=s_, ri=ri: e.tensor_tensor(out=s_.ap[0:64, :], in0=rt1[ri].ap, in1=rt2[ri].ap, op=ALU.add),
                         reads=[rt1[ri].reg, rt2[ri].reg], writes=[s_.reg])
                    store(QR.ap[2 * g + hh, :, tsl], QR.reg, s_, s_.ap[0:64, :])
                b, wap = getw(U["kn%d" % g], nk=4)
                for hh in range(2):
                    ps = proj_fm(wap, b.reg, hh * 128, 128, ckvn, ckvn.reg, nk=4)
                    s_ = stg16[nxt("s16", 3)]
                    E.op("act", lambda e, ps=ps, s_=s_: e.activation(out=s_.ap, in_=ps.ap, func=AF.Copy),
                         reads=[ps.reg], writes=[s_.reg])
                    store(KN.ap[2 * g + hh, :, tsl], KN.reg, s_, s_.ap)
                b, wap = getw(U["vv%d" % g], nk=4)
                for blk in range(4):
                    ps = psum[4 + nxt("ps", 4) % 2]
                    for k in range(4):
                        mm(ps.ap[:, 0:256], ckvn.ap[:, k, blk * 128:(blk + 1) * 128], wap[:, k, :], k == 0, k == 3,
                           [b.reg, ckvn.reg], [ps.reg])
                    s_ = stg16[nxt("s16", 3)]
                    E.op("act", lambda e, ps=ps, s_=s_: e.activation(out=s_.ap[:, 0:256], in_=ps.ap[:, 0:256], func=AF.Copy),
                         reads=[ps.reg], writes=[s_.reg])
                    r0 = ti * T + blk * 128
                    store(V_tm.ap[r0:r0 + 128, g * 256:(g + 1) * 256], V_tm.reg, s_, s_.ap[:, 0:256])
            pump(2)

    def linattn(l):
        E.barrier()
        pos = [persist_top]
        NC_ = S // 128
        q32 = carve(128, [S], F32, pos)
        k32 = carve(128, [S], F32, pos)
        la = carve(128, [S], F32, pos)
        ex = carve(128, [S], F32, pos)
        rmask = carve(128, [S], F32, pos)
        qd = carve(128, [S], BF16, pos)
        kd = carve(128, [S], BF16, pos)
        ks = carve(128, [S], BF16, pos)
        ks_tm = carve(128, [NC_, 128], BF16, pos)
        vaug = carve(128, [NC_, 256], BF16, pos)
        dec = carve(128, [NC_], F32, pos)
        S32 = carve(128, [256], F32, pos)
        Sbf = [carve(128, [256], BF16, pos) for _ in range(2)]
        am = [carve(128, [128], BF16, pos) for _ in range(2)]
        gt = carve(128, [T], BF16, pos)
        hh32 = carve(128, [T], F32, pos)
        t32 = carve(128, [T], F32, pos)
        mean = carve(128, [T], F32, pos)
        rstd = carve(128, [T], F32, pos)
        zb = carve(128, [T], BF16, pos)
        sqb_ = carve(128, [T], BF16, pos)
        yst = carve(128, [T], BF16, pos)
        mi = carve(128, [S], I32, pos)
        lnq = carve(128, [1], F32, pos)
        lnk = carve(128, [1], F32, pos)
        pt = Buf(psum[7].ap.bitcast(BF16), psum[7].reg)

        E.op("pool", lambda e: e.iota(mi.ap, pattern=[[0, NC_], [1, 128]], base=0, channel_multiplier=0), writes=[mi.reg])
        E.op("dve", lambda e: e.tensor_copy(out=rmask.ap, in_=mi.ap), reads=[mi.reg], writes=[rmask.reg])
        E.op("dve", lambda e: e.tensor_scalar_min(out=rmask.ap, in0=rmask.ap, scalar1=1.0), reads=[rmask.reg], writes=[rmask.reg])
        E.op("pool", lambda e: e.memset(vaug.ap[:, :, 128:256], 1.0), writes=[vaug.reg])

        for kind in ("ml", "gl"):
            dk = 128 if kind == "ml" else 64
            for hd in range(4):
                if kind == "ml":
                    E.dma(q32.ap, mlq32.ap[hd * 128:(hd + 1) * 128, :], reads=[mlq32.reg], writes=[q32.reg])
                    E.dma(k32.ap, mlk32.ap[hd * 128:(hd + 1) * 128, :], reads=[mlk32.reg], writes=[k32.reg])
                    E.dma(la.ap, gate_f.ap[hd:hd + 1, :].partition_broadcast(128), reads=[gate_f.reg], writes=[la.reg])
                    E.dma(ex.ap, gate_i.ap[hd:hd + 1, :].partition_broadcast(128), reads=[gate_i.reg], writes=[ex.reg])
                    vsrc = mlv_tm
                    sc_dec = -1.0
                    qscale, kscale = 1.0, 128.0 ** -0.5
                else:
                    E.dma(q32.ap[0:64, :], glq32.ap[hd], reads=[glq32.reg], writes=[q32.reg])
                    E.dma(k32.ap[0:64, :], glk32.ap[hd], reads=[glk32.reg], writes=[k32.reg])
                    E.dma(la.ap[0:64, :], gla_neg.ap[hd], reads=[gla_neg.reg], writes=[la.reg])
                    vsrc = glv_tm
                    sc_dec = -1.0 / 16.0
                    qscale, kscale = 64.0 ** -0.5, 1.0
                E.dma(vaug.ap[:, :, 0:128],
                      vsrc.ap[:, hd * 128:(hd + 1) * 128].rearrange("(c p) d -> p c d", p=128),
                      reads=[vsrc.reg], writes=[vaug.reg])
                P = slice(0, dk)
                E.op("pool", lambda e, qscale=qscale: e.memset(lnq.ap, math.log(qscale)), writes=[lnq.reg])
                E.op("pool", lambda e, kscale=kscale: e.memset(lnk.ap, math.log(kscale)), writes=[lnk.reg])
                E.op("dve", lambda e, P=P: e.tensor_tensor_scan(out=la.ap[P, :], data0=rmask.ap[P, :], data1=la.ap[P, :],
                                                                initial=0.0, op0=ALU.mult, op1=ALU.add),
                     reads=[la.reg, rmask.reg], writes=[la.reg])
                lav = la.ap.rearrange("p (c t) -> p c t", t=128)
                E.op("act", lambda e, P=P, sc_dec=sc_dec, lav=lav: e.activation(out=dec.ap[P, :], in_=lav[P, :, 127], func=AF.Exp, scale=sc_dec),
                     reads=[la.reg], writes=[dec.reg])
                if kind == "ml":
                    E.op("dve", lambda e: e.tensor_tensor(out=ex.ap, in0=ex.ap, in1=la.ap, op=ALU.add),
                         reads=[ex.reg, la.reg], writes=[ex.reg])
                    E.op("act", lambda e: e.activation(out=ex.ap, in_=ex.ap, func=AF.Exp, bias=lnk.ap[:, 0:1]),
                         reads=[ex.reg, lnk.reg], writes=[ex.reg])
                else:
                    E.op("act", lambda e, P=P: e.activation(out=ex.ap[P, :], in_=la.ap[P, :], func=AF.Exp, scale=1.0 / 16.0),
                         reads=[la.reg], writes=[ex.reg])
                E.op("dve", lambda e, P=P: e.tensor_tensor(out=k32.ap[P, :], in0=k32.ap[P, :], in1=ex.ap[P, :], op=ALU.mult),
                     reads=[k32.reg, ex.reg], writes=[k32.reg])
                E.op("act", lambda e, P=P: e.activation(out=kd.ap[P, :], in_=k32.ap[P, :], func=AF.Copy),
                     reads=[k32.reg], writes=[kd.reg])
                k3 = k32.ap.rearrange("p (c t) -> p c t", t=128)
                ks3 = ks.ap.rearrange("p (c t) -> p c t", t=128)
                E.op("dve", lambda e, P=P, dk=dk, k3=k3, ks3=ks3: e.tensor_tensor(out=ks3[P], in0=k3[P], in1=dec.ap[P, :].unsqueeze(2).to_broadcast([dk, NC_, 128]), op=ALU.mult),
                     reads=[k32.reg, dec.reg], writes=[ks.reg])
                E.op("act", lambda e, P=P, sc_dec=sc_dec: e.activation(out=ex.ap[P, :], in_=la.ap[P, :], func=AF.Exp, scale=sc_dec, bias=lnq.ap[P, 0:1]),
                     reads=[la.reg, lnq.reg, k32.reg], writes=[ex.reg])
                E.op("dve", lambda e, P=P: e.tensor_tensor(out=qd.ap[P, :], in0=q32.ap[P, :], in1=ex.ap[P, :], op=ALU.mult),
                     reads=[q32.reg, ex.reg], writes=[qd.reg])
                for c8 in range(NC_ // 8):
                    for i in range(8):
                        c = c8 * 8 + i
                        E.op("pe", lambda e, c=c, i=i, P=P, dk=dk, ks3=ks3: e.transpose(pt.ap[:, i * 128:i * 128 + dk], ks3[P, c, :], ident.ap[P, 0:dk]),
                             reads=[ks.reg, ident.reg], writes=[pt.reg])
                    ptv = pt.ap.rearrange("p (a b) -> p a b", a=8)
                    E.op("act", lambda e, c8=c8, ptv=ptv, dk=dk: e.activation(out=ks_tm.ap[:, c8 * 8:(c8 + 1) * 8, 0:dk], in_=ptv[:, :, 0:dk], func=AF.Copy),
                         reads=[pt.reg], writes=[ks_tm.reg])
                nw = 256 if kind == "ml" else 128
                pSb = [Buf(psum[6].ap[:, 0:256]), Buf(psum[6].ap[:, 256:512])]

                def stageA(c, P=P, dk=dk, kind=kind, nw=nw, pSb=pSb):
                    csl = slice(c * 128, (c + 1) * 128)
                    pa = psum[4 + c % 2]
                    mm(pa.ap[:, 0:128], kd.ap[P, csl], qd.ap[P, csl], True, True, [kd.reg, qd.reg], [pa.reg])
                    a_ = am[c % 2]
                    E.op("dve", lambda e, pa=pa, a_=a_: e.tensor_tensor(out=a_.ap, in0=pa.ap[:, 0:128], in1=tri.ap, op=ALU.mult),
                         reads=[pa.reg, tri.reg], writes=[a_.reg])
                    if c < NC_ - 1:
                        pS = pSb[c % 2]
                        mm(pS.ap[P, 0:nw], ks_tm.ap[:, c, 0:dk], vaug.ap[:, c, 0:nw], True, True,
                           [ks_tm.reg, vaug.reg], [pS.reg])

                def stageB(c, po, pd, P=P, dk=dk, kind=kind, nw=nw, pSb=pSb):
                    csl = slice(c * 128, (c + 1) * 128)
                    ci = c % 4
                    osl = slice(ci * 128, (ci + 1) * 128)
                    a_ = am[c % 2]
                    first = (c == 0)
                    Sp = Sbf[(c + 1) % 2]
                    mm(po.ap[:, osl], vaug.ap[:, c, 0:128], a_.ap, True, first, [vaug.reg, a_.reg], [po.reg])
                    if not first:
                        mm(po.ap[:, osl], Sp.ap[P, 0:128], qd.ap[P, csl], False, True, [Sp.reg, qd.reg], [po.reg])
                    if kind == "ml":
                        mm(pd.ap[:, osl], ones_bf.ap, a_.ap, True, first, [ones_bf.reg, a_.reg], [pd.reg])
                        if not first:
                            mm(pd.ap[:, osl], Sp.ap[:, 128:256], qd.ap[:, csl], False, True, [Sp.reg, qd.reg], [pd.reg])
                    if c < NC_ - 1:
                        pS = pSb[c % 2]
                        if first:
                            E.op("dve", lambda e, pS=pS: e.tensor_copy(out=S32.ap[P, 0:nw], in_=pS.ap[P, 0:nw]),
                                 reads=[pS.reg], writes=[S32.reg])
                        else:
                            E.op("dve", lambda e, pS=pS, c=c: e.scalar_tensor_tensor(
                                out=S32.ap[P, 0:nw], in0=S32.ap[P, 0:nw], scalar=dec.ap[P, c:c + 1], in1=pS.ap[P, 0:nw],
                                op0=ALU.mult, op1=ALU.add), reads=[S32.reg, dec.reg, pS.reg], writes=[S32.reg])
                        Sn = Sbf[c % 2]
                        E.op("act", lambda e, Sn=Sn: e.activation(out=Sn.ap[P, 0:nw], in_=S32.ap[P, 0:nw], func=AF.Copy),
                             reads=[S32.reg], writes=[Sn.reg])

                stageA(0)
                for tq in range(NT):
                    po = psum[tq % 2]
                    pd = psum[2 + tq % 2]
                    tsl = slice(tq * T, (tq + 1) * T)
                    if kind == "ml":
                        E.dma(gt.ap, mlo_s.ap[hd * 128:(hd + 1) * 128, tsl], reads=[mlo_s.reg], writes=[gt.reg])
                    else:
                        E.dma(gt.ap, glr_s.ap[hd * 128:(hd + 1) * 128, tsl], reads=[glr_s.reg], writes=[gt.reg])
                    for ci in range(4):
                        c = tq * 4 + ci
                        if c + 1 < NC_:
                            stageA(c + 1)
                        stageB(c, po, pd)
                    if kind == "ml":
                        E.op("act", lambda e, pd=pd: e.activation(out=t32.ap, in_=pd.ap, func=AF.Abs),
                             reads=[pd.reg], writes=[t32.reg])
                        E.op("dve", lambda e: e.tensor_scalar_max(out=t32.ap, in0=t32.ap, scalar1=1.0),
                             reads=[t32.reg], writes=[t32.reg])
                        E.op("dve", lambda e: e.reciprocal(out=t32.ap, in_=t32.ap), reads=[t32.reg], writes=[t32.reg])
                        E.op("dve", lambda e, po=po: e.tensor_tensor(out=hh32.ap, in0=po.ap, in1=t32.ap, op=ALU.mult),
                             reads=[po.reg, t32.reg], writes=[hh32.reg])
                        E.op("act", lambda e: e.activation(out=zb.ap, in_=hh32.ap, func=AF.Copy), reads=[hh32.reg], writes=[zb.reg])
                        E.op("act", lambda e: e.activation(out=sqb_.ap, in_=hh32.ap, func=AF.Square), reads=[hh32.reg], writes=[sqb_.reg])
                        pm = psum[4]
                        pq = psum[5]
                        mm(pm.ap, ones128.ap, zb.ap, True, True, [ones128.reg, zb.reg], [pm.reg])
                        mm(pq.ap, ones128.ap, sqb_.ap, True, True, [ones128.reg, sqb_.reg], [pq.reg])
                        E.op("act", lambda e, pm=pm: e.activation(out=mean.ap, in_=pm.ap, func=AF.Copy), reads=[pm.reg], writes=[mean.reg])
                        E.op("dve", lambda e: e.tensor_tensor(out=rstd.ap, in0=mean.ap, in1=mean.ap, op=ALU.mult),
                             reads=[mean.reg], writes=[rstd.reg])
                        E.op("dve", lambda e, pq=pq: e.tensor_tensor(out=rstd.ap, in0=pq.ap, in1=rstd.ap, op=ALU.subtract),
                             reads=[pq.reg, rstd.reg], writes=[rstd.reg])
                        rsqrt_inplace(rstd, EPS)
                        E.op("pool", lambda e: e.tensor_tensor(out=hh32.ap, in0=hh32.ap, in1=mean.ap, op=ALU.subtract),
                             reads=[hh32.reg, mean.reg], writes=[hh32.reg])
                        E.op("dve", lambda e: e.tensor_tensor(out=hh32.ap, in0=hh32.ap, in1=rstd.ap, op=ALU.mult),
                             reads=[hh32.reg, rstd.reg], writes=[hh32.reg])
                        E.op("dve", lambda e: e.tensor_tensor(out=yst.ap, in0=hh32.ap, in1=gt.ap, op=ALU.mult),
                             reads=[hh32.reg, gt.reg], writes=[yst.reg])
                        row = hd * 128
                    else:
                        E.op("act", lambda e, po=po: e.activation(out=sqb_.ap, in_=po.ap, func=AF.Square), reads=[po.reg], writes=[sqb_.reg])
                        pq = psum[5]
                        mm(pq.ap, ones128.ap, sqb_.ap, True, True, [ones128.reg, sqb_.reg], [pq.reg])
                        rsqrt_to(rstd, pq.ap, [pq.reg], EPS)
                        E.op("dve", lambda e, po=po: e.tensor_tensor(out=hh32.ap, in0=po.ap, in1=rstd.ap, op=ALU.mult),
                             reads=[po.reg, rstd.reg], writes=[hh32.reg])
                        E.op("dve", lambda e: e.tensor_tensor(out=yst.ap, in0=hh32.ap, in1=gt.ap, op=ALU.mult),
                             reads=[hh32.reg, gt.reg], writes=[yst.reg])
                        row = 512 + hd * 128
                    E.dma(ymix.ap[row:row + 128, tsl], yst.ap, reads=[yst.reg], writes=[ymix.reg], eng="act")
                pump(4)

    def mla_attn(l):
        E.barrier()
        pos = [persist_top]
        NB = S // 128
        scale = 192.0 ** -0.5
        qn = carve(128, [S], BF16, pos)
        qr = carve(65, [S], BF16, pos)
        kn = carve(128, [S], BF16, pos)
        kr = carve(65, [S], BF16, pos)
        vt = carve(128, [NB, 128], BF16, pos)
        sq = [carve(128, [T], BF16, pos) for _ in range(2)]
        qn2 = carve(128, [S], F32, pos)
        kmx = carve(128, [NT], F32, pos)
        kmax = carve(128, [1], F32, pos)
        pb = [carve(128, [T], BF16, pos) for _ in range(4)]
        rl = carve(128, [T], F32, pos)
        yst = [carve(128, [T], BF16, pos) for _ in range(2)]
        E.dma(kr.ap[0:64, :], KR.ap, reads=[KR.reg], writes=[kr.reg])
        E.op("pool", lambda e: e.memset(kr.ap[64:65, :], 1.0), reads=[], writes=[kr.reg])
        for hd in range(8):
            E.dma(qn.ap, QN.ap[hd], reads=[QN.reg], writes=[qn.reg])
            E.dma(qr.ap[0:64, :], QR.ap[hd], reads=[QR.reg], writes=[qr.reg])
            E.dma(kn.ap, KN.ap[hd], reads=[KN.reg], writes=[kn.reg])
            E.dma(vt.ap, V_tm.ap[:, hd * 128:(hd + 1) * 128].rearrange("(c p) d -> p c d", p=128),
                  reads=[V_tm.reg], writes=[vt.reg])
            for ti in range(NT):
                tsl = slice(ti * T, (ti + 1) * T)
                pk = psum[6]
                pq = psum[7]
                a, b2 = sq[0], sq[1]
                E.op("act", lambda e, a=a, tsl=tsl: e.activation(out=a.ap, in_=kn.ap[:, tsl], func=AF.Square), reads=[kn.reg], writes=[a.reg])
                mm(pk.ap, ones_bf.ap, a.ap, True, False, [ones_bf.reg, a.reg], [pk.reg])
                E.op("act", lambda e, b2=b2, tsl=tsl: e.activation(out=b2.ap[0:64, :], in_=kr.ap[0:64, tsl], func=AF.Square), reads=[kr.reg], writes=[b2.reg])
                mm(pk.ap, ones_bf.ap[0:64, :], b2.ap[0:64, :], False, True, [ones_bf.reg, b2.reg], [pk.reg])
                E.op("dve", lambda e, pk=pk, ti=ti: e.reduce_max(out=kmx.ap[:, ti:ti + 1], in_=pk.ap, axis=AX.X), reads=[pk.reg], writes=[kmx.reg])
                E.op("act", lambda e, a=a, tsl=tsl: e.activation(out=a.ap, in_=qn.ap[:, tsl], func=AF.Square), reads=[qn.reg], writes=[a.reg])
                mm(pq.ap, ones_bf.ap, a.ap, True, False, [ones_bf.reg, a.reg], [pq.reg])
                E.op("act", lambda e, b2=b2, tsl=tsl: e.activation(out=b2.ap[0:64, :], in_=qr.ap[0:64, tsl], func=AF.Square), reads=[qr.reg], writes=[b2.reg])
                mm(pq.ap, ones_bf.ap[0:64, :], b2.ap[0:64, :], False, True, [ones_bf.reg, b2.reg], [pq.reg])
                E.op("act", lambda e, pq=pq, tsl=tsl: e.activation(out=qn2.ap[:, tsl], in_=pq.ap, func=AF.Copy), reads=[pq.reg], writes=[qn2.reg])
            E.op("dve", lambda e: e.reduce_max(out=kmax.ap, in_=kmx.ap, axis=AX.X), reads=[kmx.reg], writes=[kmax.reg])
            E.op("dve", lambda e: e.tensor_scalar_mul(out=qn2.ap[64:65, :], in0=qn2.ap[64:65, :], scalar1=kmax.ap[64:65, 0:1]),
                 reads=[qn2.reg, kmax.reg], writes=[qn2.reg])
            E.op("act", lambda e: e.activation(out=qn2.ap[64:65, :], in_=qn2.ap[64:65, :], func=AF.Sqrt), reads=[qn2.reg], writes=[qn2.reg])
            E.op("dve", lambda e: e.tensor_scalar_mul(out=qr.ap[64:65, :], in0=qn2.ap[64:65, :], scalar1=-1.0),
                 reads=[qn2.reg], writes=[qr.reg])
            work = []
            for qt in range(NT):
                nkb = 4 * (qt + 1)
                for kb in range(nkb):
                    work.append((qt, kb, nkb))
            LA = 2

            def stage_qk(i):
                qt, kb, nkb = work[i]
                d_ = kb - 4 * qt
                off = d_ * 128 if d_ > 0 else 0
                qsl = slice(qt * T + off, (qt + 1) * T)
                n = T - off
                ksl = slice(kb * 128, (kb + 1) * 128)
                ps_ = psum[4 + i % 4]
                mm(ps_.ap[:, 0:n], kn.ap[:, ksl], qn.ap[:, qsl], True, False, [kn.reg, qn.reg], [ps_.reg])
                mm(ps_.ap[:, 0:n], kr.ap[:, ksl], qr.ap[:, qsl], False, True, [kr.reg, qr.reg], [ps_.reg])
                p_ = pb[i % 4]
                E.op("act", lambda e, ps_=ps_, p_=p_, n=n: e.activation(out=p_.ap[:, 0:n], in_=ps_.ap[:, 0:n], func=AF.Exp, scale=scale),
                     reads=[ps_.reg], writes=[p_.reg])
                if d_ >= 0:
                    E.op("pool", lambda e, p_=p_: e.tensor_tensor(out=p_.ap[:, 0:128], in0=p_.ap[:, 0:128], in1=tri.ap, op=ALU.mult),
                         reads=[p_.reg, tri.reg], writes=[p_.reg])

            def stage_pv(i, hd=hd):
                qt, kb, nkb = work[i]
                d_ = kb - 4 * qt
                off = d_ * 128 if d_ > 0 else 0
                n = T - off
                po = psum[qt % 2]
                pl = psum[2 + qt % 2]
                p_ = pb[i % 4]
                mm(po.ap[:, off:T], vt.ap[:, kb, :], p_.ap[:, 0:n], kb == 0, kb == nkb - 1, [vt.reg, p_.reg], [po.reg])
                mm(pl.ap[:, off:T], ones_bf.ap, p_.ap[:, 0:n], kb == 0, kb == nkb - 1, [ones_bf.reg, p_.reg], [pl.reg])
                if kb == nkb - 1:
                    E.op("dve", lambda e, pl=pl: e.reciprocal(out=rl.ap, in_=pl.ap), reads=[pl.reg], writes=[rl.reg])
                    y_ = yst[qt % 2]
                    E.op("dve", lambda e, po=po, y_=y_: e.tensor_tensor(out=y_.ap, in0=po.ap, in1=rl.ap, op=ALU.mult),
                         reads=[po.reg, rl.reg], writes=[y_.reg])
                    row = 1024 + hd * 128
                    E.dma(ymix.ap[row:row + 128, qt * T:(qt + 1) * T], y_.ap, reads=[y_.reg], writes=[ymix.reg], eng="act")

            for i in range(min(LA, len(work))):
                stage_qk(i)
            for i in range(len(work)):
                if i + LA < len(work):
                    stage_qk(i + LA)
                stage_pv(i)
            pump(4)

    def outproj(l, U, dst_ap, dst_regs):
        E.barrier()
        pos = [persist_top]
        x32 = carve(128, [NK, T], F32, pos)
        yb = [carve(128, [NK, T], BF16, pos) for _ in range(2)]
        wob = [carve(128, [NK, 128], BF16, pos) for _ in range(4)]
        tmp = ln_tmps(pos)
        wos = Stream(list(U["wo"]) * NT, wob)
        wos.prime()
        yv = ymix.ap.rearrange("(k p) t -> p k t", p=128)
        E.dma(yb[0].ap, yv[:, :, 0:T], reads=[ymix.reg], writes=[yb[0].reg])
        for ti in range(NT):
            load_x_tile(xs.ap, xs_tiles[ti], ti, x32)
            if ti + 1 < NT:
                E.dma(yb[(ti + 1) % 2].ap, yv[:, :, (ti + 1) * T:(ti + 2) * T], reads=[ymix.reg], writes=[yb[(ti + 1) % 2].reg])
            proj_ln(1, yb[ti % 2], NK, wos, x32, tmp, dst_ap, dst_regs[ti], ti)

    for l in range(n_layers):
        Ud = {}
        Ud["f1i"] = wi_units(ffn_wi[0], l, "f1")
        Ud["f1o"] = wo_units(ffn_wo[0], l, "f1")
        Ud["mix"] = mixer_units(l)
        Ud["f2i"] = wi_units(ffn_wi[1], l, "f2")
        Ud["f2o"] = wo_units(ffn_wo[1], l, "f2")
        units.append(Ud)

    rope_done = [False]
    for l in range(n_layers):
        Ud = units[l]
        E.barrier()
        adaln(l, persist_top)
        src_ap, src_regs = (x_in, [Reg() for _ in range(NT)]) if l == 0 else (xs.ap, xs_tiles)
        last = (stop_after == (l, 0))
        ffn(l, 0, Ud["f1i"], Ud["f1o"], src_ap, src_regs,
            y_out if last else xs.ap, out_tiles if last else xs_tiles)
        if last:
            break
        if not rope_done[0]:
            rope_tables()
            rope_done[0] = True
        inproj(l, Ud["mix"])
        linattn(l)
        mla_attn(l)
        last = (stop_after == (l, 1))
        outproj(l, Ud["mix"], y_out if last else xs.ap, out_tiles if last else xs_tiles)
        if last:
            break
        last = (l == n_layers - 1) or (stop_after == (l, 2))
        ffn(l, 2, Ud["f2i"], Ud["f2o"], xs.ap, xs_tiles,
            y_out if last else xs.ap, out_tiles if last else xs_tiles)
        if last:
            break

    E.finish()
    st.close()
    return nc, E


def host_layout(inputs, b, nl=DEPTH):
    f = np.ascontiguousarray
    half = 32
    invf = (10000.0 ** (-np.arange(half, dtype=np.float32) / half)).astype(np.float32)
    m = {
        "x": f(inputs["x"][b].T),
        "c": f(inputs["c"][b].reshape(NK, 128).T),
        "pos": f(inputs["positions"][b].reshape(1, S).astype(np.int32)),
        "invf": f(np.concatenate([invf, invf]).reshape(64, 1)),
        "w_ada": inputs["w_ada"][:nl],
        "b_ada": f(inputs["b_ada"][:nl].reshape(nl, 144, 128).transpose(0, 2, 1)),
        "ln_g": f(inputs["ln_g"][:nl].reshape(nl, 3, NK, 128).transpose(0, 3, 1, 2)),
        "ln_b": f(inputs["ln_b"][:nl].reshape(nl, 3, NK, 128).transpose(0, 3, 1, 2)),
        "ffn1_wi": inputs["ffn1_wi"][:nl], "ffn1_wo": inputs["ffn1_wo"][:nl],
        "ffn2_wi": inputs["ffn2_wi"][:nl], "ffn2_wo": inputs["ffn2_wo"][:nl],
        "w_in": inputs["w_in"][:nl],
        "ml_conv": f(inputs["ml_conv"][:nl].reshape(nl, 4, 8, 128).transpose(0, 3, 2, 1)),
        "ml_bi": f(inputs["ml_bi"][:nl].reshape(nl, 4, 1)),
        "ml_bf": f(inputs["ml_bf"][:nl].reshape(nl, 4, 1)),
        "gla_wg": inputs["gla_wg"][:nl],
        "gla_bg": f(inputs["gla_bg"][:nl].reshape(nl, 4, 64).transpose(0, 2, 1)),
        "mla_gq": f(inputs["mla_gq"][:nl].reshape(nl, 4, 128).transpose(0, 2, 1)),
        "mla_gkv": f(inputs["mla_gkv"][:nl].reshape(nl, 4, 128).transpose(0, 2, 1)),
        "mla_wuq": inputs["mla_wuq"][:nl], "mla_wuk": inputs["mla_wuk"][:nl], "mla_wuv": inputs["mla_wuv"][:nl],
        "w_out": inputs["w_out"][:nl],
    }
    return m


_USED = None


def run(inputs, cores, trace=False, **bk):
    nc, E = build(**bk)
    nl = bk.get('n_layers', DEPTH)
    in_maps = [host_layout(inputs, b, nl) for b in cores]
    if trace:
        res = run_bass_kernel_spmd(nc, in_maps, core_ids=list(range(len(cores))), trace=True)
        print("EXEC_TIME_NS", res.exec_time_ns)
    else:
        res = run_bass_kernel_spmd(nc, in_maps, core_ids=list(range(len(cores))))
    return [np.ascontiguousarray(r["y"].T) for r in res.results]


def kernel(**inputs):
    inputs = {k: np.asarray(v) for k, v in inputs.items()}
    outs = run(inputs, list(range(8)))
    return np.stack(outs, 0).astype(np.float32)
```

```python
import contextlib
import math
import numpy as np
import concourse.bass as bass
import concourse.mybir as mybir
from concourse.bass_utils import run_bass_kernel_spmd

F32 = mybir.dt.float32
BF16 = mybir.dt.bfloat16
I32 = mybir.dt.int32
U8 = mybir.dt.uint8
AF = mybir.ActivationFunctionType
ALU = mybir.AluOpType
AX = mybir.AxisListType

D = 2048
S = 4096
DEPTH = 4
FF = 5632
NK = 16
NJ = 44
T = 512
NT = S // T
ALPHA = (2.0 * DEPTH) ** 0.25
EPS = 1e-5
D_IN = 4696
C_MLQ, C_MLK, C_MLV, C_MLI, C_MLF, C_MLO = 0, 512, 1024, 1536, 1540, 1544
C_GLQ, C_GLK, C_GLV, C_GLLR, C_GLR = 2056, 2312, 2568, 3080, 3096
C_CQ, C_CKV, C_KR = 3608, 4120, 4632


class Reg:
    __slots__ = ("w", "r")

    def __init__(self):
        self.w = None
        self.r = []


class Emitter:
    ENGS = ("pe", "act", "dve", "pool", "sp")

    def __init__(self, nc, n_dma_sems=None):
        self.nc = nc
        self.thunks = {e: [] for e in self.ENGS}
        self.count = {e: 0 for e in self.ENGS}
        self.waited = {e: {} for e in self.ENGS}
        self.sem = {}
        self.dma_sems = []
        self.dma_sem_cnt = []
        self.pool = {"sp": list(range(0, 36)), "act": list(range(36, 64))}
        self.n_dma_sems = 64
        self.dma_rr = {"sp": 0, "act": 0}
        self.ninstr = 0

    def setup_sems(self, stack):
        for e in ("pe", "act", "dve", "pool"):
            self.sem[e] = stack.enter_context(self.nc.semaphore("c_" + e))
        for i in range(self.n_dma_sems):
            self.dma_sems.append(stack.enter_context(self.nc.semaphore("d%d" % i)))
            self.dma_sem_cnt.append(0)

    def _wait(self, eng, tok, force=False):
        kind, key, val = tok
        if kind == "eng" and key == eng and not force:
            return
        k = (kind, key)
        if self.waited[eng].get(k, 0) >= val:
            return
        self.waited[eng][k] = val
        sem = self.sem[key] if kind == "eng" else self.dma_sems[key]
        self.thunks[eng].append(lambda e, sem=sem, val=val: e.wait_ge(sem, val))

    def _deps(self, eng, reads, writes, force=False):
        for r in reads:
            if r.w is not None:
                self._wait(eng, r.w, force)
        for w in writes:
            if w.w is not None:
                self._wait(eng, w.w, force)
            for t in w.r:
                self._wait(eng, t, force)

    def _mark(self, tok, reads, writes):
        for r in reads:
            r.r.append(tok)
            if len(r.r) > 64:
                last = {}
                for t in r.r:
                    k = (t[0], t[1])
                    if k not in last or last[k][2] < t[2]:
                        last[k] = t
                r.r = list(last.values())
        for w in writes:
            w.w = tok
            w.r = []
        self.ninstr += 1

    def op(self, eng, fn, reads=(), writes=()):
        self._deps(eng, reads, writes)
        self.count[eng] += 1
        sem = self.sem[eng]
        self.thunks[eng].append(lambda e, fn=fn, sem=sem: fn(e).then_inc(sem, 1))
        tok = ("eng", eng, self.count[eng])
        self._mark(tok, reads, writes)
        return tok

    def dma(self, out, in_, reads=(), writes=(), eng="sp", **kw):
        pl = self.pool[eng]
        i = pl[self.dma_rr[eng] % len(pl)]
        self.dma_rr[eng] += 1
        if self.dma_sem_cnt[i] > 0:
            self._wait(eng, ("dma", i, self.dma_sem_cnt[i]))
        self._deps(eng, reads, writes, force=True)
        self.dma_sem_cnt[i] += 16
        sem = self.dma_sems[i]
        self.thunks[eng].append(
            lambda e, out=out, in_=in_, sem=sem, kw=kw: e.dma_start(out=out, in_=in_, **kw).then_inc(sem, 16))
        tok = ("dma", i, self.dma_sem_cnt[i])
        self._mark(tok, reads, writes)
        return tok

    def barrier(self):
        toks = [("eng", e, self.count[e]) for e in ("pe", "act", "dve", "pool") if self.count[e] > 0]
        toks += [("dma", i, c) for i, c in enumerate(self.dma_sem_cnt) if c > 0]
        for e in self.ENGS:
            for t in toks:
                self._wait(e, t)

    def finish(self):
        self.barrier()
        nc = self.nc
        th = self.thunks
        with nc.Block() as block:
            @block.sync
            def _(e):
                for t in th["sp"]:
                    t(e)

            @block.tensor
            def _(e):
                for t in th["pe"]:
                    t(e)

            @block.scalar
            def _(e):
                for t in th["act"]:
                    t(e)

            @block.vector
            def _(e):
                for t in th["dve"]:
                    t(e)

            @block.gpsimd
            def _(e):
                for t in th["pool"]:
                    t(e)


class Buf:
    __slots__ = ("ap", "reg")

    def __init__(self, ap, reg=None):
        self.ap = ap
        self.reg = reg if reg is not None else Reg()


def build(n_layers=DEPTH, stop_after=None, dbg=False):
    nc = bass.Bass("TRN2", target_bir_lowering=False)
    st = contextlib.ExitStack()
    E = Emitter(nc)
    E.setup_sems(st)

    def din(name, shape, dt=F32):
        return nc.dram_tensor(name, list(shape), dt, kind="ExternalInput").ap()

    def dscr(name, shape, dt):
        return nc.dram_tensor(name, list(shape), dt, kind="Internal").ap()

    x_in = din("x", [D, S])
    c_in = din("c", [128, NK])
    pos_in = din("pos", [1, S], I32)
    invf_in = din("invf", [64, 1])
    w_ada = din("w_ada", [n_layers, D, 9 * D])
    b_ada = din("b_ada", [n_layers, 128, 144])
    ln_g = din("ln_g", [n_layers, 128, 3, NK])
    ln_b = din("ln_b", [n_layers, 128, 3, NK])
    ffn_wi = [din("ffn1_wi", [n_layers, D, 2 * FF]), din("ffn2_wi", [n_layers, D, 2 * FF])]
    ffn_wo = [din("ffn1_wo", [n_layers, FF, D]), din("ffn2_wo", [n_layers, FF, D])]
    w_in = din("w_in", [n_layers, D, D_IN])
    ml_conv = din("ml_conv", [n_layers, 128, 8, 4])
    ml_bi = din("ml_bi", [n_layers, 4, 1])
    ml_bf = din("ml_bf", [n_layers, 4, 1])
    gla_wg = din("gla_wg", [n_layers, 16, 256])
    gla_bg = din("gla_bg", [n_layers, 64, 4])
    mla_gq = din("mla_gq", [n_layers, 128, 4])
    mla_gkv = din("mla_gkv", [n_layers, 128, 4])
    mla_wuq = din("mla_wuq", [n_layers, 512, 1536])
    mla_wuk = din("mla_wuk", [n_layers, 512, 1024])
    mla_wuv = din("mla_wuv", [n_layers, 512, 1024])
    w_out = din("w_out", [n_layers, D, D])
    y_out = nc.dram_tensor("y", [D, S], F32, kind="ExternalOutput").ap()

    xs = Buf(dscr("xs", [D, S], F32))
    xs_tiles = [Reg() for _ in range(NT)]
    out_tiles = [Reg() for _ in range(NT)]

    ARENA = 210000
    arena = nc.alloc_sbuf_tensor("arena", [128, ARENA], U8)
    top = [0]

    def carve(nparts, free_shape, dt, pos=None):
        n = int(np.prod(free_shape))
        nbytes = n * (4 if dt in (F32, I32) else 2)
        if pos is None:
            off = top[0]
            top[0] = off + (nbytes + 63) // 64 * 64
            assert top[0] <= ARENA, ("sbuf overflow", top[0])
        else:
            off = pos[0]
            pos[0] = off + (nbytes + 63) // 64 * 64
            assert pos[0] <= ARENA, ("sbuf overflow (phase)", pos[0])
        ap = arena[0:nparts, off:off + nbytes].bitcast(dt)
        if len(free_shape) == 2:
            ap = ap.rearrange("p (a b) -> p a b", a=free_shape[0])
        elif len(free_shape) == 3:
            ap = ap.rearrange("p (a b c) -> p a b c", a=free_shape[0], b=free_shape[1])
        return Buf(ap)

    psum = [Buf(st.enter_context(nc.psum_tensor("ps%d" % i, [128, 512], F32))[:]) for i in range(8)]

    ones_bf = carve(128, [128], BF16)
    onesD = carve(128, [128], BF16)
    ones512 = carve(128, [128], BF16)
    ones128 = carve(128, [128], BF16)
    tri = carve(128, [128], BF16)
    ident = carve(128, [128], BF16)
    cact = carve(128, [NK], F32)
    modsb = carve(128, [144], F32)
    sc1 = carve(128, [3, NK], F32)
    gz = carve(128, [3, NK], F32)
    lng = carve(128, [3, NK], F32)
    lnb = carve(128, [3, NK], F32)
    tmp144 = carve(128, [144], F32)
    iot_i = carve(128, [128], I32)
    iot_f = carve(128, [128], F32)
    negpi = carve(128, [1], F32)
    CV_F = 1024
    cv_ld = [carve(128, [CV_F], F32) for _ in range(3)]
    cv_st = [carve(128, [CV_F], BF16) for _ in range(3)]
    persist_top = top[0]

    def mm(out, lhsT, rhs, start, stop, reads, writes):
        E.op("pe", lambda e: e.matmul(out, lhsT=lhsT, rhs=rhs, start=start, stop=stop),
             reads=reads, writes=writes)

    def rsqrt_inplace(b, eps, ap=None):
        a = b.ap if ap is None else ap
        E.op("act", lambda e: e.activation(out=a, in_=a, func=AF.Sqrt, bias=float(eps)),
             reads=[b.reg], writes=[b.reg])
        E.op("dve", lambda e: e.reciprocal(out=a, in_=a), reads=[b.reg], writes=[b.reg])

    E.op("pool", lambda e: e.memset(ones_bf.ap, 1.0), writes=[ones_bf.reg])
    E.op("pool", lambda e: e.memset(onesD.ap, 1.0 / D), writes=[onesD.reg])
    E.op("pool", lambda e: e.memset(ones512.ap, 1.0 / 512), writes=[ones512.reg])
    E.op("pool", lambda e: e.memset(ones128.ap, 1.0 / 128), writes=[ones128.reg])
    E.op("pool", lambda e: e.memset(negpi.ap, -math.pi), writes=[negpi.reg])
    E.op("pool", lambda e: e.iota(iot_i.ap, pattern=[[1, 128]], base=0, channel_multiplier=-1),
         writes=[iot_i.reg])
    E.op("dve", lambda e: e.tensor_copy(out=iot_f.ap, in_=iot_i.ap), reads=[iot_i.reg], writes=[iot_f.reg])
    E.op("dve", lambda e: e.tensor_single_scalar(out=tri.ap, in_=iot_f.ap, scalar=0.0, op=ALU.is_ge),
         reads=[iot_f.reg], writes=[tri.reg])
    E.op("dve", lambda e: e.tensor_single_scalar(out=ident.ap, in_=iot_f.ap, scalar=0.0, op=ALU.is_equal),
         reads=[iot_f.reg], writes=[ident.reg])
    E.dma(cact.ap, c_in, writes=[cact.reg])
    E.op("act", lambda e: e.activation(out=cact.ap, in_=cact.ap, func=AF.Silu), reads=[cact.reg], writes=[cact.reg])

    class Unit:
        __slots__ = ("dst", "regs", "jobs", "nemit", "shape")

    cv_jobs = []
    cv_state = {"i": 0, "n": 0}
    cv_pending = []

    def new_unit(name, free_shape, srcs):
        u = Unit()
        n = int(np.prod(free_shape))
        u.dst = dscr(name, [128, n], BF16)
        u.shape = list(free_shape)
        u.nemit = 0
        u.jobs = []
        u.regs = []
        dst3 = u.dst.rearrange("p (a b) -> p a b", a=free_shape[0])
        for src, sel in srcs:
            u.jobs.append((src, sel(dst3)))
            u.regs.append(Reg())
            cv_jobs.append((u, len(u.jobs) - 1))
        return u

    def flush_cv(keep=0):
        while len(cv_pending) > keep:
            dst, sb, sreg, jreg = cv_pending.pop(0)
            E.dma(dst, sb, reads=[sreg], writes=[jreg], eng="act")

    def emit_job(u, ji):
        assert ji == u.nemit
        u.nemit += 1
        src, dst = u.jobs[ji]
        i = cv_state["n"] % len(cv_ld)
        cv_state["n"] += 1
        shp = src.shape
        n = int(np.prod(shp[1:]))
        assert n <= CV_F, n
        ld = cv_ld[i].ap[:, 0:n].rearrange("p (a b) -> p a b", a=shp[1])
        sb = cv_st[i].ap[:, 0:n].rearrange("p (a b) -> p a b", a=shp[1])
        E.dma(ld, src, writes=[cv_ld[i].reg], eng="act")
        flush_cv(keep=1)
        E.op("pool", lambda e, ld=ld, sb=sb: e.tensor_copy(out=sb, in_=ld),
             reads=[cv_ld[i].reg], writes=[cv_st[i].reg])
        cv_pending.append((dst, sb, cv_st[i].reg, u.regs[ji]))

    def pump(n=1):
        while n > 0 and cv_state["i"] < len(cv_jobs):
            u, ji = cv_jobs[cv_state["i"]]
            cv_state["i"] += 1
            if ji >= u.nemit:
                emit_job(u, ji)
                n -= 1

    def load_unit(u, buf_ap, buf_reg):
        if u.nemit < len(u.jobs):
            while u.nemit < len(u.jobs):
                emit_job(u, u.nemit)
        if any(p[3] in u.regs for p in cv_pending):
            flush_cv()
        src = u.dst.rearrange("p (a b) -> p a b", a=u.shape[0])
        E.dma(buf_ap, src, reads=u.regs, writes=[buf_reg])

    class Stream:
        def __init__(self, seq, bufs, depth=None):
            self.seq = seq
            self.bufs = bufs
            self.depth = len(bufs) - 1 if depth is None else depth
            self.il = 0
            self.iu = 0

        def _view(self, k):
            u = self.seq[k]
            b = self.bufs[k % len(self.bufs)]
            n = int(np.prod(u.shape))
            flat = b.ap
            if len(flat.shape) == 3:
                flat = flat.rearrange("p a b -> p (a b)")
            return b, flat[:, 0:n].rearrange("p (a b) -> p a b", a=u.shape[0])

        def _fill(self):
            while self.il < len(self.seq) and self.il <= self.iu + self.depth:
                b, v = self._view(self.il)
                load_unit(self.seq[self.il], v, b.reg)
                self.il += 1

        def prime(self):
            self._fill()

        def get(self):
            self._fill()
            b, v = self._view(self.iu)
            self.iu += 1
            return b, v

    def wi_units(w, l, tag):
        wv = w[l].rearrange("(k p) n -> p k n", p=128)
        us = []
        for j in range(NJ):
            srcs = []
            for k0 in (0, 8):
                srcs.append((wv[:, k0:k0 + 8, j * 128:(j + 1) * 128], lambda d, k0=k0: d[:, k0:k0 + 8, 0:128]))
                srcs.append((wv[:, k0:k0 + 8, FF + j * 128:FF + (j + 1) * 128], lambda d, k0=k0: d[:, k0:k0 + 8, 128:256]))
            us.append(new_unit("%s_wi%d_%d" % (tag, l, j), [NK, 256], srcs))
        return us

    def wo_units(w, l, tag):
        wv = w[l].rearrange("(k p) n -> p k n", p=128)
        us = []
        for m in range(NK):
            srcs = []
            for k0, k1 in ((0, 8), (8, 16), (16, 24), (24, 32), (32, 40), (40, 44)):
                srcs.append((wv[:, k0:k1, m * 128:(m + 1) * 128], lambda d, k0=k0, k1=k1: d[:, k0:k1, :]))
            us.append(new_unit("%s_wo%d_%d" % (tag, l, m), [NJ, 128], srcs))
        return us

    def colunit(w2d, name, c0, width, nk=NK):
        wv = w2d.rearrange("(k p) n -> p k n", p=128)
        kp = max(1, CV_F // width)
        srcs = []
        for k0 in range(0, nk, kp):
            k1 = min(nk, k0 + kp)
            srcs.append((wv[:, k0:k1, c0:c0 + width], lambda d, k0=k0, k1=k1: d[:, k0:k1, :]))
        return new_unit(name, [nk, width], srcs)

    units = []
    _mix_decl = []

    def adaln(l, pos0):
        pos = [pos0]
        NB_ = 4
        wbuf = [carve(128, [NK, 128], F32, pos) for _ in range(NB_)]
        bsb = carve(128, [144], F32, pos)
        E.dma(bsb.ap, b_ada[l], writes=[bsb.reg])
        E.dma(lng.ap, ln_g[l], writes=[lng.reg])
        E.dma(lnb.ap, ln_b[l], writes=[lnb.reg])
        wv = w_ada[l].rearrange("(k p) n -> p k n", p=128)
        ps = psum[0]

        def ld(j):
            wb = wbuf[j % NB_]
            E.dma(wb.ap, wv[:, :, j * 128:(j + 1) * 128], writes=[wb.reg])
        for j in range(NB_ - 1):
            ld(j)
        for j in range(144):
            if j + NB_ - 1 < 144:
                ld(j + NB_ - 1)
            wb = wbuf[j % NB_]
            for k in range(NK):
                mm(ps.ap[:, j:j + 1], wb.ap[:, k, :], cact.ap[:, k:k + 1], k == 0, k == NK - 1,
                   [wb.reg, cact.reg], [ps.reg])
        E.op("dve", lambda e: e.tensor_tensor(out=modsb.ap, in0=ps.ap[:, 0:144], in1=bsb.ap, op=ALU.add),
             reads=[ps.reg, bsb.reg], writes=[modsb.reg])
        m4 = modsb.ap.rearrange("p (s w k) -> p s w k", s=3, w=3)
        E.op("dve", lambda e: e.tensor_scalar_add(out=sc1.ap, in0=m4[:, :, 1, :], scalar1=1.0),
             reads=[modsb.reg], writes=[sc1.reg])
        for s_ in range(3):
            rw = (0.5 if s_ != 1 else 1.0) / ALPHA
            E.op("dve", lambda e, s_=s_, rw=rw: e.tensor_scalar(
                out=gz.ap[:, s_, :], in0=m4[:, s_, 2, :], scalar1=1.0, scalar2=rw, op0=ALU.add, op1=ALU.mult),
                reads=[modsb.reg], writes=[gz.reg])

    def shift_ap(s_):
        return modsb.ap.rearrange("p (s w k) -> p s w k", s=3, w=3)[:, s_, 0, :]

    def load_x_tile(src_ap, src_reg, ti, x32):
        v = src_ap.rearrange("(k p) t -> p k t", p=128)[:, :, ti * T:(ti + 1) * T]
        E.dma(x32.ap, v, reads=[src_reg], writes=[x32.reg])

    def modulate(s_, x32, u):
        sh = shift_ap(s_)
        for k in range(NK):
            E.op("dve", lambda e, k=k: e.tensor_scalar(
                out=u.ap[:, k, :], in0=x32.ap[:, k, :], scalar1=sc1.ap[:, s_, k:k + 1], scalar2=sh[:, k:k + 1],
                op0=ALU.mult, op1=ALU.add), reads=[x32.reg, sc1.reg, modsb.reg], writes=[u.reg])

    def proj_ln(s_, hbuf, nkk, wstream, x32, tmp, dst_ap, dst_reg, ti, xres=None):
        ps_y = [psum[4], psum[5]]
        ps_mu, ps_sq = psum[6], psum[7]
        zb, sqb = tmp["zb"], tmp["sqb"]

        def stats(m):
            z = zb[m % 2]
            q = sqb[m % 2]
            mm(ps_mu.ap, onesD.ap, z.ap, m == 0, m == NK - 1, [onesD.reg, z.reg], [ps_mu.reg])
            mm(ps_sq.ap, onesD.ap, q.ap, m == 0, m == NK - 1, [onesD.reg, q.reg], [ps_sq.reg])

        for m in range(NK):
            wb, wap = wstream.get()
            py = ps_y[m % 2]
            for kk in range(nkk):
                mm(py.ap, wap[:, kk, :], hbuf.ap[:, kk, :], kk == 0, kk == nkk - 1,
                   [wb.reg, hbuf.reg], [py.reg])
            if m >= 1:
                stats(m - 1)
            if xres is None:
                E.op("dve", lambda e, m=m, py=py: e.scalar_tensor_tensor(
                    out=x32.ap[:, m, :], in0=py.ap, scalar=gz.ap[:, s_, m:m + 1], in1=x32.ap[:, m, :],
                    op0=ALU.mult, op1=ALU.add), reads=[py.reg, gz.reg, x32.reg], writes=[x32.reg])
            else:
                xr_ = xres(m)
                E.op("dve", lambda e, m=m, py=py, xr_=xr_: e.scalar_tensor_tensor(
                    out=x32.ap[:, m, :], in0=py.ap, scalar=gz.ap[:, s_, m:m + 1], in1=xr_.ap,
                    op0=ALU.mult, op1=ALU.add), reads=[py.reg, gz.reg, xr_.reg], writes=[x32.reg])
            pump(1)
            z = zb[m % 2]
            q = sqb[m % 2]
            E.op("act", lambda e, m=m, z=z: e.activation(out=z.ap, in_=x32.ap[:, m, :], func=AF.Copy),
                 reads=[x32.reg], writes=[z.reg])
            E.op("act", lambda e, m=m, q=q: e.activation(out=q.ap, in_=x32.ap[:, m, :], func=AF.Square),
                 reads=[x32.reg], writes=[q.reg])
        stats(NK - 1)
        mean, rstd = tmp["mean"], tmp["rstd"]
        E.op("act", lambda e: e.activation(out=mean.ap, in_=ps_mu.ap, func=AF.Copy),
             reads=[ps_mu.reg], writes=[mean.reg])
        E.op("dve", lambda e: e.tensor_tensor(out=rstd.ap, in0=mean.ap, in1=mean.ap, op=ALU.mult),
             reads=[mean.reg], writes=[rstd.reg])
        E.op("dve", lambda e: e.tensor_tensor(out=rstd.ap, in0=ps_sq.ap, in1=rstd.ap, op=ALU.subtract),
             reads=[ps_sq.reg, rstd.reg], writes=[rstd.reg])
        rsqrt_inplace(rstd, EPS / (ALPHA * ALPHA))
        t1 = tmp["t1"]
        for k in range(NK):
            a = t1[k % 2]
            E.op("pool", lambda e, k=k, a=a: e.tensor_tensor(out=a.ap, in0=x32.ap[:, k, :], in1=mean.ap,
                                                           op=ALU.subtract),
                 reads=[x32.reg, mean.reg], writes=[a.reg])
            E.op("dve", lambda e, k=k, a=a: e.tensor_tensor(out=a.ap, in0=a.ap, in1=rstd.ap, op=ALU.mult),
                 reads=[a.reg, rstd.reg], writes=[a.reg])
            E.op("act", lambda e, k=k, a=a: e.activation(
                out=x32.ap[:, k, :], in_=a.ap, func=AF.Identity, scale=lng.ap[:, s_, k:k + 1],
                bias=lnb.ap[:, s_, k:k + 1]), reads=[a.reg, lng.reg, lnb.reg], writes=[x32.reg])
        v = dst_ap.rearrange("(k p) t -> p k t", p=128)[:, :, ti * T:(ti + 1) * T]
        E.dma(v, x32.ap, reads=[x32.reg], writes=[dst_reg], eng="act")

    def ln_tmps(pos):
        return {
            "zb": [carve(128, [T], BF16, pos) for _ in range(2)],
            "sqb": [carve(128, [T], BF16, pos) for _ in range(2)],
            "mean": carve(128, [T], F32, pos),
            "rstd": carve(128, [T], F32, pos),
            "t1": [carve(128, [T], F32, pos) for _ in range(2)],
        }

    def ffn(l, s_, wiu, wou, src_ap, src_regs, dst_ap, dst_regs):
        E.barrier()
        pos = [persist_top]
        x32 = carve(128, [NK, T], F32, pos)
        u = [carve(128, [NK, T], BF16, pos) for _ in range(2)]
        h = carve(128, [NJ, T], BF16, pos)
        wib = [carve(128, [NK, 256], BF16, pos) for _ in range(3)]
        wob = [carve(128, [NJ, 128], BF16, pos) for _ in range(2)]
        sg = [carve(128, [T], F32, pos) for _ in range(2)]
        xst = [carve(128, [T], F32, pos) for _ in range(2)]
        xrs = [carve(128, [T], F32, pos) for _ in range(2)]
        tmp = ln_tmps(pos)
        wis = Stream(list(wiu) * NT, wib)
        wos = Stream(list(wou) * NT, wob)
        wis.prime()
        srcv = src_ap.rearrange("(k p) t -> p k t", p=128)
        sh = shift_ap(s_)
        cnt = {"st": 0, "xr": 0}

        def prep_u(ti):
            ub = u[ti % 2]
            for k in range(NK):
                xb = xst[cnt["st"] % 2]
                cnt["st"] += 1
                E.dma(xb.ap, srcv[:, k, ti * T:(ti + 1) * T], reads=[src_regs[ti]], writes=[xb.reg])
                E.op("dve", lambda e, k=k, xb=xb, ub=ub: e.tensor_scalar(
                    out=ub.ap[:, k, :], in0=xb.ap, scalar1=sc1.ap[:, s_, k:k + 1], scalar2=sh[:, k:k + 1],
                    op0=ALU.mult, op1=ALU.add), reads=[xb.reg, sc1.reg, modsb.reg], writes=[ub.reg])

        prep_u(0)
        for ti in range(NT):
            ub = u[ti % 2]
            pend = {}

            def issue_xr(m, ti=ti, pend=pend):
                xb = xrs[cnt["xr"] % 2]
                cnt["xr"] += 1
                E.dma(xb.ap, srcv[:, m, ti * T:(ti + 1) * T], reads=[src_regs[ti]], writes=[xb.reg])
                pend[m] = xb

            def xres(m, pend=pend, issue_xr=issue_xr):
                if m not in pend:
                    issue_xr(m)
                b = pend.pop(m)
                return b

            for j in range(NJ):
                wb, wap = wis.get()
                if j == 2:
                    wos.prime()
                pg, pu = psum[j % 2], psum[2 + j % 2]
                for k in range(NK):
                    mm(pg.ap, wap[:, k, 0:128], ub.ap[:, k, :], k == 0, k == NK - 1, [wb.reg, ub.reg], [pg.reg])
                for k in range(NK):
                    mm(pu.ap, wap[:, k, 128:256], ub.ap[:, k, :], k == 0, k == NK - 1, [wb.reg, ub.reg], [pu.reg])
                g_ = sg[j % 2]
                E.op("act", lambda e, pg=pg, g_=g_: e.activation(out=g_.ap, in_=pg.ap, func=AF.Silu),
                     reads=[pg.reg], writes=[g_.reg])
                E.op("dve", lambda e, pu=pu, g_=g_, j=j: e.tensor_tensor(
                    out=h.ap[:, j, :], in0=pu.ap, in1=g_.ap, op=ALU.mult),
                    reads=[pu.reg, g_.reg], writes=[h.reg])
                pump(1)
                if j == 8 and ti + 1 < NT:
                    prep_u(ti + 1)
            issue_xr(0)
            proj_ln(s_, h, NJ, wos, x32, tmp, dst_ap, dst_regs[ti], ti, xres=xres)

    mlq32 = Buf(dscr("mlq32", [512, S], F32)); mlk32 = Buf(dscr("mlk32", [512, S], F32))
    gate_i = Buf(dscr("gate_i", [4, S], F32)); gate_f = Buf(dscr("gate_f", [4, S], F32))
    mlv_tm = Buf(dscr("mlv_tm", [S, 512], BF16)); mlo_s = Buf(dscr("mlo_s", [512, S], BF16))
    glq32 = Buf(dscr("glq32", [4, 64, S], F32)); glk32 = Buf(dscr("glk32", [4, 64, S], F32))
    gla_neg = Buf(dscr("gla_neg", [4, 64, S], F32))
    glv_tm = Buf(dscr("glv_tm", [S, 512], BF16)); glr_s = Buf(dscr("glr_s", [512, S], BF16))
    QN = Buf(dscr("QN", [8, 128, S], BF16)); QR = Buf(dscr("QR", [8, 64, S], BF16))
    KN = Buf(dscr("KN", [8, 128, S], BF16)); KR = Buf(dscr("KR", [64, S], BF16))
    V_tm = Buf(dscr("V_tm", [S, 1024], BF16))
    ymix = Buf(dscr("ymix", [D, S], BF16))
    cos_s = Buf(dscr("cos_s", [64, S], F32)); sin_s = Buf(dscr("sin_s", [64, S], F32))

    def mixer_units(l):
        U = {}
        wi2 = w_in[l]
        wv = wi2.rearrange("(k p) n -> p k n", p=128)
        for nm, c0 in (("mlq0", C_MLQ), ("mlq1", C_MLQ + 256), ("mlk0", C_MLK), ("mlk1", C_MLK + 256),
                       ("mlo0", C_MLO), ("mlo1", C_MLO + 256), ("glr0", C_GLR), ("glr1", C_GLR + 256),
                       ("cq0", C_CQ), ("cq1", C_CQ + 256), ("ckv0", C_CKV), ("ckv1", C_CKV + 256),
                       ("glq", C_GLQ), ("glk", C_GLK),
                       ("mlv0", C_MLV), ("mlv1", C_MLV + 256), ("glv0", C_GLV), ("glv1", C_GLV + 256)):
            U[nm] = colunit(wi2, "u%d_%s" % (l, nm), c0, 256)
        U["small"] = new_unit("u%d_small" % l, [NK, 152], [
            (wv[:, :, C_KR:C_KR + 64], lambda d: d[:, :, 0:64]),
            (wv[:, :, C_KR + 32:C_KR + 64], lambda d: d[:, :, 64:96]),
            (wv[:, :, C_KR:C_KR + 32], lambda d: d[:, :, 96:128]),
            (wv[:, :, C_GLLR:C_GLLR + 16], lambda d: d[:, :, 128:144]),
            (wv[:, :, C_MLI:C_MLI + 8], lambda d: d[:, :, 144:152])])
        wq = mla_wuq[l].rearrange("(k p) n -> p k n", p=128)
        for g in range(4):
            U["qn%d" % g] = new_unit("u%d_qn%d" % (l, g), [4, 256], [
                (wq[:, :, (2 * g) * 192:(2 * g) * 192 + 128], lambda d: d[:, :, 0:128]),
                (wq[:, :, (2 * g + 1) * 192:(2 * g + 1) * 192 + 128], lambda d: d[:, :, 128:256])])
            srcs = []
            for hh in range(2):
                b0 = (2 * g + hh) * 192 + 128
                o = hh * 128
                srcs.append((wq[:, :, b0:b0 + 64], lambda d, o=o: d[:, :, o:o + 64]))
                srcs.append((wq[:, :, b0 + 32:b0 + 64], lambda d, o=o: d[:, :, o + 64:o + 96]))
                srcs.append((wq[:, :, b0:b0 + 32], lambda d, o=o: d[:, :, o + 96:o + 128]))
            U["qr%d" % g] = new_unit("u%d_qr%d" % (l, g), [4, 256], srcs)
            U["kn%d" % g] = colunit(mla_wuk[l], "u%d_kn%d" % (l, g), g * 256, 256, nk=4)
            U["vv%d" % g] = colunit(mla_wuv[l], "u%d_vv%d" % (l, g), g * 256, 256, nk=4)
        U["wo"] = [colunit(w_out[l], "u%d_wout%d" % (l, m), m * 128, 128) for m in range(NK)]
        return U

    def rsqrt_to(outb, in_ap, in_regs, eps, out_ap=None):
        o = outb.ap if out_ap is None else out_ap
        E.op("act", lambda e: e.activation(out=o, in_=in_ap, func=AF.Sqrt, bias=float(eps)),
             reads=in_regs, writes=[outb.reg])
        E.op("dve", lambda e: e.reciprocal(out=o, in_=o), reads=[outb.reg], writes=[outb.reg])

    def rope_tables():
        E.barrier()
        pos = [persist_top]
        pi_ = carve(64, [S], I32, pos)
        pf = carve(64, [S], F32, pos)
        ang = carve(64, [S], F32, pos)
        res = carve(64, [S], F32, pos)
        invf = carve(64, [1], F32, pos)
        E.dma(invf.ap, invf_in, writes=[invf.reg])
        E.dma(pi_.ap, pos_in.partition_broadcast(64), writes=[pi_.reg])
        E.op("dve", lambda e: e.tensor_copy(out=pf.ap, in_=pi_.ap), reads=[pi_.reg], writes=[pf.reg])
        E.op("dve", lambda e: e.tensor_scalar_mul(out=pf.ap, in0=pf.ap, scalar1=invf.ap[:, 0:1]),
             reads=[pf.reg, invf.reg], writes=[pf.reg])
        twopi = 2.0 * math.pi
        ki = carve(64, [S], I32, pos)
        kf = carve(64, [S], F32, pos)

        def reduce_sin(shift, negate_lo, dstb):
            E.op("dve", lambda e: e.tensor_scalar(out=kf.ap, in0=pf.ap, scalar1=float(shift), scalar2=1.0 / twopi,
                                                  op0=ALU.add, op1=ALU.mult), reads=[pf.reg], writes=[kf.reg])
            E.op("dve", lambda e: e.tensor_copy(out=ki.ap, in_=kf.ap), reads=[kf.reg], writes=[ki.reg])
            E.op("dve", lambda e: e.tensor_copy(out=kf.ap, in_=ki.ap), reads=[ki.reg], writes=[kf.reg])
            E.op("dve", lambda e: e.tensor_scalar_add(out=ang.ap, in0=pf.ap, scalar1=float(shift)),
                 reads=[pf.reg, res.reg], writes=[ang.reg])
            E.op("dve", lambda e: e.scalar_tensor_tensor(out=ang.ap, in0=kf.ap, scalar=-twopi, in1=ang.ap,
                                                         op0=ALU.mult, op1=ALU.add), reads=[kf.reg, ang.reg], writes=[ang.reg])
            E.op("dve", lambda e: e.tensor_single_scalar(out=kf.ap, in_=ang.ap, scalar=math.pi, op=ALU.is_gt),
                 reads=[ang.reg], writes=[kf.reg])
            E.op("dve", lambda e: e.scalar_tensor_tensor(out=ang.ap, in0=kf.ap, scalar=-twopi, in1=ang.ap,
                                                         op0=ALU.mult, op1=ALU.add), reads=[kf.reg, ang.reg], writes=[ang.reg])
            E.op("dve", lambda e: e.tensor_single_scalar(out=kf.ap, in_=ang.ap, scalar=-math.pi, op=ALU.is_lt),
                 reads=[ang.reg], writes=[kf.reg])
            E.op("dve", lambda e: e.scalar_tensor_tensor(out=ang.ap, in0=kf.ap, scalar=twopi, in1=ang.ap,
                                                         op0=ALU.mult, op1=ALU.add), reads=[kf.reg, ang.reg], writes=[ang.reg])
            E.op("dve", lambda e: e.tensor_scalar(out=ang.ap, in0=ang.ap, scalar1=-3.1415925, scalar2=3.1415925,
                                                  op0=ALU.max, op1=ALU.min), reads=[ang.reg], writes=[ang.reg])
            E.op("act", lambda e: e.activation(out=res.ap, in_=ang.ap, func=AF.Sin), reads=[ang.reg, dstb.reg], writes=[res.reg])
            if negate_lo:
                E.op("dve", lambda e: e.tensor_scalar_mul(out=res.ap[0:32, :], in0=res.ap[0:32, :], scalar1=-1.0),
                     reads=[res.reg], writes=[res.reg])
            E.dma(dstb.ap, res.ap, reads=[res.reg], writes=[dstb.reg])

        reduce_sin(0.0, True, sin_s)
        reduce_sin(0.5 * math.pi, False, cos_s)

    def inproj(l, U):
        E.barrier()
        pos = [persist_top]
        x32 = carve(128, [NK, T], F32, pos)
        u = carve(128, [NK, T], BF16, pos)
        wb = [carve(128, [NK, 256], BF16, pos) for _ in range(4)]
        conv = carve(128, [8, 4], F32, pos)
        halo = carve(128, [8, 3], F32, pos)
        bi = carve(4, [1], F32, pos); nbf = carve(4, [1], F32, pos)
        wg = carve(16, [256], F32, pos); nbg = carve(64, [4], F32, pos)
        gq = carve(128, [4], F32, pos); gkv = carve(128, [4], F32, pos)
        pre = [carve(128, [T + 3], F32, pos) for _ in range(2)]
        acc = [carve(128, [T], F32, pos) for _ in range(2)]
        stg32 = [carve(128, [T], F32, pos) for _ in range(3)]
        stg16 = [carve(128, [T], BF16, pos) for _ in range(3)]
        lr32 = carve(16, [T], F32, pos)
        cq32 = carve(128, [4, T], F32, pos)
        cqn = carve(128, [4, T], BF16, pos)
        ckvn = carve(128, [4, T], BF16, pos)
        sqb = [carve(128, [T], BF16, pos) for _ in range(2)]
        rstd = carve(128, [T], F32, pos)
        cs = carve(64, [T], F32, pos); sn = carve(64, [T], F32, pos)
        rt1 = [carve(64, [T], F32, pos) for _ in range(2)]
        rt2 = [carve(64, [T], F32, pos) for _ in range(2)]
        cnt = {"ps": 0, "s32": 0, "s16": 0, "pre": 0, "rt": 0}

        E.dma(conv.ap, ml_conv[l], writes=[conv.reg])
        E.dma(bi.ap, ml_bi[l], writes=[bi.reg])
        E.dma(nbf.ap, ml_bf[l], writes=[nbf.reg])
        E.op("dve", lambda e: e.tensor_scalar_mul(out=nbf.ap, in0=nbf.ap, scalar1=-1.0), reads=[nbf.reg], writes=[nbf.reg])
        E.dma(wg.ap, gla_wg[l], writes=[wg.reg])
        E.dma(nbg.ap, gla_bg[l], writes=[nbg.reg])
        E.op("dve", lambda e: e.tensor_scalar_mul(out=nbg.ap, in0=nbg.ap, scalar1=-1.0), reads=[nbg.reg], writes=[nbg.reg])
        E.dma(gq.ap, mla_gq[l], writes=[gq.reg])
        E.dma(gkv.ap, mla_gkv[l], writes=[gkv.reg])
        E.op("pool", lambda e: e.memset(halo.ap, 0.0), writes=[halo.reg])

        def nxt(key, n):
            v = cnt[key] % n
            cnt[key] += 1
            return v

        order = (["mlq0", "mlq1", "mlk0", "mlk1", "mlo0", "mlo1", "glr0", "glr1", "glq", "glk", "small",
                  "mlv0", "mlv1", "glv0", "glv1", "cq0", "cq1", "ckv0", "ckv1"]
                 + [n_ % g for g in range(4) for n_ in ("qn%d", "qr%d", "kn%d", "vv%d")])
        ws = Stream([U[n_] for n_ in order] * NT, wb)
        ws.prime()

        def getw(unit, nk=NK):
            b, v = ws.get()
            assert ws.seq[ws.iu - 1] is unit
            return b, v

        def proj_fm(wap, wreg, c0, w, rhs, rreg, nk=NK, tcols=None):
            ps = psum[nxt("ps", 4)]
            for k in range(nk):
                mm(ps.ap[0:w, :], wap[:, k, c0:c0 + w], rhs.ap[:, k, :], k == 0, k == nk - 1,
                   [wreg, rreg], [ps.reg])
            return ps

        def store(dst_ap, dst_reg, sb, sb_ap):
            E.dma(dst_ap, sb_ap, reads=[sb.reg], writes=[dst_reg], eng="act")

        for ti in range(NT):
            tsl = slice(ti * T, (ti + 1) * T)
            load_x_tile(xs.ap, xs_tiles[ti], ti, x32)
            modulate(1, x32, u)
            E.dma(cs.ap, cos_s.ap[:, tsl], reads=[cos_s.reg], writes=[cs.reg])
            E.dma(sn.ap, sin_s.ap[:, tsl], reads=[sin_s.reg], writes=[sn.reg])

            for gi, (nm, dstb) in enumerate((("mlq0", mlq32), ("mlq1", mlq32), ("mlk0", mlk32), ("mlk1", mlk32))):
                b, wap = getw(U[nm])
                for sub in range(2):
                    ch = gi * 2 + sub
                    ps = proj_fm(wap, b.reg, sub * 128, 128, u, u.reg)
                    p_ = pre[nxt("pre", 2)]
                    a_ = acc[cnt["pre"] % 2]
                    E.op("act", lambda e, p_=p_, ps=ps: e.activation(out=p_.ap[:, 3:T + 3], in_=ps.ap, func=AF.Copy),
                         reads=[ps.reg], writes=[p_.reg])
                    E.op("pool", lambda e, p_=p_, ch=ch: e.tensor_copy(out=p_.ap[:, 0:3], in_=halo.ap[:, ch, :]),
                         reads=[halo.reg], writes=[p_.reg])
                    E.op("dve", lambda e, p_=p_, a_=a_, ch=ch: e.tensor_scalar_mul(
                        out=a_.ap, in0=p_.ap[:, 0:T], scalar1=conv.ap[:, ch, 0:1]),
                        reads=[p_.reg, conv.reg], writes=[a_.reg])
                    for j in range(1, 4):
                        E.op("dve", lambda e, p_=p_, a_=a_, ch=ch, j=j: e.scalar_tensor_tensor(
                            out=a_.ap, in0=p_.ap[:, j:j + T], scalar=conv.ap[:, ch, j:j + 1], in1=a_.ap,
                            op0=ALU.mult, op1=ALU.add), reads=[p_.reg, conv.reg, a_.reg], writes=[a_.reg])
                    E.op("pool", lambda e, p_=p_, ch=ch: e.tensor_copy(out=halo.ap[:, ch, :], in_=p_.ap[:, T:T + 3]),
                         reads=[p_.reg], writes=[halo.reg])
                    s_ = stg32[nxt("s32", 3)]
                    E.op("act", lambda e, a_=a_, s_=s_: e.activation(out=s_.ap, in_=a_.ap, func=AF.Silu),
                         reads=[a_.reg], writes=[s_.reg])
                    row = (ch % 4) * 128
                    store(dstb.ap[row:row + 128, tsl], dstb.reg, s_, s_.ap)
            for gi, (nm, dstb, fn) in enumerate((("mlo0", mlo_s, AF.Sigmoid), ("mlo1", mlo_s, AF.Sigmoid),
                                                 ("glr0", glr_s, AF.Silu), ("glr1", glr_s, AF.Silu))):
                b, wap = getw(U[nm])
                for sub in range(2):
                    ps = proj_fm(wap, b.reg, sub * 128, 128, u, u.reg)
                    s_ = stg16[nxt("s16", 3)]
                    E.op("act", lambda e, ps=ps, s_=s_, fn=fn: e.activation(out=s_.ap, in_=ps.ap, func=fn),
                         reads=[ps.reg], writes=[s_.reg])
                    row = ((gi % 2) * 2 + sub) * 128
                    store(dstb.ap[row:row + 128, tsl], dstb.reg, s_, s_.ap)
            for nm, dstb in (("glq", glq32), ("glk", glk32)):
                b, wap = getw(U[nm])
                for hh in range(4):
                    ps = proj_fm(wap, b.reg, hh * 64, 64, u, u.reg)
                    s_ = stg32[nxt("s32", 3)]
                    E.op("act", lambda e, ps=ps, s_=s_: e.activation(out=s_.ap[0:64, :], in_=ps.ap[0:64, :], func=AF.Copy),
                         reads=[ps.reg], writes=[s_.reg])
                    store(dstb.ap[hh, :, tsl], dstb.reg, s_, s_.ap[0:64, :])
            b, wap = getw(U["small"])
            ps_r = proj_fm(wap, b.reg, 0, 64, u, u.reg)
            ps_w = proj_fm(wap, b.reg, 64, 64, u, u.reg)
            ri = nxt("rt", 2)
            E.op("dve", lambda e, ps_r=ps_r, ri=ri: e.tensor_tensor(out=rt1[ri].ap, in0=ps_r.ap[0:64, :], in1=cs.ap, op=ALU.mult),
                 reads=[ps_r.reg, cs.reg], writes=[rt1[ri].reg])
            E.op("dve", lambda e, ps_w=ps_w, ri=ri: e.tensor_tensor(out=rt2[ri].ap, in0=ps_w.ap[0:64, :], in1=sn.ap, op=ALU.mult),
                 reads=[ps_w.reg, sn.reg], writes=[rt2[ri].reg])
            s_ = stg16[nxt("s16", 3)]
            E.op("pool", lambda e, s_=s_, ri=ri: e.tensor_tensor(out=s_.ap[0:64, :], in0=rt1[ri].ap, in1=rt2[ri].ap, op=ALU.add),
                 reads=[rt1[ri].reg, rt2[ri].reg], writes=[s_.reg])
            store(KR.ap[:, tsl], KR.reg, s_, s_.ap[0:64, :])
            ps = proj_fm(wap, b.reg, 128, 16, u, u.reg)
            E.op("act", lambda e, ps=ps: e.activation(out=lr32.ap, in_=ps.ap[0:16, :], func=AF.Copy),
                 reads=[ps.reg], writes=[lr32.reg])
            for hh in range(4):
                ps2 = psum[nxt("ps", 4)]
                mm(ps2.ap[0:64, :], wg.ap[:, hh * 64:(hh + 1) * 64], lr32.ap, True, True, [wg.reg, lr32.reg], [ps2.reg])
                s_ = stg32[nxt("s32", 3)]
                E.op("act", lambda e, ps2=ps2, s_=s_, hh=hh: e.activation(
                    out=s_.ap[0:64, :], in_=ps2.ap[0:64, :], func=AF.Exp, scale=-1.0, bias=nbg.ap[:, hh:hh + 1]),
                    reads=[ps2.reg, nbg.reg], writes=[s_.reg])
                E.op("act", lambda e, s_=s_: e.activation(out=s_.ap[0:64, :], in_=s_.ap[0:64, :], func=AF.Ln, bias=1.0),
                     reads=[s_.reg], writes=[s_.reg])
                store(gla_neg.ap[hh, :, tsl], gla_neg.reg, s_, s_.ap[0:64, :])
            ps = proj_fm(wap, b.reg, 144, 4, u, u.reg)
            s_ = stg32[nxt("s32", 3)]
            E.op("act", lambda e, ps=ps, s_=s_: e.activation(out=s_.ap[0:4, :], in_=ps.ap[0:4, :], func=AF.Identity, bias=bi.ap[:, 0:1]),
                 reads=[ps.reg, bi.reg], writes=[s_.reg])
            store(gate_i.ap[:, tsl], gate_i.reg, s_, s_.ap[0:4, :])
            ps = proj_fm(wap, b.reg, 148, 4, u, u.reg)
            s_ = stg32[nxt("s32", 3)]
            E.op("act", lambda e, ps=ps, s_=s_: e.activation(out=s_.ap[0:4, :], in_=ps.ap[0:4, :], func=AF.Exp, scale=-1.0, bias=nbf.ap[:, 0:1]),
                 reads=[ps.reg, nbf.reg], writes=[s_.reg])
            E.op("act", lambda e, s_=s_: e.activation(out=s_.ap[0:4, :], in_=s_.ap[0:4, :], func=AF.Ln, bias=1.0),
                 reads=[s_.reg], writes=[s_.reg])
            store(gate_f.ap[:, tsl], gate_f.reg, s_, s_.ap[0:4, :])
            for nm, dstb, c0 in (("mlv0", mlv_tm, 0), ("mlv1", mlv_tm, 256), ("glv0", glv_tm, 0), ("glv1", glv_tm, 256)):
                b, wap = getw(U[nm])
                for blk in range(4):
                    ps = psum[4 + nxt("ps", 4) % 2]
                    for k in range(NK):
                        mm(ps.ap[:, 0:256], u.ap[:, k, blk * 128:(blk + 1) * 128], wap[:, k, :], k == 0, k == NK - 1,
                           [b.reg, u.reg], [ps.reg])
                    s_ = stg16[nxt("s16", 3)]
                    E.op("act", lambda e, ps=ps, s_=s_: e.activation(out=s_.ap[:, 0:256], in_=ps.ap[:, 0:256], func=AF.Copy),
                         reads=[ps.reg], writes=[s_.reg])
                    r0 = ti * T + blk * 128
                    store(dstb.ap[r0:r0 + 128, c0:c0 + 256], dstb.reg, s_, s_.ap[:, 0:256])
            for nm0, nm1, gsb, outn in (("cq0", "cq1", gq, cqn), ("ckv0", "ckv1", gkv, ckvn)):
                ps_ms = psum[6]
                for gi, nm in enumerate((nm0, nm1)):
                    b, wap = getw(U[nm])
                    for sub in range(2):
                        cidx = gi * 2 + sub
                        ps = proj_fm(wap, b.reg, sub * 128, 128, u, u.reg)
                        E.op("act", lambda e, ps=ps, cidx=cidx: e.activation(out=cq32.ap[:, cidx, :], in_=ps.ap, func=AF.Copy),
                             reads=[ps.reg], writes=[cq32.reg])
                        q_ = sqb[cidx % 2]
                        E.op("act", lambda e, ps=ps, q_=q_: e.activation(out=q_.ap, in_=ps.ap, func=AF.Square),
                             reads=[ps.reg], writes=[q_.reg])
                        mm(ps_ms.ap, ones512.ap, q_.ap, cidx == 0, cidx == 3, [ones512.reg, q_.reg], [ps_ms.reg])
                rsqrt_to(rstd, ps_ms.ap, [ps_ms.reg], EPS)
                for cidx in range(4):
                    E.op("dve", lambda e, cidx=cidx, gsb=gsb, outn=outn: e.scalar_tensor_tensor(
                        out=outn.ap[:, cidx, :], in0=cq32.ap[:, cidx, :], scalar=gsb.ap[:, cidx:cidx + 1], in1=rstd.ap,
                        op0=ALU.mult, op1=ALU.mult), reads=[cq32.reg, gsb.reg, rstd.reg], writes=[outn.reg])
            for g in range(4):
                b, wap = getw(U["qn%d" % g], nk=4)
                for hh in range(2):
                    ps = proj_fm(wap, b.reg, hh * 128, 128, cqn, cqn.reg, nk=4)
                    s_ = stg16[nxt("s16", 3)]
                    E.op("act", lambda e, ps=ps, s_=s_: e.activation(out=s_.ap, in_=ps.ap, func=AF.Copy),
                         reads=[ps.reg], writes=[s_.reg])
                    store(QN.ap[2 * g + hh, :, tsl], QN.reg, s_, s_.ap)
                b, wap = getw(U["qr%d" % g], nk=4)
                for hh in range(2):
                    ps_r = proj_fm(wap, b.reg, hh * 128, 64, cqn, cqn.reg, nk=4)
                    ps_w = proj_fm(wap, b.reg, hh * 128 + 64, 64, cqn, cqn.reg, nk=4)
                    ri = nxt("rt", 2)
                    E.op("dve", lambda e, ps_r=ps_r, ri=ri: e.tensor_tensor(out=rt1[ri].ap, in0=ps_r.ap[0:64, :], in1=cs.ap, op=ALU.mult),
                         reads=[ps_r.reg, cs.reg], writes=[rt1[ri].reg])
                    E.op("dve", lambda e, ps_w=ps_w, ri=ri: e.tensor_tensor(out=rt2[ri].ap, in0=ps_w.ap[0:64, :], in1=sn.ap, op=ALU.mult),
                         reads=[ps_w.reg, sn.reg], writes=[rt2[ri].reg])
                    s_ = stg16[nxt("s16", 3)]
                    E.op("pool", lambda e, s_=s_, ri=ri: e.tensor_tensor(out=s_.ap[0:64, :], in0=rt1[ri].ap, in1=rt2[ri].ap, op=ALU.add),
                         reads=[rt1[ri].reg, rt2[ri].reg], writes=[s_.reg])
                    store(QR.ap[2 * g + hh, :, tsl], QR.reg, s_, s_.ap[0:64, :])
                b, wap = getw(U["kn%d" % g], nk=4)
                for hh in range(2):
                    ps = proj_fm(wap, b.reg, hh * 128, 128, ckvn, ckvn.reg, nk=4)
                    s_ = stg16[nxt("s16", 3)]
                    E.op("act", lambda e, ps=ps, s_=s_: e.activation(out=s_.ap, in_=ps.ap, func=AF.Copy),
                         reads=[ps.reg], writes=[s_.reg])
                    store(KN.ap[2 * g + hh, :, tsl], KN.reg, s_, s_.ap)
                b, wap = getw(U["vv%d" % g], nk=4)
                for blk in range(4):
                    ps = psum[4 + nxt("ps", 4) % 2]
                    for k in range(4):
                        mm(ps.ap[:, 0:256], ckvn.ap[:, k, blk * 128:(blk + 1) * 128], wap[:, k, :], k == 0, k == 3,
                           [b.reg, ckvn.reg], [ps.reg])
                    s_ = stg16[nxt("s16", 3)]
                    E.op("act", lambda e, ps=ps, s_=s_: e.activation(out=s_.ap[:, 0:256], in_=ps.ap[:, 0:256], func=AF.Copy),
                         reads=[ps.reg], writes=[s_.reg])
                    r0 = ti * T + blk * 128
                    store(V_tm.ap[r0:r0 + 128, g * 256:(g + 1) * 256], V_tm.reg, s_, s_.ap[:, 0:256])
            pump(2)

    def linattn(l):
        E.barrier()
        pos = [persist_top]
        NC_ = S // 128
        q32 = carve(128, [S], F32, pos)
        k32 = carve(128, [S], F32, pos)
        la = carve(128, [S], F32, pos)
        ex = carve(128, [S], F32, pos)
        rmask = carve(128, [S], F32, pos)
        qd = carve(128, [S], BF16, pos)
        kd = carve(128, [S], BF16, pos)
        ks = carve(128, [S], BF16, pos)
        ks_tm = carve(128, [NC_, 128], BF16, pos)
        vaug = carve(128, [NC_, 256], BF16, pos)
        dec = carve(128, [NC_], F32, pos)
        S32 = carve(128, [256], F32, pos)
        Sbf = [carve(128, [256], BF16, pos) for _ in range(2)]
        am = [carve(128, [128], BF16, pos) for _ in range(2)]
        gt = carve(128, [T], BF16, pos)
        hh32 = carve(128, [T], F32, pos)
        t32 = carve(128, [T], F32, pos)
        mean = carve(128, [T], F32, pos)
        rstd = carve(128, [T], F32, pos)
        zb = carve(128, [T], BF16, pos)
        sqb_ = carve(128, [T], BF16, pos)
        yst = carve(128, [T], BF16, pos)
        mi = Buf(ex.ap.bitcast(I32), ex.reg)
        lnq = carve(128, [1], F32, pos)
        lnk = carve(128, [1], F32, pos)
        pt = Buf(psum[7].ap.bitcast(BF16), psum[7].reg)

        E.op("pool", lambda e: e.iota(mi.ap, pattern=[[0, NC_], [1, 128]], base=0, channel_multiplier=0), writes=[mi.reg])
        E.op("dve", lambda e: e.tensor_copy(out=rmask.ap, in_=mi.ap), reads=[mi.reg], writes=[rmask.reg])
        E.op("dve", lambda e: e.tensor_scalar_min(out=rmask.ap, in0=rmask.ap, scalar1=1.0), reads=[rmask.reg], writes=[rmask.reg])
        E.op("pool", lambda e: e.memset(vaug.ap[:, :, 128:256], 1.0), writes=[vaug.reg])

        for kind in ("ml", "gl"):
            dk = 128 if kind == "ml" else 64
            for hd in range(4):
                if kind == "ml":
                    E.dma(q32.ap, mlq32.ap[hd * 128:(hd + 1) * 128, :], reads=[mlq32.reg], writes=[q32.reg])
                    E.dma(k32.ap, mlk32.ap[hd * 128:(hd + 1) * 128, :], reads=[mlk32.reg], writes=[k32.reg])
                    E.dma(la.ap, gate_f.ap[hd:hd + 1, :].partition_broadcast(128), reads=[gate_f.reg], writes=[la.reg])
                    E.dma(ex.ap, gate_i.ap[hd:hd + 1, :].partition_broadcast(128), reads=[gate_i.reg], writes=[ex.reg])
                    vsrc = mlv_tm
                    sc_dec = -1.0
                    qscale, kscale = 1.0, 128.0 ** -0.5
                else:
                    E.dma(q32.ap[0:64, :], glq32.ap[hd], reads=[glq32.reg], writes=[q32.reg])
                    E.dma(k32.ap[0:64, :], glk32.ap[hd], reads=[glk32.reg], writes=[k32.reg])
                    E.dma(la.ap[0:64, :], gla_neg.ap[hd], reads=[gla_neg.reg], writes=[la.reg])
                    vsrc = glv_tm
                    sc_dec = -1.0 / 16.0
                    qscale, kscale = 64.0 ** -0.5, 1.0
                E.dma(vaug.ap[:, :, 0:128],
                      vsrc.ap[:, hd * 128:(hd + 1) * 128].rearrange("(c p) d -> p c d", p=128),
                      reads=[vsrc.reg], writes=[vaug.reg])
                P = slice(0, dk)
                E.op("pool", lambda e, qscale=qscale: e.memset(lnq.ap, math.log(qscale)), writes=[lnq.reg])
                E.op("pool", lambda e, kscale=kscale: e.memset(lnk.ap, math.log(kscale)), writes=[lnk.reg])
                E.op("dve", lambda e, P=P: e.tensor_tensor_scan(out=la.ap[P, :], data0=rmask.ap[P, :], data1=la.ap[P, :],
                                                                initial=0.0, op0=ALU.mult, op1=ALU.add),
                     reads=[la.reg, rmask.reg], writes=[la.reg])
                lav = la.ap.rearrange("p (c t) -> p c t", t=128)
                E.op("act", lambda e, P=P, sc_dec=sc_dec, lav=lav: e.activation(out=dec.ap[P, :], in_=lav[P, :, 127], func=AF.Exp, scale=sc_dec),
                     reads=[la.reg], writes=[dec.reg])
                if kind == "ml":
                    E.op("dve", lambda e: e.tensor_tensor(out=ex.ap, in0=ex.ap, in1=la.ap, op=ALU.add),
                         reads=[ex.reg, la.reg], writes=[ex.reg])
                    E.op("act", lambda e: e.activation(out=ex.ap, in_=ex.ap, func=AF.Exp, bias=lnk.ap[:, 0:1]),
                         reads=[ex.reg, lnk.reg], writes=[ex.reg])
                else:
                    E.op("act", lambda e, P=P: e.activation(out=ex.ap[P, :], in_=la.ap[P, :], func=AF.Exp, scale=1.0 / 16.0),
                         reads=[la.reg], writes=[ex.reg])
                E.op("dve", lambda e, P=P: e.tensor_tensor(out=k32.ap[P, :], in0=k32.ap[P, :], in1=ex.ap[P, :], op=ALU.mult),
                     reads=[k32.reg, ex.reg], writes=[k32.reg])
                E.op("act", lambda e, P=P: e.activation(out=kd.ap[P, :], in_=k32.ap[P, :], func=AF.Copy),
                     reads=[k32.reg], writes=[kd.reg])
                k3 = k32.ap.rearrange("p (c t) -> p c t", t=128)
                ks3 = ks.ap.rearrange("p (c t) -> p c t", t=128)
                E.op("dve", lambda e, P=P, dk=dk, k3=k3, ks3=ks3: e.tensor_tensor(out=ks3[P], in0=k3[P], in1=dec.ap[P, :].unsqueeze(2).to_broadcast([dk, NC_, 128]), op=ALU.mult),
                     reads=[k32.reg, dec.reg], writes=[ks.reg])
                E.op("act", lambda e, P=P, sc_dec=sc_dec: e.activation(out=ex.ap[P, :], in_=la.ap[P, :], func=AF.Exp, scale=sc_dec, bias=lnq.ap[P, 0:1]),
                     reads=[la.reg, lnq.reg, k32.reg], writes=[ex.reg])
                E.op("dve", lambda e, P=P: e.tensor_tensor(out=qd.ap[P, :], in0=q32.ap[P, :], in1=ex.ap[P, :], op=ALU.mult),
                     reads=[q32.reg, ex.reg], writes=[qd.reg])
                for c8 in range(NC_ // 8):
                    for i in range(8):
                        c = c8 * 8 + i
                        E.op("pe", lambda e, c=c, i=i, P=P, dk=dk, ks3=ks3: e.transpose(pt.ap[:, i * 128:i * 128 + dk], ks3[P, c, :], ident.ap[P, 0:dk]),
                             reads=[ks.reg, ident.reg], writes=[pt.reg])
                    ptv = pt.ap.rearrange("p (a b) -> p a b", a=8)
                    E.op("act", lambda e, c8=c8, ptv=ptv, dk=dk: e.activation(out=ks_tm.ap[:, c8 * 8:(c8 + 1) * 8, 0:dk], in_=ptv[:, :, 0:dk], func=AF.Copy),
                         reads=[pt.reg], writes=[ks_tm.reg])
                nw = 256 if kind == "ml" else 128
                pSb = [Buf(psum[6].ap[:, 0:256]), Buf(psum[6].ap[:, 256:512])]

                def stageA(c, P=P, dk=dk, kind=kind, nw=nw, pSb=pSb):
                    csl = slice(c * 128, (c + 1) * 128)
                    pa = psum[4 + c % 2]
                    mm(pa.ap[:, 0:128], kd.ap[P, csl], qd.ap[P, csl], True, True, [kd.reg, qd.reg], [pa.reg])
                    a_ = am[c % 2]
                    E.op("dve", lambda e, pa=pa, a_=a_: e.tensor_tensor(out=a_.ap, in0=pa.ap[:, 0:128], in1=tri.ap, op=ALU.mult),
                         reads=[pa.reg, tri.reg], writes=[a_.reg])
                    if c < NC_ - 1:
                        pS = pSb[c % 2]
                        mm(pS.ap[P, 0:nw], ks_tm.ap[:, c, 0:dk], vaug.ap[:, c, 0:nw], True, True,
                           [ks_tm.reg, vaug.reg], [pS.reg])

                def stageB(c, po, pd, P=P, dk=dk, kind=kind, nw=nw, pSb=pSb):
                    csl = slice(c * 128, (c + 1) * 128)
                    ci = c % 4
                    osl = slice(ci * 128, (ci + 1) * 128)
                    a_ = am[c % 2]
                    first = (c == 0)
                    Sp = Sbf[(c + 1) % 2]
                    mm(po.ap[:, osl], vaug.ap[:, c, 0:128], a_.ap, True, first, [vaug.reg, a_.reg], [po.reg])
                    if not first:
                        mm(po.ap[:, osl], Sp.ap[P, 0:128], qd.ap[P, csl], False, True, [Sp.reg, qd.reg], [po.reg])
                    if kind == "ml":
                        mm(pd.ap[:, osl], ones_bf.ap, a_.ap, True, first, [ones_bf.reg, a_.reg], [pd.reg])
                        if not first:
                            mm(pd.ap[:, osl], Sp.ap[:, 128:256], qd.ap[:, csl], False, True, [Sp.reg, qd.reg], [pd.reg])
                    if c < NC_ - 1:
                        pS = pSb[c % 2]
                        if first:
                            E.op("dve", lambda e, pS=pS: e.tensor_copy(out=S32.ap[P, 0:nw], in_=pS.ap[P, 0:nw]),
                                 reads=[pS.reg], writes=[S32.reg])
                        else:
                            E.op("dve", lambda e, pS=pS, c=c: e.scalar_tensor_tensor(
                                out=S32.ap[P, 0:nw], in0=S32.ap[P, 0:nw], scalar=dec.ap[P, c:c + 1], in1=pS.ap[P, 0:nw],
                                op0=ALU.mult, op1=ALU.add), reads=[S32.reg, dec.reg, pS.reg], writes=[S32.reg])
                        Sn = Sbf[c % 2]
                        E.op("act", lambda e, Sn=Sn: e.activation(out=Sn.ap[P, 0:nw], in_=S32.ap[P, 0:nw], func=AF.Copy),
                             reads=[S32.reg], writes=[Sn.reg])

                stageA(0)
                for tq in range(NT):
                    po = psum[tq % 2]
                    pd = psum[2 + tq % 2]
                    tsl = slice(tq * T, (tq + 1) * T)
                    if kind == "ml":
                        E.dma(gt.ap, mlo_s.ap[hd * 128:(hd + 1) * 128, tsl], reads=[mlo_s.reg], writes=[gt.reg])
                    else:
                        E.dma(gt.ap, glr_s.ap[hd * 128:(hd + 1) * 128, tsl], reads=[glr_s.reg], writes=[gt.reg])
                    for ci in range(4):
                        c = tq * 4 + ci
                        if c + 1 < NC_:
                            stageA(c + 1)
                        stageB(c, po, pd)
                    if kind == "ml":
                        E.op("act", lambda e, pd=pd: e.activation(out=t32.ap, in_=pd.ap, func=AF.Abs),
                             reads=[pd.reg], writes=[t32.reg])
                        E.op("dve", lambda e: e.tensor_scalar_max(out=t32.ap, in0=t32.ap, scalar1=1.0),
                             reads=[t32.reg], writes=[t32.reg])
                        E.op("dve", lambda e: e.reciprocal(out=t32.ap, in_=t32.ap), reads=[t32.reg], writes=[t32.reg])
                        E.op("dve", lambda e, po=po: e.tensor_tensor(out=hh32.ap, in0=po.ap, in1=t32.ap, op=ALU.mult),
                             reads=[po.reg, t32.reg], writes=[hh32.reg])
                        E.op("act", lambda e: e.activation(out=zb.ap, in_=hh32.ap, func=AF.Copy), reads=[hh32.reg], writes=[zb.reg])
                        E.op("act", lambda e: e.activation(out=sqb_.ap, in_=hh32.ap, func=AF.Square), reads=[hh32.reg], writes=[sqb_.reg])
                        pm = psum[4]
                        pq = psum[5]
                        mm(pm.ap, ones128.ap, zb.ap, True, True, [ones128.reg, zb.reg], [pm.reg])
                        mm(pq.ap, ones128.ap, sqb_.ap, True, True, [ones128.reg, sqb_.reg], [pq.reg])
                        E.op("act", lambda e, pm=pm: e.activation(out=mean.ap, in_=pm.ap, func=AF.Copy), reads=[pm.reg], writes=[mean.reg])
                        E.op("dve", lambda e: e.tensor_tensor(out=rstd.ap, in0=mean.ap, in1=mean.ap, op=ALU.mult),
                             reads=[mean.reg], writes=[rstd.reg])
                        E.op("dve", lambda e, pq=pq: e.tensor_tensor(out=rstd.ap, in0=pq.ap, in1=rstd.ap, op=ALU.subtract),
                             reads=[pq.reg, rstd.reg], writes=[rstd.reg])
                        rsqrt_inplace(rstd, EPS)
                        E.op("pool", lambda e: e.tensor_tensor(out=hh32.ap, in0=hh32.ap, in1=mean.ap, op=ALU.subtract),
                             reads=[hh32.reg, mean.reg], writes=[hh32.reg])
                        E.op("dve", lambda e: e.tensor_tensor(out=hh32.ap, in0=hh32.ap, in1=rstd.ap, op=ALU.mult),
                             reads=[hh32.reg, rstd.reg], writes=[hh32.reg])
                        E.op("dve", lambda e: e.tensor_tensor(out=yst.ap, in0=hh32.ap, in1=gt.ap, op=ALU.mult),
                             reads=[hh32.reg, gt.reg], writes=[yst.reg])
                        row = hd * 128
                    else:
                        E.op("act", lambda e, po=po: e.activation(out=sqb_.ap, in_=po.ap, func=AF.Square), reads=[po.reg], writes=[sqb_.reg])
                        pq = psum[5]
                        mm(pq.ap, ones128.ap, sqb_.ap, True, True, [ones128.reg, sqb_.reg], [pq.reg])
                        rsqrt_to(rstd, pq.ap, [pq.reg], EPS)
                        E.op("dve", lambda e, po=po: e.tensor_tensor(out=hh32.ap, in0=po.ap, in1=rstd.ap, op=ALU.mult),
                             reads=[po.reg, rstd.reg], writes=[hh32.reg])
                        E.op("dve", lambda e: e.tensor_tensor(out=yst.ap, in0=hh32.ap, in1=gt.ap, op=ALU.mult),
                             reads=[hh32.reg, gt.reg], writes=[yst.reg])
                        row = 512 + hd * 128
                    E.dma(ymix.ap[row:row + 128, tsl], yst.ap, reads=[yst.reg], writes=[ymix.reg], eng="act")
                pump(4)

    def mla_attn(l):
        E.barrier()
        pos = [persist_top]
        NB = S // 128
        scale = 192.0 ** -0.5
        qn = carve(128, [S], BF16, pos)
        qr = carve(65, [S], BF16, pos)
        kn = carve(128, [S], BF16, pos)
        kr = carve(65, [S], BF16, pos)
        vt = carve(128, [NB, 128], BF16, pos)
        sq = [carve(128, [T], BF16, pos) for _ in range(2)]
        qn2 = carve(128, [S], F32, pos)
        kmx = carve(128, [NT], F32, pos)
        kmax = carve(128, [1], F32, pos)
        pb = [carve(128, [T], BF16, pos) for _ in range(4)]
        rl = carve(128, [T], F32, pos)
        yst = [carve(128, [T], BF16, pos) for _ in range(2)]
        E.dma(kr.ap[0:64, :], KR.ap, reads=[KR.reg], writes=[kr.reg])
        E.op("pool", lambda e: e.memset(kr.ap[64:65, :], 1.0), reads=[], writes=[kr.reg])
        for hd in range(8):
            E.dma(qn.ap, QN.ap[hd], reads=[QN.reg], writes=[qn.reg])
            E.dma(qr.ap[0:64, :], QR.ap[hd], reads=[QR.reg], writes=[qr.reg])
            E.dma(kn.ap, KN.ap[hd], reads=[KN.reg], writes=[kn.reg])
            E.dma(vt.ap, V_tm.ap[:, hd * 128:(hd + 1) * 128].rearrange("(c p) d -> p c d", p=128),
                  reads=[V_tm.reg], writes=[vt.reg])
            for ti in range(NT):
                tsl = slice(ti * T, (ti + 1) * T)
                pk = psum[6]
                pq = psum[7]
                a, b2 = sq[0], sq[1]
                E.op("act", lambda e, a=a, tsl=tsl: e.activation(out=a.ap, in_=kn.ap[:, tsl], func=AF.Square), reads=[kn.reg], writes=[a.reg])
                mm(pk.ap, ones_bf.ap, a.ap, True, False, [ones_bf.reg, a.reg], [pk.reg])
                E.op("act", lambda e, b2=b2, tsl=tsl: e.activation(out=b2.ap[0:64, :], in_=kr.ap[0:64, tsl], func=AF.Square), reads=[kr.reg], writes=[b2.reg])
                mm(pk.ap, ones_bf.ap[0:64, :], b2.ap[0:64, :], False, True, [ones_bf.reg, b2.reg], [pk.reg])
                E.op("dve", lambda e, pk=pk, ti=ti: e.reduce_max(out=kmx.ap[:, ti:ti + 1], in_=pk.ap, axis=AX.X), reads=[pk.reg], writes=[kmx.reg])
                E.op("act", lambda e, a=a, tsl=tsl: e.activation(out=a.ap, in_=qn.ap[:, tsl], func=AF.Square), reads=[qn.reg], writes=[a.reg])
                mm(pq.ap, ones_bf.ap, a.ap, True, False, [ones_bf.reg, a.reg], [pq.reg])
                E.op("act", lambda e, b2=b2, tsl=tsl: e.activation(out=b2.ap[0:64, :], in_=qr.ap[0:64, tsl], func=AF.Square), reads=[qr.reg], writes=[b2.reg])
                mm(pq.ap, ones_bf.ap[0:64, :], b2.ap[0:64, :], False, True, [ones_bf.reg, b2.reg], [pq.reg])
                E.op("act", lambda e, pq=pq, tsl=tsl: e.activation(out=qn2.ap[:, tsl], in_=pq.ap, func=AF.Copy), reads=[pq.reg], writes=[qn2.reg])
            E.op("dve", lambda e: e.reduce_max(out=kmax.ap, in_=kmx.ap, axis=AX.X), reads=[kmx.reg], writes=[kmax.reg])
            E.op("dve", lambda e: e.tensor_scalar_mul(out=qn2.ap[64:65, :], in0=qn2.ap[64:65, :], scalar1=kmax.ap[64:65, 0:1]),
                 reads=[qn2.reg, kmax.reg], writes=[qn2.reg])
            E.op("act", lambda e: e.activation(out=qn2.ap[64:65, :], in_=qn2.ap[64:65, :], func=AF.Sqrt), reads=[qn2.reg], writes=[qn2.reg])
            E.op("dve", lambda e: e.tensor_scalar_mul(out=qr.ap[64:65, :], in0=qn2.ap[64:65, :], scalar1=-1.0),
                 reads=[qn2.reg], writes=[qr.reg])
            work = []
            for qt in range(NT):
                nkb = 4 * (qt + 1)
                for kb in range(nkb):
                    work.append((qt, kb, nkb))
            LA = 2

            def stage_qk(i):
                qt, kb, nkb = work[i]
                d_ = kb - 4 * qt
                off = d_ * 128 if d_ > 0 else 0
                qsl = slice(qt * T + off, (qt + 1) * T)
                n = T - off
                ksl = slice(kb * 128, (kb + 1) * 128)
                ps_ = psum[4 + i % 4]
                mm(ps_.ap[:, 0:n], kn.ap[:, ksl], qn.ap[:, qsl], True, False, [kn.reg, qn.reg], [ps_.reg])
                mm(ps_.ap[:, 0:n], kr.ap[:, ksl], qr.ap[:, qsl], False, True, [kr.reg, qr.reg], [ps_.reg])
                p_ = pb[i % 4]
                E.op("act", lambda e, ps_=ps_, p_=p_, n=n: e.activation(out=p_.ap[:, 0:n], in_=ps_.ap[:, 0:n], func=AF.Exp, scale=scale),
                     reads=[ps_.reg], writes=[p_.reg])
                if d_ >= 0:
                    E.op("pool", lambda e, p_=p_: e.tensor_tensor(out=p_.ap[:, 0:128], in0=p_.ap[:, 0:128], in1=tri.ap, op=ALU.mult),
                         reads=[p_.reg, tri.reg], writes=[p_.reg])

            def stage_pv(i, hd=hd):
                qt, kb, nkb = work[i]
                d_ = kb - 4 * qt
                off = d_ * 128 if d_ > 0 else 0
                n = T - off
                po = psum[qt % 2]
                pl = psum[2 + qt % 2]
                p_ = pb[i % 4]
                mm(po.ap[:, off:T], vt.ap[:, kb, :], p_.ap[:, 0:n], kb == 0, kb == nkb - 1, [vt.reg, p_.reg], [po.reg])
                mm(pl.ap[:, off:T], ones_bf.ap, p_.ap[:, 0:n], kb == 0, kb == nkb - 1, [ones_bf.reg, p_.reg], [pl.reg])
                if kb == nkb - 1:
                    E.op("dve", lambda e, pl=pl: e.reciprocal(out=rl.ap, in_=pl.ap), reads=[pl.reg], writes=[rl.reg])
                    y_ = yst[qt % 2]
                    E.op("dve", lambda e, po=po, y_=y_: e.tensor_tensor(out=y_.ap, in0=po.ap, in1=rl.ap, op=ALU.mult),
                         reads=[po.reg, rl.reg], writes=[y_.reg])
                    row = 1024 + hd * 128
                    E.dma(ymix.ap[row:row + 128, qt * T:(qt + 1) * T], y_.ap, reads=[y_.reg], writes=[ymix.reg], eng="act")

            for i in range(min(LA, len(work))):
                stage_qk(i)
            for i in range(len(work)):
                if i + LA < len(work):
                    stage_qk(i + LA)
                stage_pv(i)
            pump(4)

    def outproj(l, U, dst_ap, dst_regs):
        E.barrier()
        pos = [persist_top]
        x32 = carve(128, [NK, T], F32, pos)
        yb = [carve(128, [NK, T], BF16, pos) for _ in range(2)]
        wob = [carve(128, [NK, 128], BF16, pos) for _ in range(4)]
        tmp = ln_tmps(pos)
        wos = Stream(list(U["wo"]) * NT, wob)
        wos.prime()
        yv = ymix.ap.rearrange("(k p) t -> p k t", p=128)
        E.dma(yb[0].ap, yv[:, :, 0:T], reads=[ymix.reg], writes=[yb[0].reg])
        for ti in range(NT):
            load_x_tile(xs.ap, xs_tiles[ti], ti, x32)
            if ti + 1 < NT:
                E.dma(yb[(ti + 1) % 2].ap, yv[:, :, (ti + 1) * T:(ti + 2) * T], reads=[ymix.reg], writes=[yb[(ti + 1) % 2].reg])
            proj_ln(1, yb[ti % 2], NK, wos, x32, tmp, dst_ap, dst_regs[ti], ti)

    for l in range(n_layers):
        Ud = {}
        Ud["f1i"] = wi_units(ffn_wi[0], l, "f1")
        Ud["f1o"] = wo_units(ffn_wo[0], l, "f1")
        Ud["mix"] = mixer_units(l)
        Ud["f2i"] = wi_units(ffn_wi[1], l, "f2")
        Ud["f2o"] = wo_units(ffn_wo[1], l, "f2")
        units.append(Ud)

    rope_done = [False]
    for l in range(n_layers):
        Ud = units[l]
        E.barrier()
        if l == 0:
            pump(4 * NJ + 6 * NK)
        adaln(l, persist_top)
        src_ap, src_regs = (x_in, [Reg() for _ in range(NT)]) if l == 0 else (xs.ap, xs_tiles)
        last = (stop_after == (l, 0))
        ffn(l, 0, Ud["f1i"], Ud["f1o"], src_ap, src_regs,
            y_out if last else xs.ap, out_tiles if last else xs_tiles)
        if last:
            break
        if not rope_done[0]:
            rope_tables()
            rope_done[0] = True
        inproj(l, Ud["mix"])
        linattn(l)
        mla_attn(l)
        last = (stop_after == (l, 1))
        outproj(l, Ud["mix"], y_out if last else xs.ap, out_tiles if last else xs_tiles)
        if last:
            break
        last = (l == n_layers - 1) or (stop_after == (l, 2))
        ffn(l, 2, Ud["f2i"], Ud["f2o"], xs.ap, xs_tiles,
            y_out if last else xs.ap, out_tiles if last else xs_tiles)
        if last:
            break

    E.finish()
    st.close()
    return nc, E


def host_layout(inputs, b, nl=DEPTH):
    f = np.ascontiguousarray
    half = 32
    invf = (10000.0 ** (-np.arange(half, dtype=np.float32) / half)).astype(np.float32)
    m = {
        "x": f(inputs["x"][b].T),
        "c": f(inputs["c"][b].reshape(NK, 128).T),
        "pos": f(inputs["positions"][b].reshape(1, S).astype(np.int32)),
        "invf": f(np.concatenate([invf, invf]).reshape(64, 1)),
        "w_ada": inputs["w_ada"][:nl],
        "b_ada": f(inputs["b_ada"][:nl].reshape(nl, 144, 128).transpose(0, 2, 1)),
        "ln_g": f(inputs["ln_g"][:nl].reshape(nl, 3, NK, 128).transpose(0, 3, 1, 2)),
        "ln_b": f(inputs["ln_b"][:nl].reshape(nl, 3, NK, 128).transpose(0, 3, 1, 2)),
        "ffn1_wi": inputs["ffn1_wi"][:nl], "ffn1_wo": inputs["ffn1_wo"][:nl],
        "ffn2_wi": inputs["ffn2_wi"][:nl], "ffn2_wo": inputs["ffn2_wo"][:nl],
        "w_in": inputs["w_in"][:nl],
        "ml_conv": f(inputs["ml_conv"][:nl].reshape(nl, 4, 8, 128).transpose(0, 3, 2, 1)),
        "ml_bi": f(inputs["ml_bi"][:nl].reshape(nl, 4, 1)),
        "ml_bf": f(inputs["ml_bf"][:nl].reshape(nl, 4, 1)),
        "gla_wg": inputs["gla_wg"][:nl],
        "gla_bg": f(inputs["gla_bg"][:nl].reshape(nl, 4, 64).transpose(0, 2, 1)),
        "mla_gq": f(inputs["mla_gq"][:nl].reshape(nl, 4, 128).transpose(0, 2, 1)),
        "mla_gkv": f(inputs["mla_gkv"][:nl].reshape(nl, 4, 128).transpose(0, 2, 1)),
        "mla_wuq": inputs["mla_wuq"][:nl], "mla_wuk": inputs["mla_wuk"][:nl], "mla_wuv": inputs["mla_wuv"][:nl],
        "w_out": inputs["w_out"][:nl],
    }
    return m


_USED = None


def run(inputs, cores, trace=False, **bk):
    nc, E = build(**bk)
    nl = bk.get('n_layers', DEPTH)
    in_maps = [host_layout(inputs, b, nl) for b in cores]
    if trace:
        res = run_bass_kernel_spmd(nc, in_maps, core_ids=list(range(len(cores))), trace=True)
        print("EXEC_TIME_NS", res.exec_time_ns)
    else:
        res = run_bass_kernel_spmd(nc, in_maps, core_ids=list(range(len(cores))))
    return [np.ascontiguousarray(r["y"].T) for r in res.results]


def kernel(**inputs):
    inputs = {k: np.asarray(v) for k, v in inputs.items()}
    outs = run(inputs, list(range(8)))
    return np.stack(outs, 0).astype(np.float32)
```

```python
import contextlib
import math
import numpy as np
import concourse.bass as bass
import concourse.mybir as mybir
from concourse.bass_utils import run_bass_kernel_spmd

F32 = mybir.dt.float32
BF16 = mybir.dt.bfloat16
I32 = mybir.dt.int32
U8 = mybir.dt.uint8
AF = mybir.ActivationFunctionType
ALU = mybir.AluOpType
AX = mybir.AxisListType

D = 2048
S = 4096
DEPTH = 4
FF = 5632
NK = 16
NJ = 44
T = 512
NT = S // T
ALPHA = (2.0 * DEPTH) ** 0.25
EPS = 1e-5
D_IN = 4696
C_MLQ, C_MLK, C_MLV, C_MLI, C_MLF, C_MLO = 0, 512, 1024, 1536, 1540, 1544
C_GLQ, C_GLK, C_GLV, C_GLLR, C_GLR = 2056, 2312, 2568, 3080, 3096
C_CQ, C_CKV, C_KR = 3608, 4120, 4632


class Reg:
    __slots__ = ("w", "r")

    def __init__(self):
        self.w = None
        self.r = []


class Emitter:
    ENGS = ("pe", "act", "dve", "pool", "sp")

    def __init__(self, nc, n_dma_sems=None):
        self.nc = nc
        self.thunks = {e: [] for e in self.ENGS}
        self.count = {e: 0 for e in self.ENGS}
        self.waited = {e: {} for e in self.ENGS}
        self.sem = {}
        self.dma_sems = []
        self.dma_sem_cnt = []
        self.pool = {"sp": list(range(0, 36)), "act": list(range(36, 64))}
        self.n_dma_sems = 64
        self.dma_rr = {"sp": 0, "act": 0}
        self.ninstr = 0

    def setup_sems(self, stack):
        for e in ("pe", "act", "dve", "pool"):
            self.sem[e] = stack.enter_context(self.nc.semaphore("c_" + e))
        for i in range(self.n_dma_sems):
            self.dma_sems.append(stack.enter_context(self.nc.semaphore("d%d" % i)))
            self.dma_sem_cnt.append(0)

    def _wait(self, eng, tok, force=False):
        kind, key, val = tok
        if kind == "eng" and key == eng and not force:
            return
        k = (kind, key)
        if self.waited[eng].get(k, 0) >= val:
            return
        self.waited[eng][k] = val
        sem = self.sem[key] if kind == "eng" else self.dma_sems[key]
        self.thunks[eng].append(lambda e, sem=sem, val=val: e.wait_ge(sem, val))

    def _deps(self, eng, reads, writes, force=False):
        for r in reads:
            if r.w is not None:
                self._wait(eng, r.w, force)
        for w in writes:
            if w.w is not None:
                self._wait(eng, w.w, force)
            for t in w.r:
                self._wait(eng, t, force)

    def _mark(self, tok, reads, writes):
        for r in reads:
            r.r.append(tok)
            if len(r.r) > 64:
                last = {}
                for t in r.r:
                    k = (t[0], t[1])
                    if k not in last or last[k][2] < t[2]:
                        last[k] = t
                r.r = list(last.values())
        for w in writes:
            w.w = tok
            w.r = []
        self.ninstr += 1

    def op(self, eng, fn, reads=(), writes=()):
        self._deps(eng, reads, writes)
        self.count[eng] += 1
        sem = self.sem[eng]
        self.thunks[eng].append(lambda e, fn=fn, sem=sem: fn(e).then_inc(sem, 1))
        tok = ("eng", eng, self.count[eng])
        self._mark(tok, reads, writes)
        return tok

    def dma(self, out, in_, reads=(), writes=(), eng="sp", **kw):
        pl = self.pool[eng]
        i = pl[self.dma_rr[eng] % len(pl)]
        self.dma_rr[eng] += 1
        if self.dma_sem_cnt[i] > 0:
            self._wait(eng, ("dma", i, self.dma_sem_cnt[i]))
        self._deps(eng, reads, writes, force=True)
        self.dma_sem_cnt[i] += 16
        sem = self.dma_sems[i]
        self.thunks[eng].append(
            lambda e, out=out, in_=in_, sem=sem, kw=kw: e.dma_start(out=out, in_=in_, **kw).then_inc(sem, 16))
        tok = ("dma", i, self.dma_sem_cnt[i])
        self._mark(tok, reads, writes)
        return tok

    def barrier(self):
        toks = [("eng", e, self.count[e]) for e in ("pe", "act", "dve", "pool") if self.count[e] > 0]
        toks += [("dma", i, c) for i, c in enumerate(self.dma_sem_cnt) if c > 0]
        for e in self.ENGS:
            for t in toks:
                self._wait(e, t)

    def finish(self):
        self.barrier()
        nc = self.nc
        th = self.thunks
        with nc.Block() as block:
            @block.sync
            def _(e):
                for t in th["sp"]:
                    t(e)

            @block.tensor
            def _(e):
                for t in th["pe"]:
                    t(e)

            @block.scalar
            def _(e):
                for t in th["act"]:
                    t(e)

            @block.vector
            def _(e):
                for t in th["dve"]:
                    t(e)

            @block.gpsimd
            def _(e):
                for t in th["pool"]:
                    t(e)


class Buf:
    __slots__ = ("ap", "reg")

    def __init__(self, ap, reg=None):
        self.ap = ap
        self.reg = reg if reg is not None else Reg()


def build(n_layers=DEPTH, stop_after=None, dbg=False):
    nc = bass.Bass("TRN2", target_bir_lowering=False)
    st = contextlib.ExitStack()
    E = Emitter(nc)
    E.setup_sems(st)

    def din(name, shape, dt=F32):
        return nc.dram_tensor(name, list(shape), dt, kind="ExternalInput").ap()

    def dscr(name, shape, dt):
        return nc.dram_tensor(name, list(shape), dt, kind="Internal").ap()

    x_in = din("x", [D, S])
    c_in = din("c", [128, NK])
    pos_in = din("pos", [1, S], I32)
    invf_in = din("invf", [64, 1])
    w_ada = din("w_ada", [n_layers, D, 9 * D])
    b_ada = din("b_ada", [n_layers, 128, 144])
    ln_g = din("ln_g", [n_layers, 128, 3, NK])
    ln_b = din("ln_b", [n_layers, 128, 3, NK])
    ffn_wi = [din("ffn1_wi", [n_layers, D, 2 * FF]), din("ffn2_wi", [n_layers, D, 2 * FF])]
    ffn_wo = [din("ffn1_wo", [n_layers, FF, D]), din("ffn2_wo", [n_layers, FF, D])]
    w_in = din("w_in", [n_layers, D, D_IN])
    ml_conv = din("ml_conv", [n_layers, 128, 8, 4])
    ml_bi = din("ml_bi", [n_layers, 4, 1])
    ml_bf = din("ml_bf", [n_layers, 4, 1])
    gla_wg = din("gla_wg", [n_layers, 16, 256])
    gla_bg = din("gla_bg", [n_layers, 64, 4])
    mla_gq = din("mla_gq", [n_layers, 128, 4])
    mla_gkv = din("mla_gkv", [n_layers, 128, 4])
    mla_wuq = din("mla_wuq", [n_layers, 512, 1536])
    mla_wuk = din("mla_wuk", [n_layers, 512, 1024])
    mla_wuv = din("mla_wuv", [n_layers, 512, 1024])
    w_out = din("w_out", [n_layers, D, D])
    y_out = nc.dram_tensor("y", [D, S], F32, kind="ExternalOutput").ap()

    xs = Buf(dscr("xs", [D, S], F32))
    xs_tiles = [Reg() for _ in range(NT)]
    out_tiles = [Reg() for _ in range(NT)]

    ARENA = 210000
    arena = nc.alloc_sbuf_tensor("arena", [128, ARENA], U8)
    top = [0]

    def carve(nparts, free_shape, dt, pos=None):
        n = int(np.prod(free_shape))
        nbytes = n * (4 if dt in (F32, I32) else 2)
        if pos is None:
            off = top[0]
            top[0] = off + (nbytes + 63) // 64 * 64
            assert top[0] <= ARENA, ("sbuf overflow", top[0])
        else:
            off = pos[0]
            pos[0] = off + (nbytes + 63) // 64 * 64
            assert pos[0] <= ARENA, ("sbuf overflow (phase)", pos[0])
        ap = arena[0:nparts, off:off + nbytes].bitcast(dt)
        if len(free_shape) == 2:
            ap = ap.rearrange("p (a b) -> p a b", a=free_shape[0])
        elif len(free_shape) == 3:
            ap = ap.rearrange("p (a b c) -> p a b c", a=free_shape[0], b=free_shape[1])
        return Buf(ap)

    psum = [Buf(st.enter_context(nc.psum_tensor("ps%d" % i, [128, 512], F32))[:]) for i in range(8)]

    ones_bf = carve(128, [128], BF16)
    onesD = carve(128, [128], BF16)
    ones512 = carve(128, [128], BF16)
    ones128 = carve(128, [128], BF16)
    tri = carve(128, [128], BF16)
    ident = carve(128, [128], BF16)
    cact = carve(128, [NK], F32)
    modsb = carve(128, [144], F32)
    sc1 = carve(128, [3, NK], F32)
    gz = carve(128, [3, NK], F32)
    lng = carve(128, [3, NK], F32)
    lnb = carve(128, [3, NK], F32)
    tmp144 = carve(128, [144], F32)
    iot_i = carve(128, [128], I32)
    iot_f = carve(128, [128], F32)
    negpi = carve(128, [1], F32)
    CV_F = 1024
    cv_ld = [carve(128, [CV_F], F32) for _ in range(3)]
    cv_st = [carve(128, [CV_F], BF16) for _ in range(3)]
    persist_top = top[0]

    def mm(out, lhsT, rhs, start, stop, reads, writes):
        E.op("pe", lambda e: e.matmul(out, lhsT=lhsT, rhs=rhs, start=start, stop=stop),
             reads=reads, writes=writes)

    def rsqrt_inplace(b, eps, ap=None):
        a = b.ap if ap is None else ap
        E.op("act", lambda e: e.activation(out=a, in_=a, func=AF.Sqrt, bias=float(eps)),
             reads=[b.reg], writes=[b.reg])
        E.op("dve", lambda e: e.reciprocal(out=a, in_=a), reads=[b.reg], writes=[b.reg])

    E.op("pool", lambda e: e.memset(ones_bf.ap, 1.0), writes=[ones_bf.reg])
    E.op("pool", lambda e: e.memset(onesD.ap, 1.0 / D), writes=[onesD.reg])
    E.op("pool", lambda e: e.memset(ones512.ap, 1.0 / 512), writes=[ones512.reg])
    E.op("pool", lambda e: e.memset(ones128.ap, 1.0 / 128), writes=[ones128.reg])
    E.op("pool", lambda e: e.memset(negpi.ap, -math.pi), writes=[negpi.reg])
    E.op("pool", lambda e: e.iota(iot_i.ap, pattern=[[1, 128]], base=0, channel_multiplier=-1),
         writes=[iot_i.reg])
    E.op("dve", lambda e: e.tensor_copy(out=iot_f.ap, in_=iot_i.ap), reads=[iot_i.reg], writes=[iot_f.reg])
    E.op("dve", lambda e: e.tensor_single_scalar(out=tri.ap, in_=iot_f.ap, scalar=0.0, op=ALU.is_ge),
         reads=[iot_f.reg], writes=[tri.reg])
    E.op("dve", lambda e: e.tensor_single_scalar(out=ident.ap, in_=iot_f.ap, scalar=0.0, op=ALU.is_equal),
         reads=[iot_f.reg], writes=[ident.reg])
    E.dma(cact.ap, c_in, writes=[cact.reg])
    E.op("act", lambda e: e.activation(out=cact.ap, in_=cact.ap, func=AF.Silu), reads=[cact.reg], writes=[cact.reg])

    class Unit:
        __slots__ = ("dst", "regs", "jobs", "nemit", "shape")

    cv_jobs = []
    cv_state = {"i": 0, "n": 0}
    cv_pending = []

    def new_unit(name, free_shape, srcs):
        u = Unit()
        n = int(np.prod(free_shape))
        u.dst = dscr(name, [128, n], BF16)
        u.shape = list(free_shape)
        u.nemit = 0
        u.jobs = []
        u.regs = []
        dst3 = u.dst.rearrange("p (a b) -> p a b", a=free_shape[0])
        for src, sel in srcs:
            u.jobs.append((src, sel(dst3)))
            u.regs.append(Reg())
            cv_jobs.append((u, len(u.jobs) - 1))
        return u

    def flush_cv(keep=0):
        while len(cv_pending) > keep:
            dst, sb, sreg, jreg = cv_pending.pop(0)
            E.dma(dst, sb, reads=[sreg], writes=[jreg], eng="act")

    def emit_job(u, ji):
        assert ji == u.nemit
        u.nemit += 1
        src, dst = u.jobs[ji]
        i = cv_state["n"] % len(cv_ld)
        cv_state["n"] += 1
        shp = src.shape
        n = int(np.prod(shp[1:]))
        assert n <= CV_F, n
        ld = cv_ld[i].ap[:, 0:n].rearrange("p (a b) -> p a b", a=shp[1])
        sb = cv_st[i].ap[:, 0:n].rearrange("p (a b) -> p a b", a=shp[1])
        E.dma(ld, src, writes=[cv_ld[i].reg], eng="act")
        flush_cv(keep=1)
        E.op("pool", lambda e, ld=ld, sb=sb: e.tensor_copy(out=sb, in_=ld),
             reads=[cv_ld[i].reg], writes=[cv_st[i].reg])
        cv_pending.append((dst, sb, cv_st[i].reg, u.regs[ji]))

    def pump(n=1):
        while n > 0 and cv_state["i"] < len(cv_jobs):
            u, ji = cv_jobs[cv_state["i"]]
            cv_state["i"] += 1
            if ji >= u.nemit:
                emit_job(u, ji)
                n -= 1

    def load_unit(u, buf_ap, buf_reg):
        if u.nemit < len(u.jobs):
            while u.nemit < len(u.jobs):
                emit_job(u, u.nemit)
        if any(p[3] in u.regs for p in cv_pending):
            flush_cv()
        src = u.dst.rearrange("p (a b) -> p a b", a=u.shape[0])
        E.dma(buf_ap, src, reads=u.regs, writes=[buf_reg])

    class Stream:
        def __init__(self, seq, bufs, depth=None):
            self.seq = seq
            self.bufs = bufs
            self.depth = len(bufs) - 1 if depth is None else depth
            self.il = 0
            self.iu = 0

        def _view(self, k):
            u = self.seq[k]
            b = self.bufs[k % len(self.bufs)]
            n = int(np.prod(u.shape))
            flat = b.ap
            if len(flat.shape) == 3:
                flat = flat.rearrange("p a b -> p (a b)")
            return b, flat[:, 0:n].rearrange("p (a b) -> p a b", a=u.shape[0])

        def _fill(self):
            while self.il < len(self.seq) and self.il <= self.iu + self.depth:
                b, v = self._view(self.il)
                load_unit(self.seq[self.il], v, b.reg)
                self.il += 1

        def prime(self):
            self._fill()

        def get(self):
            self._fill()
            b, v = self._view(self.iu)
            self.iu += 1
            return b, v

    def wi_units(w, l, tag):
        wv = w[l].rearrange("(k p) n -> p k n", p=128)
        us = []
        for j in range(NJ):
            srcs = []
            for k0 in (0, 8):
                srcs.append((wv[:, k0:k0 + 8, j * 128:(j + 1) * 128], lambda d, k0=k0: d[:, k0:k0 + 8, 0:128]))
                srcs.append((wv[:, k0:k0 + 8, FF + j * 128:FF + (j + 1) * 128], lambda d, k0=k0: d[:, k0:k0 + 8, 128:256]))
            us.append(new_unit("%s_wi%d_%d" % (tag, l, j), [NK, 256], srcs))
        return us

    def wo_units(w, l, tag):
        wv = w[l].rearrange("(k p) n -> p k n", p=128)
        us = []
        for m in range(NK):
            srcs = []
            for k0, k1 in ((0, 8), (8, 16), (16, 24), (24, 32), (32, 40), (40, 44)):
                srcs.append((wv[:, k0:k1, m * 128:(m + 1) * 128], lambda d, k0=k0, k1=k1: d[:, k0:k1, :]))
            us.append(new_unit("%s_wo%d_%d" % (tag, l, m), [NJ, 128], srcs))
        return us

    def colunit(w2d, name, c0, width, nk=NK):
        wv = w2d.rearrange("(k p) n -> p k n", p=128)
        kp = max(1, CV_F // width)
        srcs = []
        for k0 in range(0, nk, kp):
            k1 = min(nk, k0 + kp)
            srcs.append((wv[:, k0:k1, c0:c0 + width], lambda d, k0=k0, k1=k1: d[:, k0:k1, :]))
        return new_unit(name, [nk, width], srcs)

    units = []
    _mix_decl = []

    def adaln(l, pos0):
        pos = [pos0]
        NB_ = 4
        wbuf = [carve(128, [NK, 128], F32, pos) for _ in range(NB_)]
        bsb = carve(128, [144], F32, pos)
        E.dma(bsb.ap, b_ada[l], writes=[bsb.reg])
        E.dma(lng.ap, ln_g[l], writes=[lng.reg])
        E.dma(lnb.ap, ln_b[l], writes=[lnb.reg])
        wv = w_ada[l].rearrange("(k p) n -> p k n", p=128)
        ps = psum[0]

        def ld(j):
            wb = wbuf[j % NB_]
            E.dma(wb.ap, wv[:, :, j * 128:(j + 1) * 128], writes=[wb.reg])
        for j in range(NB_ - 1):
            ld(j)
        for j in range(144):
            if j + NB_ - 1 < 144:
                ld(j + NB_ - 1)
            wb = wbuf[j % NB_]
            for k in range(NK):
                mm(ps.ap[:, j:j + 1], wb.ap[:, k, :], cact.ap[:, k:k + 1], k == 0, k == NK - 1,
                   [wb.reg, cact.reg], [ps.reg])
        E.op("dve", lambda e: e.tensor_tensor(out=modsb.ap, in0=ps.ap[:, 0:144], in1=bsb.ap, op=ALU.add),
             reads=[ps.reg, bsb.reg], writes=[modsb.reg])
        m4 = modsb.ap.rearrange("p (s w k) -> p s w k", s=3, w=3)
        E.op("dve", lambda e: e.tensor_scalar_add(out=sc1.ap, in0=m4[:, :, 1, :], scalar1=1.0),
             reads=[modsb.reg], writes=[sc1.reg])
        for s_ in range(3):
            rw = (0.5 if s_ != 1 else 1.0) / ALPHA
            E.op("dve", lambda e, s_=s_, rw=rw: e.tensor_scalar(
                out=gz.ap[:, s_, :], in0=m4[:, s_, 2, :], scalar1=1.0, scalar2=rw, op0=ALU.add, op1=ALU.mult),
                reads=[modsb.reg], writes=[gz.reg])

    def shift_ap(s_):
        return modsb.ap.rearrange("p (s w k) -> p s w k", s=3, w=3)[:, s_, 0, :]

    def load_x_tile(src_ap, src_reg, ti, x32):
        v = src_ap.rearrange("(k p) t -> p k t", p=128)[:, :, ti * T:(ti + 1) * T]
        E.dma(x32.ap, v, reads=[src_reg], writes=[x32.reg])

    def modulate(s_, x32, u):
        sh = shift_ap(s_)
        for k in range(NK):
            E.op("dve", lambda e, k=k: e.tensor_scalar(
                out=u.ap[:, k, :], in0=x32.ap[:, k, :], scalar1=sc1.ap[:, s_, k:k + 1], scalar2=sh[:, k:k + 1],
                op0=ALU.mult, op1=ALU.add), reads=[x32.reg, sc1.reg, modsb.reg], writes=[u.reg])

    def proj_ln(s_, hbuf, nkk, wstream, x32, tmp, dst_ap, dst_reg, ti, xres=None):
        ps_y = [psum[4], psum[5]]
        ps_mu, ps_sq = psum[6], psum[7]
        zb, sqb = tmp["zb"], tmp["sqb"]

        def stats(m):
            z = zb[m % 2]
            q = sqb[m % 2]
            mm(ps_mu.ap, onesD.ap, z.ap, m == 0, m == NK - 1, [onesD.reg, z.reg], [ps_mu.reg])
            mm(ps_sq.ap, onesD.ap, q.ap, m == 0, m == NK - 1, [onesD.reg, q.reg], [ps_sq.reg])

        for m in range(NK):
            wb, wap = wstream.get()
            py = ps_y[m % 2]
            for kk in range(nkk):
                mm(py.ap, wap[:, kk, :], hbuf.ap[:, kk, :], kk == 0, kk == nkk - 1,
                   [wb.reg, hbuf.reg], [py.reg])
            if m >= 1:
                stats(m - 1)
            if xres is None:
                E.op("dve", lambda e, m=m, py=py: e.scalar_tensor_tensor(
                    out=x32.ap[:, m, :], in0=py.ap, scalar=gz.ap[:, s_, m:m + 1], in1=x32.ap[:, m, :],
                    op0=ALU.mult, op1=ALU.add), reads=[py.reg, gz.reg, x32.reg], writes=[x32.reg])
            else:
                xr_ = xres(m)
                E.op("dve", lambda e, m=m, py=py, xr_=xr_: e.scalar_tensor_tensor(
                    out=x32.ap[:, m, :], in0=py.ap, scalar=gz.ap[:, s_, m:m + 1], in1=xr_.ap,
                    op0=ALU.mult, op1=ALU.add), reads=[py.reg, gz.reg, xr_.reg], writes=[x32.reg])
            pump(1)
            z = zb[m % 2]
            q = sqb[m % 2]
            E.op("act", lambda e, m=m, z=z: e.activation(out=z.ap, in_=x32.ap[:, m, :], func=AF.Copy),
                 reads=[x32.reg], writes=[z.reg])
            E.op("act", lambda e, m=m, q=q: e.activation(out=q.ap, in_=x32.ap[:, m, :], func=AF.Square),
                 reads=[x32.reg], writes=[q.reg])
        stats(NK - 1)
        mean, rstd = tmp["mean"], tmp["rstd"]
        E.op("act", lambda e: e.activation(out=mean.ap, in_=ps_mu.ap, func=AF.Copy),
             reads=[ps_mu.reg], writes=[mean.reg])
        E.op("dve", lambda e: e.tensor_tensor(out=rstd.ap, in0=mean.ap, in1=mean.ap, op=ALU.mult),
             reads=[mean.reg], writes=[rstd.reg])
        E.op("dve", lambda e: e.tensor_tensor(out=rstd.ap, in0=ps_sq.ap, in1=rstd.ap, op=ALU.subtract),
             reads=[ps_sq.reg, rstd.reg], writes=[rstd.reg])
        rsqrt_inplace(rstd, EPS / (ALPHA * ALPHA))
        t1 = tmp["t1"]
        for k in range(NK):
            a = t1[k % 2]
            E.op("pool", lambda e, k=k, a=a: e.tensor_tensor(out=a.ap, in0=x32.ap[:, k, :], in1=mean.ap,
                                                           op=ALU.subtract),
                 reads=[x32.reg, mean.reg], writes=[a.reg])
            E.op("dve", lambda e, k=k, a=a: e.tensor_tensor(out=a.ap, in0=a.ap, in1=rstd.ap, op=ALU.mult),
                 reads=[a.reg, rstd.reg], writes=[a.reg])
            E.op("act", lambda e, k=k, a=a: e.activation(
                out=x32.ap[:, k, :], in_=a.ap, func=AF.Identity, scale=lng.ap[:, s_, k:k + 1],
                bias=lnb.ap[:, s_, k:k + 1]), reads=[a.reg, lng.reg, lnb.reg], writes=[x32.reg])
        v = dst_ap.rearrange("(k p) t -> p k t", p=128)[:, :, ti * T:(ti + 1) * T]
        E.dma(v, x32.ap, reads=[x32.reg], writes=[dst_reg], eng="act")

    def ln_tmps(pos):
        return {
            "zb": [carve(128, [T], BF16, pos) for _ in range(2)],
            "sqb": [carve(128, [T], BF16, pos) for _ in range(2)],
            "mean": carve(128, [T], F32, pos),
            "rstd": carve(128, [T], F32, pos),
            "t1": [carve(128, [T], F32, pos) for _ in range(2)],
        }

    def ffn(l, s_, wiu, wou, src_ap, src_regs, dst_ap, dst_regs):
        E.barrier()
        pos = [persist_top]
        x32 = carve(128, [NK, T], F32, pos)
        u = [carve(128, [NK, T], BF16, pos) for _ in range(2)]
        h = carve(128, [NJ, T], BF16, pos)
        wib = [carve(128, [NK, 256], BF16, pos) for _ in range(3)]
        wob = [carve(128, [NJ, 128], BF16, pos) for _ in range(2)]
        sg = [carve(128, [T], F32, pos) for _ in range(2)]
        xst = [carve(128, [T], F32, pos) for _ in range(2)]
        xrs = [carve(128, [T], F32, pos) for _ in range(2)]
        tmp = ln_tmps(pos)
        wis = Stream(list(wiu) * NT, wib)
        wos = Stream(list(wou) * NT, wob)
        wis.prime()
        srcv = src_ap.rearrange("(k p) t -> p k t", p=128)
        sh = shift_ap(s_)
        cnt = {"st": 0, "xr": 0}

        def prep_u(ti):
            ub = u[ti % 2]
            for k in range(NK):
                xb = xst[cnt["st"] % 2]
                cnt["st"] += 1
                E.dma(xb.ap, srcv[:, k, ti * T:(ti + 1) * T], reads=[src_regs[ti]], writes=[xb.reg])
                E.op("dve", lambda e, k=k, xb=xb, ub=ub: e.tensor_scalar(
                    out=ub.ap[:, k, :], in0=xb.ap, scalar1=sc1.ap[:, s_, k:k + 1], scalar2=sh[:, k:k + 1],
                    op0=ALU.mult, op1=ALU.add), reads=[xb.reg, sc1.reg, modsb.reg], writes=[ub.reg])

        prep_u(0)
        for ti in range(NT):
            ub = u[ti % 2]
            pend = {}

            def issue_xr(m, ti=ti, pend=pend):
                xb = xrs[cnt["xr"] % 2]
                cnt["xr"] += 1
                E.dma(xb.ap, srcv[:, m, ti * T:(ti + 1) * T], reads=[src_regs[ti]], writes=[xb.reg])
                pend[m] = xb

            def xres(m, pend=pend, issue_xr=issue_xr):
                if m not in pend:
                    issue_xr(m)
                if m + 1 < NK and (m + 1) not in pend:
                    issue_xr(m + 1)
                b = pend.pop(m)
                return b

            for j in range(NJ):
                wb, wap = wis.get()
                if j == 2:
                    wos.prime()
                pg, pu = psum[j % 2], psum[2 + j % 2]
                for k in range(NK):
                    mm(pg.ap, wap[:, k, 0:128], ub.ap[:, k, :], k == 0, k == NK - 1, [wb.reg, ub.reg], [pg.reg])
                for k in range(NK):
                    mm(pu.ap, wap[:, k, 128:256], ub.ap[:, k, :], k == 0, k == NK - 1, [wb.reg, ub.reg], [pu.reg])
                g_ = sg[j % 2]
                E.op("act", lambda e, pg=pg, g_=g_: e.activation(out=g_.ap, in_=pg.ap, func=AF.Silu),
                     reads=[pg.reg], writes=[g_.reg])
                E.op("dve", lambda e, pu=pu, g_=g_, j=j: e.tensor_tensor(
                    out=h.ap[:, j, :], in0=pu.ap, in1=g_.ap, op=ALU.mult),
                    reads=[pu.reg, g_.reg], writes=[h.reg])
                pump(1)
                if j == 8 and ti + 1 < NT:
                    prep_u(ti + 1)
            issue_xr(0)
            proj_ln(s_, h, NJ, wos, x32, tmp, dst_ap, dst_regs[ti], ti, xres=xres)

    mlq32 = Buf(dscr("mlq32", [512, S], F32)); mlk32 = Buf(dscr("mlk32", [512, S], F32))
    gate_i = Buf(dscr("gate_i", [4, S], F32)); gate_f = Buf(dscr("gate_f", [4, S], F32))
    mlv_tm = Buf(dscr("mlv_tm", [S, 512], BF16)); mlo_s = Buf(dscr("mlo_s", [512, S], BF16))
    glq32 = Buf(dscr("glq32", [4, 64, S], F32)); glk32 = Buf(dscr("glk32", [4, 64, S], F32))
    gla_neg = Buf(dscr("gla_neg", [4, 64, S], F32))
    glv_tm = Buf(dscr("glv_tm", [S, 512], BF16)); glr_s = Buf(dscr("glr_s", [512, S], BF16))
    QN = Buf(dscr("QN", [8, 128, S], BF16)); QR = Buf(dscr("QR", [8, 64, S], BF16))
    KN = Buf(dscr("KN", [8, 128, S], BF16)); KR = Buf(dscr("KR", [64, S], BF16))
    V_tm = Buf(dscr("V_tm", [S, 1024], BF16))
    ymix = Buf(dscr("ymix", [D, S], BF16))
    cos_s = Buf(dscr("cos_s", [64, S], F32)); sin_s = Buf(dscr("sin_s", [64, S], F32))

    def mixer_units(l):
        U = {}
        wi2 = w_in[l]
        wv = wi2.rearrange("(k p) n -> p k n", p=128)
        for nm, c0 in (("mlq0", C_MLQ), ("mlq1", C_MLQ + 256), ("mlk0", C_MLK), ("mlk1", C_MLK + 256),
                       ("mlo0", C_MLO), ("mlo1", C_MLO + 256), ("glr0", C_GLR), ("glr1", C_GLR + 256),
                       ("cq0", C_CQ), ("cq1", C_CQ + 256), ("ckv0", C_CKV), ("ckv1", C_CKV + 256),
                       ("glq", C_GLQ), ("glk", C_GLK),
                       ("mlv0", C_MLV), ("mlv1", C_MLV + 256), ("glv0", C_GLV), ("glv1", C_GLV + 256)):
            U[nm] = colunit(wi2, "u%d_%s" % (l, nm), c0, 256)
        U["small"] = new_unit("u%d_small" % l, [NK, 152], [
            (wv[:, :, C_KR:C_KR + 64], lambda d: d[:, :, 0:64]),
            (wv[:, :, C_KR + 32:C_KR + 64], lambda d: d[:, :, 64:96]),
            (wv[:, :, C_KR:C_KR + 32], lambda d: d[:, :, 96:128]),
            (wv[:, :, C_GLLR:C_GLLR + 16], lambda d: d[:, :, 128:144]),
            (wv[:, :, C_MLI:C_MLI + 8], lambda d: d[:, :, 144:152])])
        wq = mla_wuq[l].rearrange("(k p) n -> p k n", p=128)
        for g in range(4):
            U["qn%d" % g] = new_unit("u%d_qn%d" % (l, g), [4, 256], [
                (wq[:, :, (2 * g) * 192:(2 * g) * 192 + 128], lambda d: d[:, :, 0:128]),
                (wq[:, :, (2 * g + 1) * 192:(2 * g + 1) * 192 + 128], lambda d: d[:, :, 128:256])])
            srcs = []
            for hh in range(2):
                b0 = (2 * g + hh) * 192 + 128
                o = hh * 128
                srcs.append((wq[:, :, b0:b0 + 64], lambda d, o=o: d[:, :, o:o + 64]))
                srcs.append((wq[:, :, b0 + 32:b0 + 64], lambda d, o=o: d[:, :, o + 64:o + 96]))
                srcs.append((wq[:, :, b0:b0 + 32], lambda d, o=o: d[:, :, o + 96:o + 128]))
            U["qr%d" % g] = new_unit("u%d_qr%d" % (l, g), [4, 256], srcs)
            U["kn%d" % g] = colunit(mla_wuk[l], "u%d_kn%d" % (l, g), g * 256, 256, nk=4)
            U["vv%d" % g] = colunit(mla_wuv[l], "u%d_vv%d" % (l, g), g * 256, 256, nk=4)
        U["wo"] = [colunit(w_out[l], "u%d_wout%d" % (l, m), m * 128, 128) for m in range(NK)]
        return U

    def rsqrt_to(outb, in_ap, in_regs, eps, out_ap=None):
        o = outb.ap if out_ap is None else out_ap
        E.op("act", lambda e: e.activation(out=o, in_=in_ap, func=AF.Sqrt, bias=float(eps)),
             reads=in_regs, writes=[outb.reg])
        E.op("dve", lambda e: e.reciprocal(out=o, in_=o), reads=[outb.reg], writes=[outb.reg])

    def rope_tables():
        E.barrier()
        pos = [persist_top]
        pi_ = carve(64, [S], I32, pos)
        pf = carve(64, [S], F32, pos)
        ang = carve(64, [S], F32, pos)
        res = carve(64, [S], F32, pos)
        invf = carve(64, [1], F32, pos)
        E.dma(invf.ap, invf_in, writes=[invf.reg])
        E.dma(pi_.ap, pos_in.partition_broadcast(64), writes=[pi_.reg])
        E.op("dve", lambda e: e.tensor_copy(out=pf.ap, in_=pi_.ap), reads=[pi_.reg], writes=[pf.reg])
        E.op("dve", lambda e: e.tensor_scalar_mul(out=pf.ap, in0=pf.ap, scalar1=invf.ap[:, 0:1]),
             reads=[pf.reg, invf.reg], writes=[pf.reg])
        twopi = 2.0 * math.pi
        ki = carve(64, [S], I32, pos)
        kf = carve(64, [S], F32, pos)

        def reduce_sin(shift, negate_lo, dstb):
            E.op("dve", lambda e: e.tensor_scalar(out=kf.ap, in0=pf.ap, scalar1=float(shift), scalar2=1.0 / twopi,
                                                  op0=ALU.add, op1=ALU.mult), reads=[pf.reg], writes=[kf.reg])
            E.op("dve", lambda e: e.tensor_copy(out=ki.ap, in_=kf.ap), reads=[kf.reg], writes=[ki.reg])
            E.op("dve", lambda e: e.tensor_copy(out=kf.ap, in_=ki.ap), reads=[ki.reg], writes=[kf.reg])
            E.op("dve", lambda e: e.tensor_scalar_add(out=ang.ap, in0=pf.ap, scalar1=float(shift)),
                 reads=[pf.reg, res.reg], writes=[ang.reg])
            E.op("dve", lambda e: e.scalar_tensor_tensor(out=ang.ap, in0=kf.ap, scalar=-twopi, in1=ang.ap,
                                                         op0=ALU.mult, op1=ALU.add), reads=[kf.reg, ang.reg], writes=[ang.reg])
            E.op("dve", lambda e: e.tensor_single_scalar(out=kf.ap, in_=ang.ap, scalar=math.pi, op=ALU.is_gt),
                 reads=[ang.reg], writes=[kf.reg])
            E.op("dve", lambda e: e.scalar_tensor_tensor(out=ang.ap, in0=kf.ap, scalar=-twopi, in1=ang.ap,
                                                         op0=ALU.mult, op1=ALU.add), reads=[kf.reg, ang.reg], writes=[ang.reg])
            E.op("dve", lambda e: e.tensor_single_scalar(out=kf.ap, in_=ang.ap, scalar=-math.pi, op=ALU.is_lt),
                 reads=[ang.reg], writes=[kf.reg])
            E.op("dve", lambda e: e.scalar_tensor_tensor(out=ang.ap, in0=kf.ap, scalar=twopi, in1=ang.ap,
                                                         op0=ALU.mult, op1=ALU.add), reads=[kf.reg, ang.reg], writes=[ang.reg])
            E.op("dve", lambda e: e.tensor_scalar(out=ang.ap, in0=ang.ap, scalar1=-3.1415925, scalar2=3.1415925,
                                                  op0=ALU.max, op1=ALU.min), reads=[ang.reg], writes=[ang.reg])
            E.op("act", lambda e: e.activation(out=res.ap, in_=ang.ap, func=AF.Sin), reads=[ang.reg, dstb.reg], writes=[res.reg])
            if negate_lo:
                E.op("dve", lambda e: e.tensor_scalar_mul(out=res.ap[0:32, :], in0=res.ap[0:32, :], scalar1=-1.0),
                     reads=[res.reg], writes=[res.reg])
            E.dma(dstb.ap, res.ap, reads=[res.reg], writes=[dstb.reg])

        reduce_sin(0.0, True, sin_s)
        reduce_sin(0.5 * math.pi, False, cos_s)

    def inproj(l, U):
        E.barrier()
        pos = [persist_top]
        x32 = carve(128, [NK, T], F32, pos)
        ubufs = [carve(128, [NK, T], BF16, pos) for _ in range(2)]
        wb = [carve(128, [NK, 256], BF16, pos) for _ in range(4)]
        conv = carve(128, [8, 4], F32, pos)
        halo = carve(128, [8, 3], F32, pos)
        bi = carve(4, [1], F32, pos); nbf = carve(4, [1], F32, pos)
        wg = carve(16, [256], F32, pos); nbg = carve(64, [4], F32, pos)
        gq = carve(128, [4], F32, pos); gkv = carve(128, [4], F32, pos)
        pre = [carve(128, [T + 3], F32, pos) for _ in range(2)]
        acc = [carve(128, [T], F32, pos) for _ in range(2)]
        stg32 = [carve(128, [T], F32, pos) for _ in range(3)]
        stg16 = [carve(128, [T], BF16, pos) for _ in range(3)]
        lr32 = carve(16, [T], F32, pos)
        cq32 = carve(128, [4, T], F32, pos)
        cqn = carve(128, [4, T], BF16, pos)
        ckvn = carve(128, [4, T], BF16, pos)
        sqb = [carve(128, [T], BF16, pos) for _ in range(2)]
        rstd = carve(128, [T], F32, pos)
        cs = carve(64, [T], F32, pos); sn = carve(64, [T], F32, pos)
        rt1 = [carve(64, [T], F32, pos) for _ in range(2)]
        rt2 = [carve(64, [T], F32, pos) for _ in range(2)]
        cnt = {"ps": 0, "s32": 0, "s16": 0, "pre": 0, "rt": 0}

        E.dma(conv.ap, ml_conv[l], writes=[conv.reg])
        E.dma(bi.ap, ml_bi[l], writes=[bi.reg])
        E.dma(nbf.ap, ml_bf[l], writes=[nbf.reg])
        E.op("dve", lambda e: e.tensor_scalar_mul(out=nbf.ap, in0=nbf.ap, scalar1=-1.0), reads=[nbf.reg], writes=[nbf.reg])
        E.dma(wg.ap, gla_wg[l], writes=[wg.reg])
        E.dma(nbg.ap, gla_bg[l], writes=[nbg.reg])
        E.op("dve", lambda e: e.tensor_scalar_mul(out=nbg.ap, in0=nbg.ap, scalar1=-1.0), reads=[nbg.reg], writes=[nbg.reg])
        E.dma(gq.ap, mla_gq[l], writes=[gq.reg])
        E.dma(gkv.ap, mla_gkv[l], writes=[gkv.reg])
        E.op("pool", lambda e: e.memset(halo.ap, 0.0), writes=[halo.reg])

        def nxt(key, n):
            v = cnt[key] % n
            cnt[key] += 1
            return v

        order = (["mlq0", "mlq1", "mlk0", "mlk1", "mlo0", "mlo1", "glr0", "glr1", "glq", "glk", "small",
                  "mlv0", "mlv1", "glv0", "glv1", "cq0", "cq1", "ckv0", "ckv1"]
                 + [n_ % g for g in range(4) for n_ in ("qn%d", "qr%d", "kn%d", "vv%d")])
        ws = Stream([U[n_] for n_ in order] * NT, wb)
        ws.prime()

        def getw(unit, nk=NK):
            b, v = ws.get()
            assert ws.seq[ws.iu - 1] is unit
            return b, v

        def proj_fm(wap, wreg, c0, w, rhs, rreg, nk=NK, tcols=None):
            ps = psum[nxt("ps", 4)]
            for k in range(nk):
                mm(ps.ap[0:w, :], wap[:, k, c0:c0 + w], rhs.ap[:, k, :], k == 0, k == nk - 1,
                   [wreg, rreg], [ps.reg])
            return ps

        def store(dst_ap, dst_reg, sb, sb_ap):
            E.dma(dst_ap, sb_ap, reads=[sb.reg], writes=[dst_reg], eng="act")

        for ti in range(NT):
            tsl = slice(ti * T, (ti + 1) * T)
            u = ubufs[ti % 2]
            if ti == 0:
                load_x_tile(xs.ap, xs_tiles[0], 0, x32)
                modulate(1, x32, u)
            if ti + 1 < NT:
                load_x_tile(xs.ap, xs_tiles[ti + 1], ti + 1, x32)
            E.dma(cs.ap, cos_s.ap[:, tsl], reads=[cos_s.reg], writes=[cs.reg])
            E.dma(sn.ap, sin_s.ap[:, tsl], reads=[sin_s.reg], writes=[sn.reg])

            for gi, (nm, dstb) in enumerate((("mlq0", mlq32), ("mlq1", mlq32), ("mlk0", mlk32), ("mlk1", mlk32))):
                b, wap = getw(U[nm])
                for sub in range(2):
                    ch = gi * 2 + sub
                    ps = proj_fm(wap, b.reg, sub * 128, 128, u, u.reg)
                    p_ = pre[nxt("pre", 2)]
                    a_ = acc[cnt["pre"] % 2]
                    E.op("act", lambda e, p_=p_, ps=ps: e.activation(out=p_.ap[:, 3:T + 3], in_=ps.ap, func=AF.Copy),
                         reads=[ps.reg], writes=[p_.reg])
                    E.op("pool", lambda e, p_=p_, ch=ch: e.tensor_copy(out=p_.ap[:, 0:3], in_=halo.ap[:, ch, :]),
                         reads=[halo.reg], writes=[p_.reg])
                    E.op("dve", lambda e, p_=p_, a_=a_, ch=ch: e.tensor_scalar_mul(
                        out=a_.ap, in0=p_.ap[:, 0:T], scalar1=conv.ap[:, ch, 0:1]),
                        reads=[p_.reg, conv.reg], writes=[a_.reg])
                    for j in range(1, 4):
                        E.op("dve", lambda e, p_=p_, a_=a_, ch=ch, j=j: e.scalar_tensor_tensor(
                            out=a_.ap, in0=p_.ap[:, j:j + T], scalar=conv.ap[:, ch, j:j + 1], in1=a_.ap,
                            op0=ALU.mult, op1=ALU.add), reads=[p_.reg, conv.reg, a_.reg], writes=[a_.reg])
                    E.op("pool", lambda e, p_=p_, ch=ch: e.tensor_copy(out=halo.ap[:, ch, :], in_=p_.ap[:, T:T + 3]),
                         reads=[p_.reg], writes=[halo.reg])
                    s_ = stg32[nxt("s32", 3)]
                    E.op("act", lambda e, a_=a_, s_=s_: e.activation(out=s_.ap, in_=a_.ap, func=AF.Silu),
                         reads=[a_.reg], writes=[s_.reg])
                    row = (ch % 4) * 128
                    store(dstb.ap[row:row + 128, tsl], dstb.reg, s_, s_.ap)
            for gi, (nm, dstb, fn) in enumerate((("mlo0", mlo_s, AF.Sigmoid), ("mlo1", mlo_s, AF.Sigmoid),
                                                 ("glr0", glr_s, AF.Silu), ("glr1", glr_s, AF.Silu))):
                b, wap = getw(U[nm])
                for sub in range(2):
                    ps = proj_fm(wap, b.reg, sub * 128, 128, u, u.reg)
                    s_ = stg16[nxt("s16", 3)]
                    E.op("act", lambda e, ps=ps, s_=s_, fn=fn: e.activation(out=s_.ap, in_=ps.ap, func=fn),
                         reads=[ps.reg], writes=[s_.reg])
                    row = ((gi % 2) * 2 + sub) * 128
                    store(dstb.ap[row:row + 128, tsl], dstb.reg, s_, s_.ap)
            for nm, dstb in (("glq", glq32), ("glk", glk32)):
                b, wap = getw(U[nm])
                for hh in range(4):
                    ps = proj_fm(wap, b.reg, hh * 64, 64, u, u.reg)
                    s_ = stg32[nxt("s32", 3)]
                    E.op("act", lambda e, ps=ps, s_=s_: e.activation(out=s_.ap[0:64, :], in_=ps.ap[0:64, :], func=AF.Copy),
                         reads=[ps.reg], writes=[s_.reg])
                    store(dstb.ap[hh, :, tsl], dstb.reg, s_, s_.ap[0:64, :])
            b, wap = getw(U["small"])
            ps_r = proj_fm(wap, b.reg, 0, 64, u, u.reg)
            ps_w = proj_fm(wap, b.reg, 64, 64, u, u.reg)
            ri = nxt("rt", 2)
            E.op("dve", lambda e, ps_r=ps_r, ri=ri: e.tensor_tensor(out=rt1[ri].ap, in0=ps_r.ap[0:64, :], in1=cs.ap, op=ALU.mult),
                 reads=[ps_r.reg, cs.reg], writes=[rt1[ri].reg])
            E.op("dve", lambda e, ps_w=ps_w, ri=ri: e.tensor_tensor(out=rt2[ri].ap, in0=ps_w.ap[0:64, :], in1=sn.ap, op=ALU.mult),
                 reads=[ps_w.reg, sn.reg], writes=[rt2[ri].reg])
            s_ = stg16[nxt("s16", 3)]
            E.op("pool", lambda e, s_=s_, ri=ri: e.tensor_tensor(out=s_.ap[0:64, :], in0=rt1[ri].ap, in1=rt2[ri].ap, op=ALU.add),
                 reads=[rt1[ri].reg, rt2[ri].reg], writes=[s_.reg])
            store(KR.ap[:, tsl], KR.reg, s_, s_.ap[0:64, :])
            ps = proj_fm(wap, b.reg, 128, 16, u, u.reg)
            E.op("act", lambda e, ps=ps: e.activation(out=lr32.ap, in_=ps.ap[0:16, :], func=AF.Copy),
                 reads=[ps.reg], writes=[lr32.reg])
            for hh in range(4):
                ps2 = psum[nxt("ps", 4)]
                mm(ps2.ap[0:64, :], wg.ap[:, hh * 64:(hh + 1) * 64], lr32.ap, True, True, [wg.reg, lr32.reg], [ps2.reg])
                s_ = stg32[nxt("s32", 3)]
                E.op("act", lambda e, ps2=ps2, s_=s_, hh=hh: e.activation(
                    out=s_.ap[0:64, :], in_=ps2.ap[0:64, :], func=AF.Exp, scale=-1.0, bias=nbg.ap[:, hh:hh + 1]),
                    reads=[ps2.reg, nbg.reg], writes=[s_.reg])
                E.op("act", lambda e, s_=s_: e.activation(out=s_.ap[0:64, :], in_=s_.ap[0:64, :], func=AF.Ln, bias=1.0),
                     reads=[s_.reg], writes=[s_.reg])
                store(gla_neg.ap[hh, :, tsl], gla_neg.reg, s_, s_.ap[0:64, :])
            ps = proj_fm(wap, b.reg, 144, 4, u, u.reg)
            s_ = stg32[nxt("s32", 3)]
            E.op("act", lambda e, ps=ps, s_=s_: e.activation(out=s_.ap[0:4, :], in_=ps.ap[0:4, :], func=AF.Identity, bias=bi.ap[:, 0:1]),
                 reads=[ps.reg, bi.reg], writes=[s_.reg])
            store(gate_i.ap[:, tsl], gate_i.reg, s_, s_.ap[0:4, :])
            ps = proj_fm(wap, b.reg, 148, 4, u, u.reg)
            s_ = stg32[nxt("s32", 3)]
            E.op("act", lambda e, ps=ps, s_=s_: e.activation(out=s_.ap[0:4, :], in_=ps.ap[0:4, :], func=AF.Exp, scale=-1.0, bias=nbf.ap[:, 0:1]),
                 reads=[ps.reg, nbf.reg], writes=[s_.reg])
            E.op("act", lambda e, s_=s_: e.activation(out=s_.ap[0:4, :], in_=s_.ap[0:4, :], func=AF.Ln, bias=1.0),
                 reads=[s_.reg], writes=[s_.reg])
            store(gate_f.ap[:, tsl], gate_f.reg, s_, s_.ap[0:4, :])
            for nm, dstb, c0 in (("mlv0", mlv_tm, 0), ("mlv1", mlv_tm, 256), ("glv0", glv_tm, 0), ("glv1", glv_tm, 256)):
                b, wap = getw(U[nm])
                for blk in range(4):
                    ps = psum[4 + nxt("ps", 4) % 2]
                    for k in range(NK):
                        mm(ps.ap[:, 0:256], u.ap[:, k, blk * 128:(blk + 1) * 128], wap[:, k, :], k == 0, k == NK - 1,
                           [b.reg, u.reg], [ps.reg])
                    s_ = stg16[nxt("s16", 3)]
                    E.op("act", lambda e, ps=ps, s_=s_: e.activation(out=s_.ap[:, 0:256], in_=ps.ap[:, 0:256], func=AF.Copy),
                         reads=[ps.reg], writes=[s_.reg])
                    r0 = ti * T + blk * 128
                    store(dstb.ap[r0:r0 + 128, c0:c0 + 256], dstb.reg, s_, s_.ap[:, 0:256])
            if ti + 1 < NT:
                modulate(1, x32, ubufs[(ti + 1) % 2])
            for nm0, nm1, gsb, outn in (("cq0", "cq1", gq, cqn), ("ckv0", "ckv1", gkv, ckvn)):
                ps_ms = psum[6]
                for gi, nm in enumerate((nm0, nm1)):
                    b, wap = getw(U[nm])
                    for sub in range(2):
                        cidx = gi * 2 + sub
                        ps = proj_fm(wap, b.reg, sub * 128, 128, u, u.reg)
                        E.op("act", lambda e, ps=ps, cidx=cidx: e.activation(out=cq32.ap[:, cidx, :], in_=ps.ap, func=AF.Copy),
                             reads=[ps.reg], writes=[cq32.reg])
                        q_ = sqb[cidx % 2]
                        E.op("act", lambda e, ps=ps, q_=q_: e.activation(out=q_.ap, in_=ps.ap, func=AF.Square),
                             reads=[ps.reg], writes=[q_.reg])
                        mm(ps_ms.ap, ones512.ap, q_.ap, cidx == 0, cidx == 3, [ones512.reg, q_.reg], [ps_ms.reg])
                rsqrt_to(rstd, ps_ms.ap, [ps_ms.reg], EPS)
                for cidx in range(4):
                    E.op("dve", lambda e, cidx=cidx, gsb=gsb, outn=outn: e.scalar_tensor_tensor(
                        out=outn.ap[:, cidx, :], in0=cq32.ap[:, cidx, :], scalar=gsb.ap[:, cidx:cidx + 1], in1=rstd.ap,
                        op0=ALU.mult, op1=ALU.mult), reads=[cq32.reg, gsb.reg, rstd.reg], writes=[outn.reg])
            for g in range(4):
                b, wap = getw(U["qn%d" % g], nk=4)
                for hh in range(2):
                    ps = proj_fm(wap, b.reg, hh * 128, 128, cqn, cqn.reg, nk=4)
                    s_ = stg16[nxt("s16", 3)]
                    E.op("act", lambda e, ps=ps, s_=s_: e.activation(out=s_.ap, in_=ps.ap, func=AF.Copy),
                         reads=[ps.reg], writes=[s_.reg])
                    store(QN.ap[2 * g + hh, :, tsl], QN.reg, s_, s_.ap)
                b, wap = getw(U["qr%d" % g], nk=4)
                for hh in range(2):
                    ps_r = proj_fm(wap, b.reg, hh * 128, 64, cqn, cqn.reg, nk=4)
                    ps_w = proj_fm(wap, b.reg, hh * 128 + 64, 64, cqn, cqn.reg, nk=4)
                    ri = nxt("rt", 2)
                    E.op("dve", lambda e, ps_r=ps_r, ri=ri: e.tensor_tensor(out=rt1[ri].ap, in0=ps_r.ap[0:64, :], in1=cs.ap, op=ALU.mult),
                         reads=[ps_r.reg, cs.reg], writes=[rt1[ri].reg])
                    E.op("dve", lambda e, ps_w=ps_w, ri=ri: e.tensor_tensor(out=rt2[ri].ap, in0=ps_w.ap[0:64, :], in1=sn.ap, op=ALU.mult),
                         reads=[ps_w.reg, sn.reg], writes=[rt2[ri].reg])
                    s_ = stg16[nxt("s16", 3)]
                    E.op("pool", lambda e, s_=s_, ri=ri: e.tensor_tensor(out=s_.ap[0:64, :], in0=rt1[ri].ap, in1=rt2[ri].ap, op=ALU.add),
                         reads=[rt1[ri].reg, rt2[ri].reg], writes=[s_.reg])
                    store(QR.ap[2 * g + hh, :, tsl], QR.reg, s_, s_.ap[0:64, :])
                b, wap = getw(U["kn%d" % g], nk=4)
                for hh in range(2):
                    ps = proj_fm(wap, b.reg, hh * 128, 128, ckvn, ckvn.reg, nk=4)
                    s_ = stg16[nxt("s16", 3)]
                    E.op("act", lambda e, ps=ps, s_=s_: e.activation(out=s_.ap, in_=ps.ap, func=AF.Copy),
                         reads=[ps.reg], writes=[s_.reg])
                    store(KN.ap[2 * g + hh, :, tsl], KN.reg, s_, s_.ap)
                b, wap = getw(U["vv%d" % g], nk=4)
                for blk in range(4):
                    ps = psum[4 + nxt("ps", 4) % 2]
                    for k in range(4):
                        mm(ps.ap[:, 0:256], ckvn.ap[:, k, blk * 128:(blk + 1) * 128], wap[:, k, :], k == 0, k == 3,
                           [b.reg, ckvn.reg], [ps.reg])
                    s_ = stg16[nxt("s16", 3)]
                    E.op("act", lambda e, ps=ps, s_=s_: e.activation(out=s_.ap[:, 0:256], in_=ps.ap[:, 0:256], func=AF.Copy),
                         reads=[ps.reg], writes=[s_.reg])
                    r0 = ti * T + blk * 128
                    store(V_tm.ap[r0:r0 + 128, g * 256:(g + 1) * 256], V_tm.reg, s_, s_.ap[:, 0:256])
            pump(2)

    def linattn(l):
        E.barrier()
        pos = [persist_top]
        NC_ = S // 128
        q32 = carve(128, [S], F32, pos)
        k32 = carve(128, [S], F32, pos)
        la = carve(128, [S], F32, pos)
        ex = carve(128, [S], F32, pos)
        rmask = carve(128, [S], F32, pos)
        qd = carve(128, [S], BF16, pos)
        kd = carve(128, [S], BF16, pos)
        ks = carve(128, [S], BF16, pos)
        ks_tm = carve(128, [NC_, 128], BF16, pos)
        vaug = carve(128, [NC_, 256], BF16, pos)
        dec = carve(128, [NC_], F32, pos)
        S32 = carve(128, [256], F32, pos)
        Sbf = [carve(128, [256], BF16, pos) for _ in range(2)]
        am = [carve(128, [128], BF16, pos) for _ in range(2)]
        gt = carve(128, [T], BF16, pos)
        hh32 = carve(128, [T], F32, pos)
        t32 = carve(128, [T], F32, pos)
        mean = carve(128, [T], F32, pos)
        rstd = carve(128, [T], F32, pos)
        zb = carve(128, [T], BF16, pos)
        sqb_ = carve(128, [T], BF16, pos)
        yst = carve(128, [T], BF16, pos)
        mi = Buf(ex.ap.bitcast(I32), ex.reg)
        lnq = carve(128, [1], F32, pos)
        lnk = carve(128, [1], F32, pos)
        pt = Buf(psum[7].ap.bitcast(BF16), psum[7].reg)

        E.op("pool", lambda e: e.iota(mi.ap, pattern=[[0, NC_], [1, 128]], base=0, channel_multiplier=0), writes=[mi.reg])
        E.op("dve", lambda e: e.tensor_copy(out=rmask.ap, in_=mi.ap), reads=[mi.reg], writes=[rmask.reg])
        E.op("dve", lambda e: e.tensor_scalar_min(out=rmask.ap, in0=rmask.ap, scalar1=1.0), reads=[rmask.reg], writes=[rmask.reg])
        E.op("pool", lambda e: e.memset(vaug.ap[:, :, 128:256], 1.0), writes=[vaug.reg])

        for kind in ("ml", "gl"):
            dk = 128 if kind == "ml" else 64
            for hd in range(4):
                if kind == "ml":
                    E.dma(q32.ap, mlq32.ap[hd * 128:(hd + 1) * 128, :], reads=[mlq32.reg], writes=[q32.reg])
                    E.dma(k32.ap, mlk32.ap[hd * 128:(hd + 1) * 128, :], reads=[mlk32.reg], writes=[k32.reg])
                    E.dma(la.ap, gate_f.ap[hd:hd + 1, :].partition_broadcast(128), reads=[gate_f.reg], writes=[la.reg])
                    E.dma(ex.ap, gate_i.ap[hd:hd + 1, :].partition_broadcast(128), reads=[gate_i.reg], writes=[ex.reg])
                    vsrc = mlv_tm
                    sc_dec = -1.0
                    qscale, kscale = 1.0, 128.0 ** -0.5
                else:
                    E.dma(q32.ap[0:64, :], glq32.ap[hd], reads=[glq32.reg], writes=[q32.reg])
                    E.dma(k32.ap[0:64, :], glk32.ap[hd], reads=[glk32.reg], writes=[k32.reg])
                    E.dma(la.ap[0:64, :], gla_neg.ap[hd], reads=[gla_neg.reg], writes=[la.reg])
                    vsrc = glv_tm
                    sc_dec = -1.0 / 16.0
                    qscale, kscale = 64.0 ** -0.5, 1.0
                E.dma(vaug.ap[:, :, 0:128],
                      vsrc.ap[:, hd * 128:(hd + 1) * 128].rearrange("(c p) d -> p c d", p=128),
                      reads=[vsrc.reg], writes=[vaug.reg])
                P = slice(0, dk)
                E.op("pool", lambda e, qscale=qscale: e.memset(lnq.ap, math.log(qscale)), writes=[lnq.reg])
                E.op("pool", lambda e, kscale=kscale: e.memset(lnk.ap, math.log(kscale)), writes=[lnk.reg])
                E.op("dve", lambda e, P=P: e.tensor_tensor_scan(out=la.ap[P, :], data0=rmask.ap[P, :], data1=la.ap[P, :],
                                                                initial=0.0, op0=ALU.mult, op1=ALU.add),
                     reads=[la.reg, rmask.reg], writes=[la.reg])
                lav = la.ap.rearrange("p (c t) -> p c t", t=128)
                E.op("act", lambda e, P=P, sc_dec=sc_dec, lav=lav: e.activation(out=dec.ap[P, :], in_=lav[P, :, 127], func=AF.Exp, scale=sc_dec),
                     reads=[la.reg], writes=[dec.reg])
                if kind == "ml":
                    E.op("dve", lambda e: e.tensor_tensor(out=ex.ap, in0=ex.ap, in1=la.ap, op=ALU.add),
                         reads=[ex.reg, la.reg], writes=[ex.reg])
                    E.op("act", lambda e: e.activation(out=ex.ap, in_=ex.ap, func=AF.Exp, bias=lnk.ap[:, 0:1]),
                         reads=[ex.reg, lnk.reg], writes=[ex.reg])
                else:
                    E.op("act", lambda e, P=P: e.activation(out=ex.ap[P, :], in_=la.ap[P, :], func=AF.Exp, scale=1.0 / 16.0),
                         reads=[la.reg], writes=[ex.reg])
                E.op("dve", lambda e, P=P: e.tensor_tensor(out=k32.ap[P, :], in0=k32.ap[P, :], in1=ex.ap[P, :], op=ALU.mult),
                     reads=[k32.reg, ex.reg], writes=[k32.reg])
                E.op("act", lambda e, P=P: e.activation(out=kd.ap[P, :], in_=k32.ap[P, :], func=AF.Copy),
                     reads=[k32.reg], writes=[kd.reg])
                k3 = k32.ap.rearrange("p (c t) -> p c t", t=128)
                ks3 = ks.ap.rearrange("p (c t) -> p c t", t=128)
                E.op("dve", lambda e, P=P, dk=dk, k3=k3, ks3=ks3: e.tensor_tensor(out=ks3[P], in0=k3[P], in1=dec.ap[P, :].unsqueeze(2).to_broadcast([dk, NC_, 128]), op=ALU.mult),
                     reads=[k32.reg, dec.reg], writes=[ks.reg])
                E.op("act", lambda e, P=P, sc_dec=sc_dec: e.activation(out=ex.ap[P, :], in_=la.ap[P, :], func=AF.Exp, scale=sc_dec, bias=lnq.ap[P, 0:1]),
                     reads=[la.reg, lnq.reg, k32.reg], writes=[ex.reg])
                E.op("dve", lambda e, P=P: e.tensor_tensor(out=qd.ap[P, :], in0=q32.ap[P, :], in1=ex.ap[P, :], op=ALU.mult),
                     reads=[q32.reg, ex.reg], writes=[qd.reg])
                for c8 in range(NC_ // 8):
                    for i in range(8):
                        c = c8 * 8 + i
                        E.op("pe", lambda e, c=c, i=i, P=P, dk=dk, ks3=ks3: e.transpose(pt.ap[:, i * 128:i * 128 + dk], ks3[P, c, :], ident.ap[P, 0:dk]),
                             reads=[ks.reg, ident.reg], writes=[pt.reg])
                    ptv = pt.ap.rearrange("p (a b) -> p a b", a=8)
                    E.op("act", lambda e, c8=c8, ptv=ptv, dk=dk: e.activation(out=ks_tm.ap[:, c8 * 8:(c8 + 1) * 8, 0:dk], in_=ptv[:, :, 0:dk], func=AF.Copy),
                         reads=[pt.reg], writes=[ks_tm.reg])
                nw = 256 if kind == "ml" else 128
                pSb = [Buf(psum[6].ap[:, 0:256]), Buf(psum[6].ap[:, 256:512])]

                def stageA(c, P=P, dk=dk, kind=kind, nw=nw, pSb=pSb):
                    csl = slice(c * 128, (c + 1) * 128)
                    pa = psum[4 + c % 2]
                    mm(pa.ap[:, 0:128], kd.ap[P, csl], qd.ap[P, csl], True, True, [kd.reg, qd.reg], [pa.reg])
                    a_ = am[c % 2]
                    E.op("dve", lambda e, pa=pa, a_=a_: e.tensor_tensor(out=a_.ap, in0=pa.ap[:, 0:128], in1=tri.ap, op=ALU.mult),
                         reads=[pa.reg, tri.reg], writes=[a_.reg])
                    if c < NC_ - 1:
                        pS = pSb[c % 2]
                        mm(pS.ap[P, 0:nw], ks_tm.ap[:, c, 0:dk], vaug.ap[:, c, 0:nw], True, True,
                           [ks_tm.reg, vaug.reg], [pS.reg])

                def stageB(c, po, pd, P=P, dk=dk, kind=kind, nw=nw, pSb=pSb):
                    csl = slice(c * 128, (c + 1) * 128)
                    ci = c % 4
                    osl = slice(ci * 128, (ci + 1) * 128)
                    a_ = am[c % 2]
                    first = (c == 0)
                    Sp = Sbf[(c + 1) % 2]
                    mm(po.ap[:, osl], vaug.ap[:, c, 0:128], a_.ap, True, first, [vaug.reg, a_.reg], [po.reg])
                    if not first:
                        mm(po.ap[:, osl], Sp.ap[P, 0:128], qd.ap[P, csl], False, True, [Sp.reg, qd.reg], [po.reg])
                    if kind == "ml":
                        mm(pd.ap[:, osl], ones_bf.ap, a_.ap, True, first, [ones_bf.reg, a_.reg], [pd.reg])
                        if not first:
                            mm(pd.ap[:, osl], Sp.ap[:, 128:256], qd.ap[:, csl], False, True, [Sp.reg, qd.reg], [pd.reg])
                    if c < NC_ - 1:
                        pS = pSb[c % 2]
                        if first:
                            E.op("dve", lambda e, pS=pS: e.tensor_copy(out=S32.ap[P, 0:nw], in_=pS.ap[P, 0:nw]),
                                 reads=[pS.reg], writes=[S32.reg])
                        else:
                            E.op("dve", lambda e, pS=pS, c=c: e.scalar_tensor_tensor(
                                out=S32.ap[P, 0:nw], in0=S32.ap[P, 0:nw], scalar=dec.ap[P, c:c + 1], in1=pS.ap[P, 0:nw],
                                op0=ALU.mult, op1=ALU.add), reads=[S32.reg, dec.reg, pS.reg], writes=[S32.reg])
                        Sn = Sbf[c % 2]
                        E.op("act", lambda e, Sn=Sn: e.activation(out=Sn.ap[P, 0:nw], in_=S32.ap[P, 0:nw], func=AF.Copy),
                             reads=[S32.reg], writes=[Sn.reg])

                stageA(0)
                for tq in range(NT):
                    po = psum[tq % 2]
                    pd = psum[2 + tq % 2]
                    tsl = slice(tq * T, (tq + 1) * T)
                    if kind == "ml":
                        E.dma(gt.ap, mlo_s.ap[hd * 128:(hd + 1) * 128, tsl], reads=[mlo_s.reg], writes=[gt.reg])
                    else:
                        E.dma(gt.ap, glr_s.ap[hd * 128:(hd + 1) * 128, tsl], reads=[glr_s.reg], writes=[gt.reg])
                    for ci in range(4):
                        c = tq * 4 + ci
                        if c + 1 < NC_:
                            stageA(c + 1)
                        stageB(c, po, pd)
                    if kind == "ml":
                        E.op("act", lambda e, pd=pd: e.activation(out=t32.ap, in_=pd.ap, func=AF.Abs),
                             reads=[pd.reg], writes=[t32.reg])
                        E.op("dve", lambda e: e.tensor_scalar_max(out=t32.ap, in0=t32.ap, scalar1=1.0),
                             reads=[t32.reg], writes=[t32.reg])
                        E.op("dve", lambda e: e.reciprocal(out=t32.ap, in_=t32.ap), reads=[t32.reg], writes=[t32.reg])
                        E.op("dve", lambda e, po=po: e.tensor_tensor(out=hh32.ap, in0=po.ap, in1=t32.ap, op=ALU.mult),
                             reads=[po.reg, t32.reg], writes=[hh32.reg])
                        E.op("act", lambda e: e.activation(out=zb.ap, in_=hh32.ap, func=AF.Copy), reads=[hh32.reg], writes=[zb.reg])
                        E.op("act", lambda e: e.activation(out=sqb_.ap, in_=hh32.ap, func=AF.Square), reads=[hh32.reg], writes=[sqb_.reg])
                        pm = psum[4]
                        pq = psum[5]
                        mm(pm.ap, ones128.ap, zb.ap, True, True, [ones128.reg, zb.reg], [pm.reg])
                        mm(pq.ap, ones128.ap, sqb_.ap, True, True, [ones128.reg, sqb_.reg], [pq.reg])
                        E.op("act", lambda e, pm=pm: e.activation(out=mean.ap, in_=pm.ap, func=AF.Copy), reads=[pm.reg], writes=[mean.reg])
                        E.op("dve", lambda e: e.tensor_tensor(out=rstd.ap, in0=mean.ap, in1=mean.ap, op=ALU.mult),
                             reads=[mean.reg], writes=[rstd.reg])
                        E.op("dve", lambda e, pq=pq: e.tensor_tensor(out=rstd.ap, in0=pq.ap, in1=rstd.ap, op=ALU.subtract),
                             reads=[pq.reg, rstd.reg], writes=[rstd.reg])
                        rsqrt_inplace(rstd, EPS)
                        E.op("pool", lambda e: e.tensor_tensor(out=hh32.ap, in0=hh32.ap, in1=mean.ap, op=ALU.subtract),
                             reads=[hh32.reg, mean.reg], writes=[hh32.reg])
                        E.op("dve", lambda e: e.tensor_tensor(out=hh32.ap, in0=hh32.ap, in1=rstd.ap, op=ALU.mult),
                             reads=[hh32.reg, rstd.reg], writes=[hh32.reg])
                        E.op("dve", lambda e: e.tensor_tensor(out=yst.ap, in0=hh32.ap, in1=gt.ap, op=ALU.mult),
                             reads=[hh32.reg, gt.reg], writes=[yst.reg])
                        row = hd * 128
                    else:
                        E.op("act", lambda e, po=po: e.activation(out=sqb_.ap, in_=po.ap, func=AF.Square), reads=[po.reg], writes=[sqb_.reg])
                        pq = psum[5]
                        mm(pq.ap, ones128.ap, sqb_.ap, True, True, [ones128.reg, sqb_.reg], [pq.reg])
                        rsqrt_to(rstd, pq.ap, [pq.reg], EPS)
                        E.op("dve", lambda e, po=po: e.tensor_tensor(out=hh32.ap, in0=po.ap, in1=rstd.ap, op=ALU.mult),
                             reads=[po.reg, rstd.reg], writes=[hh32.reg])
                        E.op("dve", lambda e: e.tensor_tensor(out=yst.ap, in0=hh32.ap, in1=gt.ap, op=ALU.mult),
                             reads=[hh32.reg, gt.reg], writes=[yst.reg])
                        row = 512 + hd * 128
                    E.dma(ymix.ap[row:row + 128, tsl], yst.ap, reads=[yst.reg], writes=[ymix.reg], eng="act")
                pump(4)

    def mla_attn(l):
        E.barrier()
        pos = [persist_top]
        NB = S // 128
        scale = 192.0 ** -0.5
        qn = carve(128, [S], BF16, pos)
        qr = carve(65, [S], BF16, pos)
        kn = carve(128, [S], BF16, pos)
        kr = carve(65, [S], BF16, pos)
        vt = carve(128, [NB, 128], BF16, pos)
        sq = [carve(128, [T], BF16, pos) for _ in range(2)]
        qn2 = carve(128, [S], F32, pos)
        kmx = carve(128, [NT], F32, pos)
        kmax = carve(128, [1], F32, pos)
        pb = [carve(128, [T], BF16, pos) for _ in range(4)]
        rl = carve(128, [T], F32, pos)
        yst = [carve(128, [T], BF16, pos) for _ in range(2)]
        E.dma(kr.ap[0:64, :], KR.ap, reads=[KR.reg], writes=[kr.reg])
        E.op("pool", lambda e: e.memset(kr.ap[64:65, :], 1.0), reads=[], writes=[kr.reg])
        for hd in range(8):
            E.dma(qn.ap, QN.ap[hd], reads=[QN.reg], writes=[qn.reg])
            E.dma(qr.ap[0:64, :], QR.ap[hd], reads=[QR.reg], writes=[qr.reg])
            E.dma(kn.ap, KN.ap[hd], reads=[KN.reg], writes=[kn.reg])
            E.dma(vt.ap, V_tm.ap[:, hd * 128:(hd + 1) * 128].rearrange("(c p) d -> p c d", p=128),
                  reads=[V_tm.reg], writes=[vt.reg])
            for ti in range(NT):
                tsl = slice(ti * T, (ti + 1) * T)
                pk = psum[6]
                pq = psum[7]
                a, b2 = sq[0], sq[1]
                E.op("act", lambda e, a=a, tsl=tsl: e.activation(out=a.ap, in_=kn.ap[:, tsl], func=AF.Square), reads=[kn.reg], writes=[a.reg])
                mm(pk.ap, ones_bf.ap, a.ap, True, False, [ones_bf.reg, a.reg], [pk.reg])
                E.op("act", lambda e, b2=b2, tsl=tsl: e.activation(out=b2.ap[0:64, :], in_=kr.ap[0:64, tsl], func=AF.Square), reads=[kr.reg], writes=[b2.reg])
                mm(pk.ap, ones_bf.ap[0:64, :], b2.ap[0:64, :], False, True, [ones_bf.reg, b2.reg], [pk.reg])
                E.op("dve", lambda e, pk=pk, ti=ti: e.reduce_max(out=kmx.ap[:, ti:ti + 1], in_=pk.ap, axis=AX.X), reads=[pk.reg], writes=[kmx.reg])
                E.op("act", lambda e, a=a, tsl=tsl: e.activation(out=a.ap, in_=qn.ap[:, tsl], func=AF.Square), reads=[qn.reg], writes=[a.reg])
                mm(pq.ap, ones_bf.ap, a.ap, True, False, [ones_bf.reg, a.reg], [pq.reg])
                E.op("act", lambda e, b2=b2, tsl=tsl: e.activation(out=b2.ap[0:64, :], in_=qr.ap[0:64, tsl], func=AF.Square), reads=[qr.reg], writes=[b2.reg])
                mm(pq.ap, ones_bf.ap[0:64, :], b2.ap[0:64, :], False, True, [ones_bf.reg, b2.reg], [pq.reg])
                E.op("act", lambda e, pq=pq, tsl=tsl: e.activation(out=qn2.ap[:, tsl], in_=pq.ap, func=AF.Copy), reads=[pq.reg], writes=[qn2.reg])
            E.op("dve", lambda e: e.reduce_max(out=kmax.ap, in_=kmx.ap, axis=AX.X), reads=[kmx.reg], writes=[kmax.reg])
            E.op("dve", lambda e: e.tensor_scalar_mul(out=qn2.ap[64:65, :], in0=qn2.ap[64:65, :], scalar1=kmax.ap[64:65, 0:1]),
                 reads=[qn2.reg, kmax.reg], writes=[qn2.reg])
            E.op("act", lambda e: e.activation(out=qn2.ap[64:65, :], in_=qn2.ap[64:65, :], func=AF.Sqrt), reads=[qn2.reg], writes=[qn2.reg])
            E.op("dve", lambda e: e.tensor_scalar_mul(out=qr.ap[64:65, :], in0=qn2.ap[64:65, :], scalar1=-1.0),
                 reads=[qn2.reg], writes=[qr.reg])
            work = []
            for qt in range(NT):
                nkb = 4 * (qt + 1)
                for kb in range(nkb):
                    work.append((qt, kb, nkb))
            LA = 2

            def stage_qk(i):
                qt, kb, nkb = work[i]
                d_ = kb - 4 * qt
                off = d_ * 128 if d_ > 0 else 0
                qsl = slice(qt * T + off, (qt + 1) * T)
                n = T - off
                ksl = slice(kb * 128, (kb + 1) * 128)
                ps_ = psum[4 + i % 4]
                mm(ps_.ap[:, 0:n], kn.ap[:, ksl], qn.ap[:, qsl], True, False, [kn.reg, qn.reg], [ps_.reg])
                mm(ps_.ap[:, 0:n], kr.ap[:, ksl], qr.ap[:, qsl], False, True, [kr.reg, qr.reg], [ps_.reg])
                p_ = pb[i % 4]
                E.op("act", lambda e, ps_=ps_, p_=p_, n=n: e.activation(out=p_.ap[:, 0:n], in_=ps_.ap[:, 0:n], func=AF.Exp, scale=scale),
                     reads=[ps_.reg], writes=[p_.reg])
                if d_ >= 0:
                    E.op("pool", lambda e, p_=p_: e.tensor_tensor(out=p_.ap[:, 0:128], in0=p_.ap[:, 0:128], in1=tri.ap, op=ALU.mult),
                         reads=[p_.reg, tri.reg], writes=[p_.reg])

            def stage_pv(i, hd=hd):
                qt, kb, nkb = work[i]
                d_ = kb - 4 * qt
                off = d_ * 128 if d_ > 0 else 0
                n = T - off
                po = psum[qt % 2]
                pl = psum[2 + qt % 2]
                p_ = pb[i % 4]
                mm(po.ap[:, off:T], vt.ap[:, kb, :], p_.ap[:, 0:n], kb == 0, kb == nkb - 1, [vt.reg, p_.reg], [po.reg])
                mm(pl.ap[:, off:T], ones_bf.ap, p_.ap[:, 0:n], kb == 0, kb == nkb - 1, [ones_bf.reg, p_.reg], [pl.reg])
                if kb == nkb - 1:
                    E.op("dve", lambda e, pl=pl: e.reciprocal(out=rl.ap, in_=pl.ap), reads=[pl.reg], writes=[rl.reg])
                    y_ = yst[qt % 2]
                    E.op("dve", lambda e, po=po, y_=y_: e.tensor_tensor(out=y_.ap, in0=po.ap, in1=rl.ap, op=ALU.mult),
                         reads=[po.reg, rl.reg], writes=[y_.reg])
                    row = 1024 + hd * 128
                    E.dma(ymix.ap[row:row + 128, qt * T:(qt + 1) * T], y_.ap, reads=[y_.reg], writes=[ymix.reg], eng="act")

            for i in range(min(LA, len(work))):
                stage_qk(i)
            for i in range(len(work)):
                if i + LA < len(work):
                    stage_qk(i + LA)
                stage_pv(i)
            pump(4)

    def outproj(l, U, dst_ap, dst_regs):
        E.barrier()
        pos = [persist_top]
        x32b = [carve(128, [NK, T], F32, pos) for _ in range(2)]
        yb = [carve(128, [NK, T], BF16, pos) for _ in range(2)]
        wob = [carve(128, [NK, 128], BF16, pos) for _ in range(4)]
        xrs = [carve(128, [T], F32, pos) for _ in range(3)]
        tmp = ln_tmps(pos)
        wos = Stream(list(U["wo"]) * NT, wob)
        wos.prime()
        yv = ymix.ap.rearrange("(k p) t -> p k t", p=128)
        srcv = xs.ap.rearrange("(k p) t -> p k t", p=128)
        cnt = {"xr": 0}
        E.dma(yb[0].ap, yv[:, :, 0:T], reads=[ymix.reg], writes=[yb[0].reg])
        for ti in range(NT):
            if ti + 1 < NT:
                E.dma(yb[(ti + 1) % 2].ap, yv[:, :, (ti + 1) * T:(ti + 2) * T], reads=[ymix.reg], writes=[yb[(ti + 1) % 2].reg])
            pend = {}

            def issue_xr(m, ti=ti, pend=pend):
                xb = xrs[cnt["xr"] % 3]
                cnt["xr"] += 1
                E.dma(xb.ap, srcv[:, m, ti * T:(ti + 1) * T], reads=[xs_tiles[ti]], writes=[xb.reg])
                pend[m] = xb

            def xres(m, pend=pend, issue_xr=issue_xr):
                if m + 1 < NK and (m + 1) not in pend:
                    issue_xr(m + 1)
                if m not in pend:
                    issue_xr(m)
                return pend.pop(m)

            issue_xr(0)
            proj_ln(1, yb[ti % 2], NK, wos, x32b[ti % 2], tmp, dst_ap, dst_regs[ti], ti, xres=xres)

    for l in range(n_layers):
        Ud = {}
        Ud["f1i"] = wi_units(ffn_wi[0], l, "f1")
        Ud["f1o"] = wo_units(ffn_wo[0], l, "f1")
        Ud["mix"] = mixer_units(l)
        Ud["f2i"] = wi_units(ffn_wi[1], l, "f2")
        Ud["f2o"] = wo_units(ffn_wo[1], l, "f2")
        units.append(Ud)

    rope_done = [False]
    for l in range(n_layers):
        Ud = units[l]
        E.barrier()
        if l == 0:
            pump(4 * NJ + 6 * NK)
        adaln(l, persist_top)
        src_ap, src_regs = (x_in, [Reg() for _ in range(NT)]) if l == 0 else (xs.ap, xs_tiles)
        last = (stop_after == (l, 0))
        ffn(l, 0, Ud["f1i"], Ud["f1o"], src_ap, src_regs,
            y_out if last else xs.ap, out_tiles if last else xs_tiles)
        if last:
            break
        if not rope_done[0]:
            rope_tables()
            rope_done[0] = True
        inproj(l, Ud["mix"])
        linattn(l)
        mla_attn(l)
        last = (stop_after == (l, 1))
        outproj(l, Ud["mix"], y_out if last else xs.ap, out_tiles if last else xs_tiles)
        if last:
            break
        last = (l == n_layers - 1) or (stop_after == (l, 2))
        ffn(l, 2, Ud["f2i"], Ud["f2o"], xs.ap, xs_tiles,
            y_out if last else xs.ap, out_tiles if last else xs_tiles)
        if last:
            break

    E.finish()
    st.close()
    return nc, E


def host_layout(inputs, b, nl=DEPTH):
    f = np.ascontiguousarray
    half = 32
    invf = (10000.0 ** (-np.arange(half, dtype=np.float32) / half)).astype(np.float32)
    m = {
        "x": f(inputs["x"][b].T),
        "c": f(inputs["c"][b].reshape(NK, 128).T),
        "pos": f(inputs["positions"][b].reshape(1, S).astype(np.int32)),
        "invf": f(np.concatenate([invf, invf]).reshape(64, 1)),
        "w_ada": inputs["w_ada"][:nl],
        "b_ada": f(inputs["b_ada"][:nl].reshape(nl, 144, 128).transpose(0, 2, 1)),
        "ln_g": f(inputs["ln_g"][:nl].reshape(nl, 3, NK, 128).transpose(0, 3, 1, 2)),
        "ln_b": f(inputs["ln_b"][:nl].reshape(nl, 3, NK, 128).transpose(0, 3, 1, 2)),
        "ffn1_wi": inputs["ffn1_wi"][:nl], "ffn1_wo": inputs["ffn1_wo"][:nl],
        "ffn2_wi": inputs["ffn2_wi"][:nl], "ffn2_wo": inputs["ffn2_wo"][:nl],
        "w_in": inputs["w_in"][:nl],
        "ml_conv": f(inputs["ml_conv"][:nl].reshape(nl, 4, 8, 128).transpose(0, 3, 2, 1)),
        "ml_bi": f(inputs["ml_bi"][:nl].reshape(nl, 4, 1)),
        "ml_bf": f(inputs["ml_bf"][:nl].reshape(nl, 4, 1)),
        "gla_wg": inputs["gla_wg"][:nl],
        "gla_bg": f(inputs["gla_bg"][:nl].reshape(nl, 4, 64).transpose(0, 2, 1)),
        "mla_gq": f(inputs["mla_gq"][:nl].reshape(nl, 4, 128).transpose(0, 2, 1)),
        "mla_gkv": f(inputs["mla_gkv"][:nl].reshape(nl, 4, 128).transpose(0, 2, 1)),
        "mla_wuq": inputs["mla_wuq"][:nl], "mla_wuk": inputs["mla_wuk"][:nl], "mla_wuv": inputs["mla_wuv"][:nl],
        "w_out": inputs["w_out"][:nl],
    }
    return m


_USED = None


def run(inputs, cores, trace=False, **bk):
    nc, E = build(**bk)
    nl = bk.get('n_layers', DEPTH)
    in_maps = [host_layout(inputs, b, nl) for b in cores]
    if trace:
        res = run_bass_kernel_spmd(nc, in_maps, core_ids=list(range(len(cores))), trace=True)
        print("EXEC_TIME_NS", res.exec_time_ns)
    else:
        res = run_bass_kernel_spmd(nc, in_maps, core_ids=list(range(len(cores))))
    return [np.ascontiguousarray(r["y"].T) for r in res.results]


def kernel(**inputs):
    inputs = {k: np.asarray(v) for k, v in inputs.items()}
    outs = run(inputs, list(range(8)))
    return np.stack(outs, 0).astype(np.float32)
```
